# Optimizing a Trainium2 kernel written in Bass

```python
import jax, jax.numpy as jnp
from jax import lax
import numpy as np

D_MODEL = 1024
BATCH = 4
SEQ = 4096
DEPTH = 4
DEC_BATCH = 128
DEC_SEQ = 8
PAST_LEN = 2048
PAGE_SIZE = 128

HEAD_DIM = 64
DIL_PATTERNS = ((128, 1), (512, 4), (2048, 16))
N_DIL = len(DIL_PATTERNS)
SPAN = 128
A_HEADS = 8
A_WIDTH = A_HEADS * HEAD_DIM
POOL_WINDOWS = (2, 4, 8, 16)
N_POOL = len(POOL_WINDOWS)
D_POOL = D_MODEL // 2
POOL_GC = D_POOL // N_POOL
POOL_STATE = max(POOL_WINDOWS) - 1
AB_IN = N_DIL * 3 * A_WIDTH + D_POOL
AB_OUT = A_WIDTH + D_POOL
CONV_W = 3
D_FF = 2816
ROPE_THETA = 10000.0
EPS = 1e-6
N_AB = (DEPTH + 1) // 2
N_C = DEPTH // 2

kernel_name = "hybrid_dilated_pool_conv_macaron_step"

F32 = jnp.float32


def rms_norm(x, g):
    xf = x.astype(F32)
    y = xf * lax.rsqrt(jnp.mean(xf * xf, axis=-1, keepdims=True) + EPS)
    return (y * g.astype(F32)).astype(x.dtype)


def swiglu(x, w_gu, w_down):
    g, u = jnp.split(x @ w_gu, 2, axis=-1)
    return (jax.nn.silu(g) * u) @ w_down


def rope(x, pos):
    half = HEAD_DIM // 2
    inv = jnp.power(ROPE_THETA, -jnp.arange(half, dtype=F32) / half)
    ang = pos.astype(F32)[:, None] * inv[None, :]
    shape = (1, ang.shape[0]) + (1,) * (x.ndim - 3) + (half,)
    cos = jnp.cos(ang).reshape(shape)
    sin = jnp.sin(ang).reshape(shape)
    xf = x.astype(F32)
    x1, x2 = xf[..., :half], xf[..., half:]
    return jnp.concatenate([x1 * cos - x2 * sin, x2 * cos + x1 * sin], axis=-1).astype(x.dtype)


def dilated_attn_prompt(q, k, v, dil):
    Bn, S, H, Dh = q.shape
    n_sub = S // dil
    n_blk = -(-n_sub // SPAN)
    pad = n_blk * SPAN - n_sub

    def to_sub(t):
        t = jnp.moveaxis(t.reshape(Bn, n_sub, dil, H, Dh), 2, 1)
        t = jnp.pad(t, ((0, 0), (0, 0), (0, pad), (0, 0), (0, 0)))
        return t.reshape(Bn, dil, n_blk, SPAN, H, Dh)

    def with_prev(t):
        prev = jnp.pad(t[:, :, :-1], ((0, 0), (0, 0), (1, 0), (0, 0), (0, 0), (0, 0)))
        return jnp.concatenate([prev, t], axis=3)

    qs = to_sub(q.astype(F32)) * (Dh ** -0.5)
    kb = with_prev(to_sub(k.astype(F32)))
    vb = with_prev(to_sub(v.astype(F32)))
    s = jnp.einsum('brnqhd,brnkhd->brnhqk', qs, kb)
    qi = jnp.arange(SPAN)[:, None]
    kj = jnp.arange(2 * SPAN)[None, :]
    band = (kj >= qi) & (kj <= qi + SPAN)
    has_prev = (jnp.arange(n_blk) > 0)[:, None, None] | (kj >= SPAN)[None]
    mask = band[None] & has_prev
    s = jnp.where(mask[None, None, :, None], s, -jnp.inf)
    m = jnp.max(s, axis=-1, keepdims=True)
    p = jnp.exp(s - m)
    den = jnp.sum(p, axis=-1, keepdims=True)
    lse = jnp.swapaxes((m + jnp.log(den))[..., 0], 3, 4)
    o = jnp.einsum('brnhqk,brnkhd->brnqhd', p / den, vb)

    def from_sub(t):
        rest = t.shape[4:]
        t = t.reshape((Bn, dil, n_blk * SPAN) + rest)[:, :, :n_sub]
        return jnp.moveaxis(t, 1, 2).reshape((Bn, S) + rest)

    return from_sub(o), from_sub(lse)


def dilated_attn_sample(q, kv_new, kv_buf, dil):
    L = kv_buf.shape[1]
    T, Dh = q.shape[1], q.shape[-1]
    kvc = jnp.concatenate([kv_buf.astype(kv_new.dtype), kv_new], axis=1)
    idx = L + jnp.arange(T)[:, None] - dil * jnp.arange(SPAN + 1)[None, :]
    valid = idx >= 0
    kg = kvc[:, jnp.maximum(idx, 0)].astype(F32)
    s = jnp.einsum('bthd,btkhd->bthk', q.astype(F32) * (Dh ** -0.5), kg[:, :, :, 0])
    s = jnp.where(valid[None, :, None, :], s, -jnp.inf)
    m = jnp.max(s, axis=-1, keepdims=True)
    p = jnp.exp(s - m)
    den = jnp.sum(p, axis=-1, keepdims=True)
    lse = (m + jnp.log(den))[..., 0]
    o = jnp.einsum('bthk,btkhd->bthd', p / den, kg[:, :, :, 1])
    return o, lse


def pool_mixer(u, prev, pos0, pos, pool_w, pool_scale):
    Bn, S, C = u.shape
    P = prev.shape[1]
    ext = jnp.concatenate([prev.astype(u.dtype), u], axis=1)
    ext_pos = pos0 - P + jnp.arange(P + S)
    extf = jnp.where((ext_pos >= 0)[None, :, None], ext.astype(F32), 0.0)
    cs = jnp.concatenate([jnp.zeros((Bn, 1, C), F32), lax.cumsum(extf, axis=1)], axis=1)
    hi = cs[:, P + 1:]
    uf = u.astype(F32)
    outs = []
    for gi, w in enumerate(POOL_WINDOWS):
        sl = slice(gi * POOL_GC, (gi + 1) * POOL_GC)
        lo = cs[:, P + 1 - w:P + 1 - w + S, sl]
        cnt = jnp.minimum(w, pos + 1).astype(F32)[None, :, None]
        outs.append((hi[..., sl] - lo) / cnt - uf[..., sl])
    d = jnp.stack(outs, axis=2)
    y = jnp.einsum('bsgc,gcd->bsgd', d, pool_w.astype(F32)).reshape(Bn, S, C)
    y = y * pool_scale.astype(F32)
    return y.astype(u.dtype), ext[:, -P:]


def ab_mixer(h, pos, pos0, past_kv, prev_pool, w_in, w_out, pool_w, pool_scale):
    Bn, S, _ = h.shape
    z = h @ w_in
    qkv = z[..., :N_DIL * 3 * A_WIDTH].reshape(Bn, S, N_DIL, 3, A_HEADS, HEAD_DIM)
    qk = rope(qkv[:, :, :, :2], pos)
    outs, lses, rows = [], [], []
    for g, (win, dil) in enumerate(DIL_PATTERNS):
        q, k, v = qk[:, :, g, 0], qk[:, :, g, 1], qkv[:, :, g, 2]
        kv = jnp.stack([k, v], axis=2)
        if past_kv is None:
            o, lse = dilated_attn_prompt(q, k, v, dil)
            rows.append(kv[:, -min(win, S):])
        else:
            o, lse = dilated_attn_sample(q, kv, past_kv[g], dil)
            rows.append(kv)
        outs.append(o)
        lses.append(lse)
    wts = jax.nn.softmax(jnp.stack(lses, axis=0), axis=0)[..., None]
    a = jnp.sum(wts * jnp.stack(outs, axis=0), axis=0).reshape(Bn, S, A_WIDTH).astype(h.dtype)
    p, pool_rows = pool_mixer(z[..., N_DIL * 3 * A_WIDTH:], prev_pool, pos0, pos, pool_w, pool_scale)
    return jnp.concatenate([a, p], axis=-1) @ w_out, rows, pool_rows


def conv_mixer(h, prev, w_in, conv_w, w_out):
    S = h.shape[1]
    gb, gc, v = jnp.split(h @ w_in, 3, axis=-1)
    u = gc * v
    ext = jnp.concatenate([prev.astype(u.dtype), u], axis=1)
    y = conv_w[0] * ext[:, 0:S]
    for t in range(1, CONV_W):
        y = y + conv_w[t] * ext[:, t:t + S]
    return (gb * y) @ w_out, ext[:, -(CONV_W - 1):]


def run_trunk(x, pos0, win_caches, pool_state, conv_state,
              ffn1_norm, ffn1_w_gu, ffn1_w_down, mix_norm, ffn2_norm, ffn2_w_gu, ffn2_w_down,
              ab_w_in, ab_w_out, pool_w, pool_scale, conv_w_in, conv_w, conv_w_out, final_norm):
    Bn, S, _ = x.shape
    pos = pos0 + jnp.arange(S, dtype=jnp.int32)
    new_win = [[] for _ in range(N_DIL)]
    new_pool, new_conv = [], []
    for l in range(DEPTH):
        x = x + 0.5 * swiglu(rms_norm(x, ffn1_norm[l]), ffn1_w_gu[l], ffn1_w_down[l])
        h = rms_norm(x, mix_norm[l])
        j = l // 2
        if l % 2 == 0:
            past = None if win_caches is None else tuple(c[j] for c in win_caches)
            prev = jnp.zeros((Bn, POOL_STATE, D_POOL), x.dtype) if pool_state is None else pool_state[j]
            mix, rows, prows = ab_mixer(h, pos, pos0, past, prev, ab_w_in[j], ab_w_out[j], pool_w[j], pool_scale[j])
            for g in range(N_DIL):
                new_win[g].append(rows[g])
            new_pool.append(prows)
        else:
            prev = jnp.zeros((Bn, CONV_W - 1, D_MODEL), x.dtype) if conv_state is None else conv_state[j]
            mix, crow = conv_mixer(h, prev, conv_w_in[j], conv_w[j], conv_w_out[j])
            new_conv.append(crow)
        x = x + mix
        x = x + 0.5 * swiglu(rms_norm(x, ffn2_norm[l]), ffn2_w_gu[l], ffn2_w_down[l])
    y = rms_norm(x, final_norm)
    wins = [jnp.stack(new_win[g], axis=0) for g in range(N_DIL)]
    return y, wins, jnp.stack(new_pool, axis=0), jnp.stack(new_conv, axis=0)


def setup_inputs(seed: int = 0) -> dict:
    key = jax.random.key(seed)
    ks = iter(jax.random.split(key, 32))

    def nrm(shape, scale):
        return jax.random.normal(next(ks), shape, F32) * scale

    def gain(shape):
        return 1.0 + nrm(shape, 0.02)

    d = D_MODEL
    return {
        "x_prompt": nrm((BATCH, SEQ, d), 1.0),
        "x_sample": nrm((DEC_BATCH, DEC_SEQ, d), 1.0),
        "cache_win0": nrm((N_AB, DEC_BATCH, min(DIL_PATTERNS[0][0], PAST_LEN), 2, A_HEADS, HEAD_DIM), 1.0),
        "cache_win1": nrm((N_AB, DEC_BATCH, min(DIL_PATTERNS[1][0], PAST_LEN), 2, A_HEADS, HEAD_DIM), 1.0),
        "cache_win2": nrm((N_AB, DEC_BATCH, min(DIL_PATTERNS[2][0], PAST_LEN), 2, A_HEADS, HEAD_DIM), 1.0),
        "state_pool": nrm((N_AB, DEC_BATCH, POOL_STATE, D_POOL), 1.0),
        "state_conv": nrm((N_C, DEC_BATCH, CONV_W - 1, d), 1.0),
        "ffn1_norm": gain((DEPTH, d)),
        "ffn1_w_gu": nrm((DEPTH, d, 2 * D_FF), d ** -0.5),
        "ffn1_w_down": nrm((DEPTH, D_FF, d), D_FF ** -0.5),
        "mix_norm": gain((DEPTH, d)),
        "ffn2_norm": gain((DEPTH, d)),
        "ffn2_w_gu": nrm((DEPTH, d, 2 * D_FF), d ** -0.5),
        "ffn2_w_down": nrm((DEPTH, D_FF, d), D_FF ** -0.5),
        "ab_w_in": nrm((N_AB, d, AB_IN), d ** -0.5),
        "ab_w_out": nrm((N_AB, AB_OUT, d), AB_OUT ** -0.5),
        "pool_w": nrm((N_AB, N_POOL, POOL_GC, POOL_GC), POOL_GC ** -0.5),
        "pool_scale": gain((N_AB, D_POOL)),
        "conv_w_in": nrm((N_C, d, 3 * d), d ** -0.5),
        "conv_w": nrm((N_C, CONV_W, d), CONV_W ** -0.5),
        "conv_w_out": nrm((N_C, d, d), d ** -0.5),
        "final_norm": gain((d,)),
    }


def reference(x_prompt, x_sample, cache_win0, cache_win1, cache_win2, state_pool, state_conv,
              ffn1_norm, ffn1_w_gu, ffn1_w_down, mix_norm, ffn2_norm, ffn2_w_gu, ffn2_w_down,
              ab_w_in, ab_w_out, pool_w, pool_scale, conv_w_in, conv_w, conv_w_out, final_norm):
    weights = (ffn1_norm, ffn1_w_gu, ffn1_w_down, mix_norm, ffn2_norm, ffn2_w_gu, ffn2_w_down,
               ab_w_in, ab_w_out, pool_w, pool_scale, conv_w_in, conv_w, conv_w_out, final_norm)
    y_prompt, win_p, pool_p, conv_p = run_trunk(x_prompt, 0, None, None, None, *weights)
    y_sample, win_s, pool_s, conv_s = run_trunk(x_sample, PAST_LEN, (cache_win0, cache_win1, cache_win2),
                                                state_pool, state_conv, *weights)
    return (y_prompt, y_sample, win_p[0], win_p[1], win_p[2], pool_p, conv_p,
            win_s[0], win_s[1], win_s[2], pool_s, conv_s)
```

```python
import numpy as np
import concourse.bass as bass
import concourse.mybir as mybir
from concourse.bass_utils import run_bass_kernel_spmd

F32 = mybir.dt.float32
BF16 = mybir.dt.bfloat16
AF = mybir.ActivationFunctionType
ALU = mybir.AluOpType

D = 1024
DFF = 2816
NPT = 16
NT = 17
T = NT * 128
NP = NPT * 128
TG = [(0, 512), (512, 512), (1024, 512), (1536, 512), (2048, 128)]
DILS = (1, 4, 16)
WINS = (128, 512, 2048)
EPS = 1e-6
NEG = -30000.0
SAME_ENGINE_SYNC = True
ENABLE_MIX = True
ENABLE_AB = True
ENABLE_CONV = True
import os as _os
if _os.environ.get("K_NOAB"):
    ENABLE_AB = False
if _os.environ.get("K_NOCONV"):
    ENABLE_CONV = False


class Sched:
    ENGS = ("pe", "act", "dve", "pool", "sp")

    def __init__(self, nc, n_dsem=20):
        self.nc = nc
        self.sem = {e: nc.alloc_semaphore("s_" + e) for e in self.ENGS}
        self.cnt = {e: 0 for e in self.ENGS}
        self.prog = {e: [] for e in self.ENGS}
        self.seen = {e: {} for e in self.ENGS}
        self.writer = {}
        self.readers = {}
        self.dsem = {q: [nc.alloc_semaphore("d_%s%d" % (q, i)) for i in range(n_dsem)] for q in ("sp", "pool")}
        self.dcnt = {q: [0] * n_dsem for q in ("sp", "pool")}
        self.dnext = {"sp": 0, "pool": 0}
        self.csem = []
        self.fence = []

    def _semh(self, sk):
        if isinstance(sk, str):
            return self.sem[sk]
        if sk[0] == "cc":
            return self.csem[sk[1]]
        return self.dsem[sk[0]][sk[1]]

    def _deps(self, eng, reads, writes):
        deps = {}

        def add(t):
            if t is not None and deps.get(t[0], 0) < t[1]:
                deps[t[0]] = t[1]
        for k in reads:
            add(self.writer.get(k))
        for k in writes:
            add(self.writer.get(k))
            for r in self.readers.get(k, ()):
                add(r)
        waits = []
        for sk, v in deps.items():
            if sk == eng and (eng == "pe" or not SAME_ENGINE_SYNC):
                continue
            if self.seen[eng].get(sk, 0) >= v:
                continue
            self.seen[eng][sk] = v
            waits.append((sk, v))
        return waits

    def _commit(self, tok, reads, writes):
        for k in writes:
            self.writer[k] = tok
            self.readers[k] = []
        for k in reads:
            self.readers.setdefault(k, []).append(tok)

    def op(self, eng, fn, reads=(), writes=()):
        waits = self._deps(eng, reads, writes)
        self.cnt[eng] += 1
        tok = (eng, self.cnt[eng])
        self.prog[eng].append((waits, fn, (eng, 1)))
        self._commit(tok, reads, writes)
        return tok

    def dma(self, q, out, in_, reads=(), writes=(), **kw):
        waits = self._deps(q, list(reads) + self.fence, writes)
        i = self.dnext[q]
        self.dnext[q] = (i + 1) % len(self.dsem[q])
        sk = (q, i)
        prev = self.dcnt[q][i]
        if prev > 0 and self.seen[q].get(sk, 0) < prev:
            self.seen[q][sk] = prev
            waits.append((sk, prev))
        self.dcnt[q][i] = prev + 16
        tok = (sk, prev + 16)
        self.prog[q].append((waits, lambda e: e.dma_start(out=out, in_=in_, **kw), (sk, 16)))
        self._commit(tok, reads, writes)
        return tok

    def collective(self, ins, outs, groups, reads=(), writes=()):
        waits = self._deps("pool", reads, writes)
        self.csem.append(self.nc.alloc_semaphore("cc%d" % len(self.csem)))
        sk = ("cc", len(self.csem) - 1)
        tok = (sk, 1)
        self.prog["pool"].append((waits, lambda e: e.collective_compute(
            "AllGather", ALU.bypass, replica_groups=groups, ins=[ins], outs=[outs]), (sk, 1)))
        self._commit(tok, reads, writes)
        return tok

    def barrier(self):
        engs = ("pe", "act", "dve")
        keys = [("bar", e) for e in ("pe", "act", "dve", "pool")]
        for e in ("pe", "act", "dve", "pool"):
            self.writer[("bar", e)] = (e, self.cnt[e]) if self.cnt[e] > 0 else None
        for e in engs:
            waits = self._deps(e, keys, ())
            self.prog[e].append((waits, None, None))
        self.fence = [("bar", e) for e in ("pe", "act", "dve")]

    def wait_keys(self, eng, keys):
        waits = self._deps(eng, keys, ())
        self.prog[eng].append((waits, None, None))

    def replay(self):
        nc = self.nc
        with nc.Block() as block:
            def mk(name):
                def run(e):
                    for waits, fn, inc in self.prog[name]:
                        for sk, v in waits:
                            e.wait_ge(self._semh(sk), v)
                        if fn is not None:
                            ins = fn(e)
                            ins.then_inc(self._semh(inc[0]), inc[1])
                return run
            block.tensor(mk("pe"))
            block.scalar(mk("act"))
            block.vector(mk("dve"))
            block.gpsimd(mk("pool"))
            block.sync(mk("sp"))


def build_program(DEPTH, n_cores):
    N_AB = (DEPTH + 1) // 2
    N_C = DEPTH // 2
    nc = bass.Bass("TRN2", target_bir_lowering=False)
    S = Sched(nc)
    groups = [[2 * i, 2 * i + 1] for i in range(n_cores // 2)]

    def din(name, shape, dt=F32):
        return nc.dram_tensor(name, list(shape), dt, kind="ExternalInput").ap()

    def dout(name, shape):
        return nc.dram_tensor(name, list(shape), F32, kind="ExternalOutput").ap()

    def dint(name, shape, dt):
        return nc.dram_tensor(name, list(shape), dt).ap()

    def sb(name, shape, dt):
        return nc.alloc_sbuf_tensor(name, list(shape), dt)

    x_d = din("x", [T, D])
    cw_d = [din("cw%d" % g, [N_AB, 16, WINS[g], 2, 8, 64]) for g in range(3)]
    spool_d = din("spool", [N_AB, 16, 15, 512])
    sconv_d = din("sconv", [max(N_C, 1), 16, 2, D])
    wgu_d = [din("ffn1_w_gu", [DEPTH, D, 2 * DFF]), din("ffn2_w_gu", [DEPTH, D, 2 * DFF])]
    wdn_d = [din("ffn1_w_down", [DEPTH, DFF, D]), din("ffn2_w_down", [DEPTH, DFF, D])]
    abin_d = din("ab_w_in", [N_AB, D, 5120])
    about_d = din("ab_w_out", [N_AB, D, D])
    poolw_d = din("pool_w", [N_AB, 4, 128, 128])
    cvin_d = din("conv_w_in", [max(N_C, 1), D, 3 * D])
    cvout_d = din("conv_w_out", [max(N_C, 1), D, D])
    NV = 3 * DEPTH + N_AB + 3 * N_C
    vecs_d = din("vecs", [NV, D])
    gfin_d = din("gfin", [1, D])
    ident_d = din("ident", [128, 128])
    rope_d = din("rope", [3, NT, 128, 64])
    consts_d = din("consts", [128, 4096])

    y_d = dout("y", [T, D])
    kvrow_d = [dout("kvrow%d" % g, [N_AB, min(WINS[g], NP), 2, 8, 64]) for g in range(3)]
    kvs_d = [dout("kvs%d" % g, [N_AB, 128, 2, 8, 64]) for g in range(3)]
    poolp_d = dout("poolp", [N_AB, 15, 512])
    pools_d = dout("pools", [N_AB, 16, 15, 512])
    convp_d = dout("convp", [max(N_C, 1), 2, D])
    convs_d = dout("convs", [max(N_C, 1), 16, 2, D])
    out_keys = []

    x_sb = sb("x_sb", [128, NT, D], F32)
    hT = sb("hT", [128, 8, T], BF16)
    NSLOT = 2
    wring = sb("wring", [128, NSLOT, 6144], BF16)
    big = sb("big", [128, 8704], F32)
    ident_bf = sb("ident_bf", [128, 128], BF16)
    ident_f = sb("ident_f", [128, 128], F32)
    vT = sb("vT", [128, 8, NV], F32)
    vrows = big[0:32, 0:D]
    ss = sb("ss", [128, NT], F32)
    sd = sb("sd", [128, NT], F32)
    rstd = sb("rstd", [128, NT], F32)
    eps_t = sb("eps_t", [128, 1], F32)
    xn = sb("xn", [128, 2, D], BF16)
    junk = xn[:, 0, :]
    gfin = big[:, 0:D]
    ystage = big[:, D:3 * D].rearrange("p (b n) -> p b n", b=2)
    scr = sb("scr", [128, 6144], F32)
    den_acc = sb("den_acc", [64, T], F32)
    CA = 2848
    ca = sb("ca", [128, CA], BF16)
    cb = sb("cb", [128, 640], F32)
    ps = nc.alloc_psum_tensor("ps", [128, 8, 512], F32)

    def psb(bank):
        return ps[:, bank, :].bitcast(BF16)

    chunks = []

    def slotv(s, a, b):
        return wring[:, s, a:b]

    def kview(ap2d):
        return ap2d.rearrange("(k p) n -> p k n", p=128)

    def add_ffn_chunks(l, f):
        for c in range(11):
            chunks.append([
                (lambda s: slotv(s, 0, 2048).rearrange("p (k n) -> p k n", k=8), kview(wgu_d[f][l])[:, :, c * 256:(c + 1) * 256]),
                (lambda s: slotv(s, 2048, 4096).rearrange("p (k n) -> p k n", k=8), kview(wgu_d[f][l])[:, :, DFF + c * 256:DFF + (c + 1) * 256]),
                (lambda s: slotv(s, 4096, 6144).rearrange("p (k n) -> p k n", k=2), kview(wdn_d[f][l][c * 256:(c + 1) * 256, :])),
            ])

    def k8(a, b, n):
        return lambda s: slotv(s, a, b).rearrange("p (k n) -> p k n", k=8)

    def add_pool_chunk(j):
        chunks.append([
            (k8(0, 4096, 512), kview(abin_d[j])[:, :, 4608:5120]),
            (lambda s: slotv(s, 4096, 4608).rearrange("p (g d) -> p g d", g=4), poolw_d[j].rearrange("g c d -> c g d")),
        ])

    def add_out_chunks(wd):
        for hf in range(2):
            chunks.append([(k8(0, 4096, 512), kview(wd)[:, :, hf * 512:(hf + 1) * 512])])

    def add_ab_chunks(j):
        add_pool_chunk(j)
        for g in (2, 1, 0):
            for hh in range(2):
                chunks.append([(k8(part * 2048, (part + 1) * 2048, 256),
                                kview(abin_d[j])[:, :, g * 1536 + part * 512 + hh * 256:g * 1536 + part * 512 + hh * 256 + 256]) for part in range(3)])
        add_pool_chunk(j)
        add_out_chunks(about_d[j])

    def add_conv_chunks(j):
        for rep in range(2):
            for sc in range(4):
                chunks.append([(k8(part * 2048, (part + 1) * 2048, 256),
                                kview(cvin_d[j])[:, :, part * 1024 + sc * 256:part * 1024 + sc * 256 + 256]) for part in range(3)])
        add_out_chunks(cvout_d[j])

    for l in range(DEPTH):
        add_ffn_chunks(l, 0)
        if ENABLE_MIX:
            if l % 2 == 0 and ENABLE_AB:
                add_ab_chunks(l // 2)
            if l % 2 == 1 and ENABLE_CONV:
                add_conv_chunks(l // 2)
        add_ffn_chunks(l, 1)

    wstate = {"next_load": 0, "next_use": 0}

    def issue_loads(upto):
        while wstate["next_load"] <= min(upto, len(chunks) - 1):
            i = wstate["next_load"]
            s = i % NSLOT
            for dst_fn, src in chunks[i]:
                S.dma("pool", dst_fn(s), src, writes=[("w", s)])
            wstate["next_load"] += 1

    def next_chunk():
        i = wstate["next_use"]
        wstate["next_use"] += 1
        assert i < wstate["next_load"], "chunk not loaded"
        return i % NSLOT

    def chunk_done():
        issue_loads(wstate["next_load"])

    S.op("dve", lambda e: e.memset(eps_t[:], EPS), writes=["eps"])
    for t in range(NT):
        S.dma("sp", x_sb[:, t, :], x_d[t * 128:(t + 1) * 128, :], writes=[("x", t)])
    S.dma("sp", ident_f[:], ident_d, writes=["ident_f"])
    S.dma("pool", ident_bf[:], ident_d, writes=["ident_bf"])
    S.dma("sp", vrows[0:NV, :], vecs_d, writes=["vrows"])
    S.dma("pool", ca[:], consts_d[:, 0:CA], writes=["ca"])
    S.dma("sp", cb[:], consts_d[:, 3072:3072 + 640], writes=["cb"])
    issue_loads(NSLOT - 1)
    def f_vt(e):
        for k in range(8):
            ins = e.transpose(out=ps[:, 0, k * NV:(k + 1) * NV], in_=vrows[0:NV, k * 128:(k + 1) * 128], identity=ident_f[0:NV, 0:NV])
        return ins
    S.op("pe", f_vt, reads=["vrows", "ident_f"], writes=[("ps", 0)])
    S.op("act", lambda e: e.copy(out=vT[:], in_=ps[:, 0, 0:8 * NV].rearrange("p (k v) -> p k v", k=8)), reads=[("ps", 0)], writes=["vT"])

    ALL_HT = [("hT", t) for t in range(NT)]
    nstate = {"i": 0}

    def norm_phase(vidx):
        for t in range(NT):
            S.op("act", lambda e, t=t: e.activation(out=junk, in_=x_sb[:, t, :], func=AF.Square, accum_out=ss[:, t:t + 1]),
                 reads=[("x", t)], writes=[("ss", t)])
        S.op("act", lambda e: e.activation(out=sd[:], in_=ss[:], func=AF.Sqrt, scale=1.0 / D, bias=eps_t[:]),
             reads=[("ss", t) for t in range(NT)] + ["eps"], writes=["sd"])
        S.op("dve", lambda e: e.reciprocal(out=rstd[:], in_=sd[:]), reads=["sd"], writes=["rstd"])
        for t in range(NT):
            i = nstate["i"]
            nstate["i"] += 1
            b = i % 2
            S.op("dve", lambda e, t=t, b=b: e.tensor_scalar(out=xn[:, b, :], in0=x_sb[:, t, :], scalar1=rstd[:, t:t + 1], scalar2=None, op0=ALU.mult),
                 reads=[("x", t), "rstd"], writes=[("xn", b)])

            def f_tr(e, b=b):
                for k in range(8):
                    ins = e.transpose(out=psb(b)[:, k * 128:(k + 1) * 128], in_=xn[:, b, k * 128:(k + 1) * 128], identity=ident_bf[:])
                return ins
            S.op("pe", f_tr, reads=[("xn", b), "ident_bf"], writes=[("ps", b)])
            g3 = vT[:, :, vidx:vidx + 1].broadcast_to([128, 8, 128])
            S.op("dve", lambda e, t=t, b=b, g3=g3: e.tensor_tensor(out=hT[:, :, t * 128:(t + 1) * 128], in0=psb(b).rearrange("p (k n) -> p k n", k=8), in1=g3, op=ALU.mult),
                 reads=[("ps", b), "vT"], writes=[("hT", t)])

    sg = big[:, 0:2048].rearrange("p (b h n) -> p b h n", b=2, h=2)
    hid = big[:, 2048:3072].bitcast(BF16).rearrange("p (b h n) -> p b h n", b=2, h=2)
    fstate = {"set": 0}

    def ffn(l, f):
        norm_phase(f * 2 * DEPTH + l if f == 0 else 2 * DEPTH + l)
        S.barrier()
        seq = [(c, tg) for c in range(11) for tg in range(5)]
        slots = {}

        def gu(i):
            c, tg = seq[i]
            if c not in slots:
                slots[c] = next_chunk()
            s = slots[c]
            t0, tn = TG[tg]
            b = i % 2
            hkeys = [("hT", t) for t in range(t0 // 128, (t0 + tn) // 128)]
            for bank, off in ((0, 0), (1, 128), (2, 2048), (3, 2048 + 128)):
                def f_mm(e, bank=bank, off=off, s=s, t0=t0, tn=tn):
                    for k in range(8):
                        base = (off // 2048) * 2048 + k * 256 + (off % 2048)
                        ins = e.matmul(ps[:, bank, 0:tn], lhsT=wring[:, s, base:base + 128], rhs=hT[:, k, t0:t0 + tn], start=(k == 0), stop=(k == 7))
                    return ins
                S.op("pe", f_mm, reads=[("w", s)] + hkeys, writes=[("ps", bank)])
            for h in range(2):
                S.op("act", lambda e, h=h, b=b, tn=tn: e.activation(out=sg[:, b, h, 0:tn], in_=ps[:, h, 0:tn], func=AF.Silu),
                     reads=[("ps", h)], writes=[("sg", b, h)])
                S.op("dve", lambda e, h=h, b=b, tn=tn: e.tensor_tensor(out=hid[:, b, h, 0:tn], in0=sg[:, b, h, 0:tn], in1=ps[:, 2 + h, 0:tn], op=ALU.mult),
                     reads=[("sg", b, h), ("ps", 2 + h)], writes=[("hid", b, h)])

        def down(i):
            c, tg = seq[i]
            s = slots[c]
            t0, tn = TG[tg]
            b = i % 2
            for tt in range(tn // 128):
                t = t0 // 128 + tt
                st = fstate["set"]
                fstate["set"] ^= 1
                b0 = 4 + 2 * st

                def f_dn(e, tt=tt, b0=b0, s=s, b=b):
                    for ncol in range(2):
                        for h in range(2):
                            ins = e.matmul(ps[:, b0 + ncol, :], lhsT=hid[:, b, h, tt * 128:(tt + 1) * 128],
                                           rhs=wring[:, s, 4096 + h * 1024 + ncol * 512:4096 + h * 1024 + (ncol + 1) * 512],
                                           start=(h == 0), stop=(h == 1))
                    return ins
                S.op("pe", f_dn, reads=[("w", s), ("hid", b, 0), ("hid", b, 1)], writes=[("ps", b0), ("ps", b0 + 1)])
                S.op("dve", lambda e, t=t, b0=b0: e.scalar_tensor_tensor(out=x_sb[:, t, :], in0=ps[:, b0:b0 + 2, :].rearrange("p a n -> p (a n)"), scalar=0.5, in1=x_sb[:, t, :], op0=ALU.mult, op1=ALU.add),
                     reads=[("ps", b0), ("ps", b0 + 1), ("x", t)], writes=[("x", t)])

        gu(0)
        for i in range(1, len(seq)):
            gu(i)
            down(i - 1)
            if seq[i - 1][1] == 4:
                chunk_done()
        down(len(seq) - 1)
        chunk_done()

    class Scr:
        off = 0

        def reset(self):
            self.off = 0

        def f32(self, n):
            a = scr[:, self.off:self.off + n]
            self.off += n
            assert self.off <= 6144, self.off
            return a

        def bf16(self, n):
            m = (n + 1) // 2
            a = scr[:, self.off:self.off + m].bitcast(BF16)
            self.off += m
            assert self.off <= 6144, self.off
            return a
    SC = Scr()
    ENGS4 = ("pe", "act", "dve", "pool")
    BARK = [("bar", e) for e in ENGS4]
    scr_keys = []

    def phase_barrier():
        if not _os.environ.get("K_PB_NOKEYS"):
            for e in ("pe", "act", "dve"):
                waits = S._deps(e, scr_keys, scr_keys)
                S.prog[e].append((waits, None, None))
        if not _os.environ.get("K_PB_NOBAR"):
            S.barrier()
        del scr_keys[:]
        SC.reset()

    def skey(k):
        scr_keys.append(k)
        return k

    halo_on = cb[:, 64:65]
    Ef = cb[0:64, 128:640].rearrange("p (c m) -> p c m", c=4)
    mrow_n = ca[:, 0:512]
    mrow_0 = ca[:, 512:1024]
    smask = [ca[:, 1024 + g * 256:1024 + (g + 1) * 256] for g in range(3)]
    cmask = [ca[:, 1792:1824], ca[:, 1824:1952], ca[:, 1952:2208]]
    oh = ca[:, 2208:2720].rearrange("p (h m) -> p h m", h=8)
    zlhs = ca[:, 2720:2848]
    CW0 = 3 * DEPTH + N_AB

    def out_proj(lhsT_all, tagk):
        cnt = 0
        for hf in range(2):
            s = next_chunk()
            for t in range(NT):
                bank = 6 + (cnt % 2)
                cnt += 1

                def f_mm(e, t=t, s=s, bank=bank):
                    for k in range(8):
                        ins = e.matmul(ps[:, bank, :], lhsT=lhsT_all[:, k, t * 128:(t + 1) * 128], rhs=wring[:, s, k * 512:(k + 1) * 512], start=(k == 0), stop=(k == 7))
                    return ins
                S.op("pe", f_mm, reads=[("w", s), (tagk, t)], writes=[("ps", bank)])
                S.op("dve", lambda e, t=t, hf=hf, bank=bank: e.tensor_tensor(out=x_sb[:, t, hf * 512:(hf + 1) * 512], in0=ps[:, bank, :], in1=x_sb[:, t, hf * 512:(hf + 1) * 512], op=ALU.add),
                     reads=[("ps", bank), ("x", t)], writes=[("x", t)])
            chunk_done()

    cx_in = [dint("cx_in%d" % j, [2, D], F32) for j in range(N_C)]
    cx_out = [dint("cx_out%d" % j, [4, D], F32) for j in range(N_C)]

    def conv_mixer(l):
        j = l // 2
        norm_phase(DEPTH + l)
        phase_barrier()
        mT = big[:, :].bitcast(BF16).rearrange("p (k n) -> p k n", k=8)
        urows = SC.f32(2048).rearrange("p (a n) -> p a n", a=2)
        rowbuf = SC.f32(1024)
        t1 = rowbuf.rearrange("p (a n) -> p a n", a=2)
        RBK = ["rowbuf", ("c_t1", 0), ("c_t1", 1)]
        Ub = SC.f32(2 * 514).rearrange("p (a n) -> p a n", a=2)
        yb = SC.f32(512)
        tails = SC.f32(16).rearrange("p (f n) -> p f n", f=8)
        halo = SC.f32(16).rearrange("p (f n) -> p f n", f=8)
        sprev = SC.f32(256).rearrange("p (f b n) -> p f b n", f=8, b=16)
        Us = SC.f32(2 * 160).rearrange("p (a b n) -> p a b n", a=2, b=16)
        ys = SC.f32(128).rearrange("p (b n) -> p b n", b=16)
        wi = CW0 + 3 * j
        for sc in range(4):
            s = next_chunk()
            for ti, t in enumerate((15, 16)):
                for pi, part in enumerate((1, 2)):
                    bank = ti * 2 + pi

                    def f_mm(e, t=t, s=s, bank=bank, part=part):
                        for k in range(8):
                            ins = e.matmul(ps[:, bank, 0:256], lhsT=hT[:, k, t * 128:(t + 1) * 128], rhs=wring[:, s, part * 2048 + k * 256:part * 2048 + (k + 1) * 256], start=(k == 0), stop=(k == 7))
                        return ins
                    S.op("pe", f_mm, reads=[("w", s), ("hT", t)], writes=[("ps", bank)])
                S.op("act", lambda e, ti=ti: e.copy(out=t1[:, ti, 0:256], in_=ps[:, ti * 2, 0:256]), reads=[("ps", ti * 2)], writes=[("c_t1", ti)])
                S.op("dve", lambda e, ti=ti, sc=sc: e.tensor_tensor(out=urows[:, ti, sc * 256:(sc + 1) * 256], in0=t1[:, ti, 0:256], in1=ps[:, ti * 2 + 1, 0:256], op=ALU.mult),
                     reads=[("c_t1", ti), ("ps", ti * 2 + 1)], writes=[skey(("urows", ti, sc))])
            chunk_done()
        ur0 = [("urows", 0, sc) for sc in range(4)]
        ur1 = [("urows", 1, sc) for sc in range(4)]
        S.dma("sp", convp_d[j], urows[126:128, 0, :], reads=ur0, writes=[("convp", j)])
        S.dma("sp", convs_d[j, :, 0, :], urows[6:128:8, 1, :], reads=ur1, writes=[("convs0", j)])
        S.dma("sp", convs_d[j, :, 1, :], urows[7:128:8, 1, :], reads=ur1, writes=[("convs1", j)])
        out_keys.extend([("convp", j), ("convs0", j), ("convs1", j)])
        S.dma("sp", cx_in[j], urows[126:128, 0, :], reads=ur0, writes=[("cx_in", j)])
        S.collective(cx_in[j], cx_out[j], groups, reads=[("cx_in", j)], writes=[("cx_out", j)])
        S.dma("sp", rowbuf[0:2, :], cx_out[j][0:2, :], reads=[("cx_out", j)], writes=[skey("rowbuf")] + RBK[1:])

        def f_tr2(e):
            for f in range(8):
                ins = e.transpose(out=ps[:, 4, f * 2:(f + 1) * 2], in_=rowbuf[0:2, f * 128:(f + 1) * 128], identity=ident_f[0:2, 0:2])
            return ins
        S.op("pe", f_tr2, reads=["rowbuf", "ident_f"], writes=[("ps", 4)])
        S.op("dve", lambda e: e.tensor_scalar(out=halo[:, :, :], in0=ps[:, 4, 0:16].rearrange("p (f n) -> p f n", f=8), scalar1=halo_on, scalar2=None, op0=ALU.mult),
             reads=[("ps", 4), "cb"], writes=["c_halo"])
        S.dma("sp", rowbuf[0:32, :], sconv_d[j].rearrange("b n d -> (b n) d"), reads=[("ps", 4)], writes=[skey("rowbuf")] + RBK[1:])

        def f_tr32(e):
            for f in range(8):
                ins = e.transpose(out=ps[:, 5, f * 32:(f + 1) * 32], in_=rowbuf[0:32, f * 128:(f + 1) * 128], identity=ident_f[0:32, 0:32])
            return ins
        S.op("pe", f_tr32, reads=["rowbuf", "ident_f"], writes=[("ps", 5)])
        S.op("act", lambda e: e.copy(out=sprev[:, :, :, :], in_=ps[:, 5, 0:256].rearrange("p (f b n) -> p f b n", f=8, b=16)), reads=[("ps", 5)], writes=["c_sprev"])
        ui = 0
        for sc in range(4):
            s = next_chunk()
            for tg in range(5):
                t0, tn = TG[tg]
                hkeys = [("hT", t) for t in range(t0 // 128, (t0 + tn) // 128)]
                for fh in range(2):
                    for part in range(3):
                        bank = fh * 3 + part

                        def f_mm(e, s=s, bank=bank, part=part, fh=fh, t0=t0, tn=tn):
                            for k in range(8):
                                base = part * 2048 + k * 256 + fh * 128
                                ins = e.matmul(ps[:, bank, 0:tn], lhsT=wring[:, s, base:base + 128], rhs=hT[:, k, t0:t0 + tn], start=(k == 0), stop=(k == 7))
                            return ins
                        S.op("pe", f_mm, reads=[("w", s)] + hkeys, writes=[("ps", bank)])
                for fh in range(2):
                    f = sc * 2 + fh
                    bgb, bgc, bv = fh * 3, fh * 3 + 1, fh * 3 + 2
                    w0, w1, w2 = (vT[:, f, wi + tt:wi + tt + 1] for tt in range(3))
                    ub = ui % 2
                    ui += 1
                    S.op("act", lambda e, ub=ub, bgc=bgc, tn=tn: e.copy(out=t1[:, ub, 0:tn], in_=ps[:, bgc, 0:tn]), reads=[("ps", bgc)], writes=[("c_t1", ub), "rowbuf"])
                    mkeys = [("mT", t) for t in range(t0 // 128, (t0 + tn) // 128)]
                    if tg < 4:
                        S.op("dve", lambda e, ub=ub, bv=bv: e.tensor_tensor(out=Ub[:, ub, 2:514], in0=t1[:, ub, 0:512], in1=ps[:, bv, :], op=ALU.mult),
                             reads=[("c_t1", ub), ("ps", bv)], writes=[("c_U", ub)])
                        src = halo[:, f, :] if tg == 0 else tails[:, f, :]
                        S.op("pool", lambda e, ub=ub, src=src: e.tensor_copy(out=Ub[:, ub, 0:2], in_=src), reads=["c_halo", ("c_tail", f)], writes=[("c_Uh", ub)])
                        S.op("pool", lambda e, ub=ub, f=f: e.tensor_copy(out=tails[:, f, :], in_=Ub[:, ub, 512:514]), reads=[("c_U", ub), ("c_Uh", ub)], writes=[("c_tail", f)])
                        S.op("dve", lambda e, ub=ub, w0=w0: e.tensor_scalar(out=yb[:, :], in0=Ub[:, ub, 0:512], scalar1=w0, scalar2=None, op0=ALU.mult),
                             reads=[("c_U", ub), ("c_Uh", ub), "vT"], writes=["c_y"])
                        S.op("dve", lambda e, ub=ub, w1=w1: e.scalar_tensor_tensor(out=yb[:, :], in0=Ub[:, ub, 1:513], scalar=w1, in1=yb[:, :], op0=ALU.mult, op1=ALU.add),
                             reads=["c_y"], writes=["c_y"])
                        S.op("dve", lambda e, ub=ub, w2=w2: e.scalar_tensor_tensor(out=yb[:, :], in0=Ub[:, ub, 2:514], scalar=w2, in1=yb[:, :], op0=ALU.mult, op1=ALU.add),
                             reads=["c_y"], writes=["c_y"])
                        S.op("dve", lambda e, f=f, t0=t0, bgb=bgb: e.tensor_tensor(out=mT[:, f, t0:t0 + 512], in0=yb[:, :], in1=ps[:, bgb, :], op=ALU.mult),
                             reads=["c_y", ("ps", bgb)], writes=mkeys)
                    else:
                        v3 = lambda ap: ap.rearrange("p (b n) -> p b n", b=16)
                        S.op("dve", lambda e, ub=ub, bv=bv: e.tensor_tensor(out=Us[:, ub, :, 2:10], in0=v3(t1[:, ub, 0:128]), in1=v3(ps[:, bv, 0:128]), op=ALU.mult),
                             reads=[("c_t1", ub), ("ps", bv)], writes=[("c_Us", ub)])
                        S.op("pool", lambda e, ub=ub, f=f: e.tensor_copy(out=Us[:, ub, :, 0:2], in_=sprev[:, f, :, :]), reads=["c_sprev"], writes=[("c_Ush", ub)])
                        S.op("dve", lambda e, ub=ub, w0=w0: e.tensor_scalar(out=ys[:, :, :], in0=Us[:, ub, :, 0:8], scalar1=w0, scalar2=None, op0=ALU.mult),
                             reads=[("c_Us", ub), ("c_Ush", ub), "vT"], writes=["c_ys"])
                        S.op("dve", lambda e, ub=ub, w1=w1: e.scalar_tensor_tensor(out=ys[:, :, :], in0=Us[:, ub, :, 1:9], scalar=w1, in1=ys[:, :, :], op0=ALU.mult, op1=ALU.add),
                             reads=["c_ys"], writes=["c_ys"])
                        S.op("dve", lambda e, ub=ub, w2=w2: e.scalar_tensor_tensor(out=ys[:, :, :], in0=Us[:, ub, :, 2:10], scalar=w2, in1=ys[:, :, :], op0=ALU.mult, op1=ALU.add),
                             reads=["c_ys"], writes=["c_ys"])
                        S.op("dve", lambda e, f=f, bgb=bgb: e.tensor_tensor(out=v3(mT[:, f, 2048:2176]), in0=ys[:, :, :], in1=v3(ps[:, bgb, 0:128]), op=ALU.mult),
                             reads=["c_ys", ("ps", bgb)], writes=mkeys)
            chunk_done()
        out_proj(mT, "mT")

    NHT = (1, 4, 16)
    recs = [dint("recs%d" % j, [6 * NT * 128, 768], BF16) for j in range(N_AB)]
    ex_in = [[[dint("ex_in%d_%d_%d" % (j, g, hh), [NHT[g] * 128, 256], F32) for hh in range(2)] for g in range(3)] for j in range(N_AB)]
    ex_out = [[[dint("ex_out%d_%d_%d" % (j, g, hh), [2 * NHT[g] * 128, 256], F32) for hh in range(2)] for g in range(3)] for j in range(N_AB)]
    pt_in = [dint("pt_in%d" % j, [128, 60], F32) for j in range(N_AB)]
    pt_out = [dint("pt_out%d" % j, [256, 60], F32) for j in range(N_AB)]

    def tile_geom(g, jt):
        d = DILS[g]
        cls = 16 // d
        r, n = jt // cls, jt % cls
        start = r + d * 128 * n
        return d, r, n, start

    def ab_mixer(l):
        j = l // 2
        norm_phase(DEPTH + l)
        phase_barrier()
        O_acc = big[:, :].rearrange("p (c n) -> p c n", c=4)
        S.op("pool", lambda e: e.memset(big[:, :], 0.0), writes=["Oacc"])
        S.op("pool", lambda e: e.memset(den_acc[:, :], 0.0), writes=["den"])
        ptail = SC.f32(64).rearrange("p (g n) -> p g n", g=4)
        phalo = cb[:, 66:126].rearrange("p (g n) -> p g n", g=4)
        s = next_chunk()

        def f_pt(e, s=s):
            for gi in range(4):
                for k in range(8):
                    ins = e.matmul(ps[:, 2, gi * 16:(gi + 1) * 16], lhsT=wring[:, s, k * 512 + gi * 128:k * 512 + (gi + 1) * 128], rhs=hT[:, k, NP - 16:NP], start=(k == 0), stop=(k == 7))
            return ins
        S.op("pe", f_pt, reads=[("w", s), ("hT", 15)], writes=[("ps", 2)])
        chunk_done()
        S.op("act", lambda e: e.copy(out=ptail[:, :, :], in_=ps[:, 2, 0:64].rearrange("p (g n) -> p g n", g=4)), reads=[("ps", 2)], writes=[skey("ptail")])
        S.dma("sp", pt_in[j].rearrange("p (g n) -> p g n", g=4), ptail[:, :, 1:16], reads=["ptail"], writes=[("pt_in", j)])
        S.collective(pt_in[j], pt_out[j], groups, reads=[("pt_in", j)], writes=[("pt_out", j)])
        S.dma("sp", phalo[:, :, :], pt_out[j][0:128, :].rearrange("p (g n) -> p g n", g=4), reads=[("pt_out", j)], writes=[skey("phalo_raw")])
        S.op("dve", lambda e: e.tensor_scalar(out=phalo[:, :, :], in0=phalo[:, :, :], scalar1=halo_on, scalar2=None, op0=ALU.mult), reads=["phalo_raw", "cb"], writes=["phalo"])

        AB_STOP = int(_os.environ.get("K_AB_STOP", "9"))
        used = [1]

        def bail():
            for _ in range(10 - used[0]):
                next_chunk()
                chunk_done()
        if AB_STOP <= 1:
            return bail()
        ropeG = SC.f32(NT * 64).rearrange("p (t n) -> p t n", t=NT)
        tmp = SC.f32(1024).rearrange("p (a n) -> p a n", a=4)
        rot = SC.f32(1024).rearrange("p (a n) -> p a n", a=2)
        rot_bf2 = SC.bf16(1024).rearrange("p (a n) -> p a n", a=2)
        vf = SC.f32(512).rearrange("p (a n) -> p a n", a=2)
        rec = SC.bf16(2 * 768).rearrange("p (a n) -> p a n", a=2)
        ai = 0
        pA = 0
        pend = [None]
        for g in (2, 1, 0):
            d = DILS[g]
            rows_g = min(WINS[g], NP)
            S.dma("sp", ropeG[:, :, :], rope_d[g].rearrange("t p n -> p t n"), writes=[skey("ropeG")])
            for hh in range(2):
                pid = g * 2 + hh
                s = next_chunk()
                for jt in range(NT):
                    if jt < 16:
                        d_, r, n, start = tile_geom(g, jt)
                        tok = lambda k, start=start, d=d: hT[:, k, start:start + d * 127 + 1:d]
                        hk = ALL_HT[:16]
                    else:
                        tok = lambda k: hT[:, k, NP:T]
                        hk = [("hT", 16)]
                    b = ai % 2
                    ai += 1
                    bA = 4 + 2 * (pA % 2)
                    pA += 1

                    def f_mm(e, s=s, bA=bA, tok=tok):
                        for part in range(3):
                            for k in range(8):
                                o = ps[:, bA + part // 2, (part % 2) * 256:(part % 2) * 256 + 256]
                                ins = e.matmul(o, lhsT=tok(k), rhs=wring[:, s, part * 2048 + k * 256:part * 2048 + (k + 1) * 256], start=(k == 0), stop=(k == 7))
                        return ins
                    S.op("pe", f_mm, reads=[("w", s)] + hk, writes=[("ps", bA), ("ps", bA + 1)])
                    qk4 = ps[:, bA, :].rearrange("p (a h f) -> p a h f", a=8, h=2)
                    cosb = ropeG[:, jt, 0:32].unsqueeze(1).broadcast_to([128, 8, 32])
                    sinb = ropeG[:, jt, 32:64].unsqueeze(1).broadcast_to([128, 8, 32])
                    t4 = lambda i: tmp[:, i, :].rearrange("p (a f) -> p a f", a=8)
                    for i, (hsel, tb) in enumerate(((0, cosb), (1, sinb), (1, cosb), (0, sinb))):
                        S.op("dve", lambda e, i=i, hsel=hsel, tb=tb, qk4=qk4: e.tensor_tensor(out=t4(i), in0=qk4[:, :, hsel, :], in1=tb, op=ALU.mult),
                             reads=[("ps", bA), "ropeG"], writes=[("a_tmp", i)])
                    rot4 = rot[:, b, :].rearrange("p (a h f) -> p a h f", a=8, h=2)
                    S.op("pool", lambda e, rot4=rot4: e.tensor_tensor(out=rot4[:, :, 0, :], in0=t4(0), in1=t4(1), op=ALU.subtract),
                         reads=[("a_tmp", 0), ("a_tmp", 1)], writes=[skey(("a_rot0", b))])
                    S.op("pool", lambda e, rot4=rot4: e.tensor_tensor(out=rot4[:, :, 1, :], in0=t4(2), in1=t4(3), op=ALU.add),
                         reads=[("a_tmp", 2), ("a_tmp", 3)], writes=[skey(("a_rot1", b))])
                    S.op("act", lambda e, b=b: e.copy(out=rot_bf2[:, b, :], in_=rot[:, b, :]), reads=[("a_rot0", b), ("a_rot1", b)], writes=[("a_rotbf", b)])
                    S.op("act", lambda e, b=b, bA=bA: e.copy(out=vf[:, b, :], in_=ps[:, bA + 1, 0:256]), reads=[("ps", bA + 1)], writes=[skey(("a_vf", b))])
                    S.op("act", lambda e, b=b: e.copy(out=rec[:, b, 512:768], in_=vf[:, b, :]), reads=[("a_vf", b)], writes=[skey(("a_recv", b))])
                    kk = rot[:, b, 256:512].rearrange("p (h f) -> p h f", h=4)
                    vv = vf[:, b, :].rearrange("p (h f) -> p h f", h=4)
                    if jt == 16:
                        okk, okv = ("kvsK", j, g, hh), ("kvsV", j, g, hh)
                        S.dma("sp", kvs_d[g][j, :, 0, hh * 4:hh * 4 + 4, :], kk, reads=[("a_rot0", b), ("a_rot1", b)], writes=[okk])
                        S.dma("sp", kvs_d[g][j, :, 1, hh * 4:hh * 4 + 4, :], vv, reads=[("a_vf", b)], writes=[okv])
                        out_keys.extend([okk, okv])
                    elif start + d * 127 >= NP - rows_g and n == 16 // d - 1:
                        r0 = start - (NP - rows_g)
                        okk, okv = ("kvrK", j, g, hh, jt), ("kvrV", j, g, hh, jt)
                        S.dma("sp", kvrow_d[g][j, r0:r0 + d * 127 + 1:d, 0, hh * 4:hh * 4 + 4, :], kk, reads=[("a_rot0", b), ("a_rot1", b)], writes=[okk])
                        S.dma("sp", kvrow_d[g][j, r0:r0 + d * 127 + 1:d, 1, hh * 4:hh * 4 + 4, :], vv, reads=[("a_vf", b)], writes=[okv])
                        out_keys.extend([okk, okv])
                    bT = pA % 2
                    is_halo = jt < 16 and n == 16 // d - 1
                    hi = r if jt < 16 else 0

                    def back(b=b, bT=bT, jt=jt, pid=pid, g=g, hh=hh, is_halo=is_halo, hi=hi):
                        def f_tr(e):
                            for c4 in range(4):
                                ins = e.transpose(out=psb(bT)[:, c4 * 128:(c4 + 1) * 128], in_=rot_bf2[:, b, c4 * 128:(c4 + 1) * 128], identity=ident_bf[:])
                            return ins
                        S.op("pe", f_tr, reads=[("a_rotbf", b), "ident_bf"], writes=[("ps", bT)])
                        S.op("act", lambda e: e.copy(out=rec[:, b, 0:512], in_=psb(bT)[:, 0:512]), reads=[("ps", bT)], writes=[skey(("a_recqk", b))])
                        rrow = (pid * NT + jt) * 128
                        S.dma("sp", recs[j][rrow:rrow + 128, :], rec[:, b, :], reads=[("a_recqk", b), ("a_recv", b)], writes=[("rec", j, pid, jt)])
                        if is_halo:
                            erow = hi * 128
                            S.dma("sp", ex_in[j][g][hh][erow:erow + 128, :], rec[:, b, 256:768].bitcast(F32), reads=[("a_recqk", b), ("a_recv", b)], writes=[("ex_in", j, g, hh, hi)])
                    if pend[0] is not None:
                        pend[0]()
                    pend[0] = back
                pend[0]()
                pend[0] = None
                chunk_done()
                S.collective(ex_in[j][g][hh], ex_out[j][g][hh], groups, reads=[("ex_in", j, g, hh, hi_) for hi_ in range(NHT[g])], writes=[("ex_out", j, g, hh)])

        used[0] = 7
        if AB_STOP <= 2:
            return bail()
        phase_barrier()
        qk_c = SC.bf16(3 * 512).rearrange("p (a n) -> p a n", a=3)
        v_c = SC.bf16(3 * 256).rearrange("p (a n) -> p a n", a=3)
        k_p = SC.bf16(3 * 256).rearrange("p (a n) -> p a n", a=3)
        v_p = SC.bf16(3 * 256).rearrange("p (a n) -> p a n", a=3)
        pT = SC.bf16(2 * 1024).rearrange("p (a n) -> p a n", a=2)
        kc = SC.bf16(2 * 1024).rearrange("p (a b n) -> p a b n", a=2, b=4)
        vc = SC.bf16(2 * 1024).rearrange("p (a b n) -> p a b n", a=2, b=4)
        kTs = SC.bf16(2048).rearrange("p (a b c n) -> p a b c n", a=2, b=4, c=2)
        pTc = SC.bf16(2 * 128).rearrange("p (a n) -> p a n", a=2)
        bi = 0
        ci = 0
        pendB = [None]
        for g in (0, 1, 2):
            d = DILS[g]
            nblk = (1, 4, 8)[g]
            for hh in range(2):
                pid = g * 2 + hh
                order = [jt for jt in range(16) if jt % (16 // d) != 0] + [jt for jt in range(16) if jt % (16 // d) == 0] + [16]
                if _os.environ.get("K_PB_TILES") is not None:
                    order = [int(v) for v in _os.environ["K_PB_TILES"].split(",") if v != ""]
                for jt in order:
                    b = bi % 2
                    b3 = bi % 3
                    bi += 1
                    rrow = (pid * NT + jt) * 128
                    S.dma("sp", qk_c[:, b3, :], recs[j][rrow:rrow + 128, 0:512], reads=[("rec", j, pid, jt)], writes=[skey(("b_qk", b3))])
                    S.dma("sp", v_c[:, b3, :], recs[j][rrow:rrow + 128, 512:768], reads=[("rec", j, pid, jt)], writes=[skey(("b_v", b3))])
                    blocks = [(qk_c[:, b3, 256:512], v_c[:, b3, :], [("b_qk", b3), ("b_v", b3)])]
                    if jt < 16:
                        d_, r, n, start = tile_geom(g, jt)
                        if n > 0:
                            prow = (pid * NT + jt - 1) * 128
                            S.dma("sp", k_p[:, b3, :], recs[j][prow:prow + 128, 256:512], reads=[("rec", j, pid, jt - 1)], writes=[skey(("b_kp", b3))])
                            S.dma("sp", v_p[:, b3, :], recs[j][prow:prow + 128, 512:768], reads=[("rec", j, pid, jt - 1)], writes=[skey(("b_vp", b3))])
                        else:
                            erow = r * 128
                            S.dma("sp", k_p[:, b3, :].bitcast(F32), ex_out[j][g][hh][erow:erow + 128, 0:128], reads=[("ex_out", j, g, hh)], writes=[skey(("b_kp", b3))])
                            S.dma("sp", v_p[:, b3, :].bitcast(F32), ex_out[j][g][hh][erow:erow + 128, 128:256], reads=[("ex_out", j, g, hh)], writes=[skey(("b_vp", b3))])
                        blocks.append((k_p[:, b3, :], v_p[:, b3, :], [("b_kp", b3), ("b_vp", b3)]))
                        mrow = mrow_0 if n == 0 else mrow_n
                    nb = len(blocks)
                    bkeys = [k for blk in blocks for k in blk[2]]

                    samp = (jt == 16)
                    if samp:
                        pidx = lambda h4, cp: (h4 % 2) * 512 + (h4 // 2) * 128
                    else:
                        pidx = lambda h4, cp: (h4 % 2) * 512 + (h4 // 2) * 256 + cp * 128

                    sbk = 0 if b == 0 else 6

                    def f_sc(e, b3=b3, blocks=blocks, nb=nb, g=g, samp=samp, pidx=pidx, sbk=sbk, mrow=(mrow if jt < 16 else None)):
                        for bank in range(2):
                            if samp:
                                e.matmul(ps[:, sbk + bank, 0:256], lhsT=ident_bf[:], rhs=smask[g], start=True, stop=False)
                            else:
                                e.matmul(ps[:, sbk + bank, :], lhsT=ident_bf[:], rhs=mrow, start=True, stop=False)
                        for h4 in (0, 2, 1, 3):
                            pb = (h4 % 2) * 64
                            for cp in range(nb):
                                c0 = pidx(h4, cp) % 512
                                kT = blocks[cp][0]
                                ins = e.matmul(ps[:, sbk + h4 % 2, c0:c0 + 128], lhsT=kT[pb:pb + 64, (h4 // 2) * 128:(h4 // 2) * 128 + 128], rhs=qk_c[pb:pb + 64, b3, (h4 // 2) * 128:(h4 // 2) * 128 + 128], start=False, stop=True)
                        return ins
                    S.op("pe", f_sc, reads=bkeys + ["ca", "ident_bf"], writes=[("ps", sbk), ("ps", sbk + 1)])
                    if not samp:
                        S.op("act", lambda e, b=b, sbk=sbk: e.activation(out=pT[:, b, :], in_=ps[:, sbk:sbk + 2, :].rearrange("p a n -> p (a n)"), func=AF.Exp, scale=0.125),
                             reads=[("ps", sbk), ("ps", sbk + 1)], writes=[("b_pT", b)])
                    else:
                        S.op("act", lambda e, b=b, sbk=sbk: e.activation(out=pT[:, b, :].rearrange("p (a n) -> p a n", a=2)[:, :, 0:256], in_=ps[:, sbk:sbk + 2, 0:256], func=AF.Exp, scale=0.125),
                             reads=[("ps", sbk), ("ps", sbk + 1)], writes=[("b_pT", b)])

                    def f_pv(e, b=b, blocks=blocks, nb=nb, hh=hh, pidx=pidx):
                        for hs in range(2):
                            for cc in range(2):
                                h4 = 2 * cc + hs
                                o = ps[:, 2, (hs * 2 + cc) * 128:(hs * 2 + cc + 1) * 128]
                                for cp in range(nb):
                                    V = blocks[cp][1]
                                    ins = e.matmul(o, lhsT=V[:, cc * 128:(cc + 1) * 128], rhs=pT[:, b, pidx(h4, cp):pidx(h4, cp) + 128], start=(cp == 0), stop=(cp == nb - 1))
                        first = True
                        for h4 in range(4):
                            for cp in range(nb):
                                ins = e.matmul(ps[0:64, 3, 0:128], lhsT=oh[:, hh * 4 + h4, :], rhs=pT[:, b, pidx(h4, cp):pidx(h4, cp) + 128], start=first, stop=(h4 == 3 and cp == nb - 1))
                                first = False
                        return ins
                    pv_reads = [("b_pT", b)] + bkeys + ["ca"]
                    if jt < 16:
                        cols = slice(start, start + d * 127 + 1, d)
                    else:
                        cols = slice(NP, T)
                    oa0 = O_acc[0:64, hh * 2:hh * 2 + 2, cols]
                    oa1 = O_acc[64:128, hh * 2:hh * 2 + 2, cols]

                    def accum(oa0=oa0, oa1=oa1, cols=cols):
                        S.op("dve", lambda e, oa0=oa0: e.tensor_tensor(out=oa0, in0=ps[0:64, 2, 0:256].rearrange("p (c n) -> p c n", c=2), in1=oa0, op=ALU.add),
                             reads=[("ps", 2), "Oacc"], writes=["Oacc"])
                        S.op("dve", lambda e, oa1=oa1: e.tensor_tensor(out=oa1, in0=ps[64:128, 2, 256:512].rearrange("p (c n) -> p c n", c=2), in1=oa1, op=ALU.add),
                             reads=[("ps", 2), "Oacc"], writes=["Oacc"])
                        S.op("dve", lambda e, cols=cols: e.tensor_tensor(out=den_acc[:, cols], in0=ps[0:64, 3, 0:128], in1=den_acc[:, cols], op=ALU.add),
                             reads=[("ps", 3), "den"], writes=["den"])
                    def fin(f_pv=f_pv, pv_reads=pv_reads, accum=accum):
                        S.op("pe", f_pv, reads=pv_reads, writes=[("ps", 2), ("ps", 3)])
                        accum()
                    if pendB[0] is not None:
                        pendB[0]()
                        pendB[0] = None
                    if samp:
                        fin()
                    else:
                        pendB[0] = fin
                    if samp and not _os.environ.get("K_NOCACHE"):
                        def f_z(e):
                            e.matmul(ps[:, 2, :], lhsT=zlhs, rhs=mrow_n, start=True, stop=False)
                            return e.matmul(ps[0:64, 3, 0:128], lhsT=zlhs[:, 0:64], rhs=ident_bf[:], start=True, stop=False)
                        S.op("pe", f_z, reads=["ca", "ident_bf"], writes=[("ps", 2), ("ps", 3)])
                    if jt == 16 and not _os.environ.get("K_NOCACHE"):
                        stp = [(0, nblk)] if nblk <= 4 else [(0, 4), (4, 8)]
                        steps = [(sb_, b0, b1) for sb_ in range(16) for (b0, b1) in stp]
                        NS = len(steps)

                        def cw_view(i):
                            sb_, b0, b1 = steps[i]
                            return cw_d[g][j, sb_].rearrange("(p r) kv h f -> p r kv h f", r=d), b0, b1

                        def loadK(i):
                            if i >= NS:
                                return
                            cwv, b0, b1 = cw_view(i)
                            S.dma("pool", kc[:, i % 2, 0:b1 - b0, :].rearrange("p r (h f) -> p r h f", h=4), cwv[:, b0:b1, 0, hh * 4:hh * 4 + 4, :], writes=[skey(("c_kc", i % 2))])

                        def loadV(i):
                            if i >= NS:
                                return
                            cwv, b0, b1 = cw_view(i)
                            S.dma("pool", vc[:, i % 2, 0:b1 - b0, :].rearrange("p r (h f) -> p r h f", h=4), cwv[:, b0:b1, 1, hh * 4:hh * 4 + 4, :], writes=[skey(("c_vc", i % 2))])

                        def stA(i):
                            if i >= NS:
                                return
                            sb_, b0, b1 = steps[i]
                            nbk = b1 - b0
                            c2 = i % 2

                            def f_ct(e):
                                for bl in range(nbk):
                                    for cc in range(2):
                                        ins = e.transpose(out=psb(4)[:, (bl * 2 + cc) * 128:(bl * 2 + cc + 1) * 128], in_=kc[:, c2, bl, cc * 128:(cc + 1) * 128], identity=ident_bf[:])
                                return ins
                            S.op("pe", f_ct, reads=[("c_kc", c2), "ident_bf"], writes=[("ps", 4)])
                            S.op("act", lambda e: e.copy(out=kTs[:, c2, 0:nbk, :, :], in_=psb(4)[:, 0:nbk * 256].rearrange("p (b c n) -> p b c n", b=nbk, c=2)), reads=[("ps", 4)], writes=[("c_kT", c2)])

                        def stB(i):
                            sb_, b0, b1 = steps[i]
                            nbk = b1 - b0
                            c2 = i % 2

                            def f_cs(e, g=g, b3=b3):
                                cm = cmask[g][:, b0 * 32:(b0 + nbk) * 32].rearrange("p (r h t) -> p r h t", r=nbk, h=4)[:, :, 0:2, :]
                                for par in range(2):
                                    e.matmul(ps[:, 5 + par, 0:nbk * 16].rearrange("p (r h t) -> p r h t", r=nbk, h=2), lhsT=ident_bf[:], rhs=cm, start=True, stop=False)
                                for par in range(2):
                                    pb = par * 64
                                    for bl in range(nbk):
                                        for hp in range(2):
                                            ins = e.matmul(ps[:, 5 + par, (bl * 2 + hp) * 8:(bl * 2 + hp + 1) * 8], lhsT=kTs[pb:pb + 64, c2, bl, hp, :],
                                                           rhs=qk_c[pb:pb + 64, b3, hp * 128 + sb_ * 8:hp * 128 + sb_ * 8 + 8], start=False, stop=True)
                                return ins
                            S.op("pe", f_cs, reads=[("c_kT", c2), ("b_qk", b3), "ca", "ident_bf"], writes=[("ps", 5), ("ps", 6)])
                            for par in range(2):
                                S.op("act", lambda e, par=par: e.activation(out=pTc[:, c2, par * 64:par * 64 + nbk * 16], in_=ps[:, 5 + par, 0:nbk * 16], func=AF.Exp, scale=0.125),
                                     reads=[("ps", 5 + par)], writes=[("c_pT", c2, par)])

                        def stC(i):
                            if i < 0:
                                return
                            sb_, b0, b1 = steps[i]
                            nbk = b1 - b0
                            c2 = i % 2

                            def f_cpv(e, hh=hh):
                                for hs in range(2):
                                    for cc in range(2):
                                        for bl in range(nbk):
                                            ins = e.matmul(ps[:, 2, (hs * 2 + cc) * 128 + sb_ * 8:(hs * 2 + cc) * 128 + sb_ * 8 + 8], lhsT=vc[:, c2, bl, cc * 128:(cc + 1) * 128],
                                                           rhs=pTc[:, c2, hs * 64 + (bl * 2 + cc) * 8:hs * 64 + (bl * 2 + cc + 1) * 8], start=False, stop=True)
                                for bl in range(nbk):
                                    for h4 in range(4):
                                        ins = e.matmul(ps[0:64, 3, sb_ * 8:sb_ * 8 + 8], lhsT=oh[:, hh * 4 + h4, :], rhs=pTc[:, c2, (h4 % 2) * 64 + (bl * 2 + h4 // 2) * 8:(h4 % 2) * 64 + (bl * 2 + h4 // 2 + 1) * 8], start=False, stop=True)
                                return ins
                            S.op("pe", f_cpv, reads=[("c_pT", c2, 0), ("c_pT", c2, 1), ("c_vc", c2), "ca"], writes=[("ps", 2), ("ps", 3)])

                        loadK(0)
                        loadK(1)
                        loadV(0)
                        stA(0)
                        for i in range(NS):
                            loadK(i + 2)
                            stA(i + 1)
                            stB(i)
                            stC(i - 1)
                            loadV(i + 1)
                        stC(NS - 1)
                        accum()

        if AB_STOP <= 3:
            return bail()
        phase_barrier()
        apT = hT
        Ab = SC.f32(2 * 527).rearrange("p (a n) -> p a n", a=2)
        ptl = SC.f32(60).rearrange("p (g n) -> p g n", g=4)
        dTb = SC.bf16(2 * 512).rearrange("p (a n) -> p a n", a=2)
        uoff = SC.off
        As = SC.f32(2 * 368).rearrange("p (a b n) -> p a b n", a=2, b=16)
        SCR_S0 = SC.f32(368).rearrange("p (b n) -> p b n", b=16)
        SCR_S1 = SC.f32(368).rearrange("p (b n) -> p b n", b=16)
        utok = scr[:, uoff:uoff + 1024].rearrange("p (a n) -> p a n", a=2)
        UTK = [("utok", 0), ("utok", 1)]
        sprevT = SC.f32(4 * 240).rearrange("p (g b n) -> p g b n", g=4, b=16)
        rowb = SC.f32(512)
        SCR_T0 = SC.f32(527)
        SCR_T1 = SC.f32(527)
        rden = SCR_T0[:, 0:512]
        rc_tmp = SC.f32(16)
        PS0 = 3 * DEPTH + j
        s = next_chunk()
        wpool = lambda k, a, b_: wring[:, s, k * 512 + a:k * 512 + b_]
        pw = wring[:, s, 4096:4608].rearrange("p (g d) -> p g d", g=4)
        for ti, t in enumerate((15, 16)):
            def f_mm(e, t=t, ti=ti):
                for k in range(8):
                    ins = e.matmul(ps[:, 6 + ti, :], lhsT=hT[:, k, t * 128:(t + 1) * 128], rhs=wpool(k, 0, 512), start=(k == 0), stop=(k == 7))
                return ins
            S.op("pe", f_mm, reads=[("w", s), ("hT", t)], writes=[("ps", 6 + ti)])
            S.op("act", lambda e, ti=ti: e.copy(out=utok[:, ti, :], in_=ps[:, 6 + ti, :]), reads=[("ps", 6 + ti)], writes=[skey(("utok", ti))])
        S.dma("sp", poolp_d[j], utok[113:128, 0, :], reads=[("utok", 0)], writes=[("poolp", j)])
        S.dma("sp", pools_d[j, :, 7:15, :], utok[:, 1, :], reads=[("utok", 1)], writes=[("pools_n", j)])
        S.dma("sp", pools_d[j, :, 0:7, :], spool_d[j, :, 8:15, :], writes=[("pools_o", j)])
        out_keys.extend([("poolp", j), ("pools_n", j), ("pools_o", j)])
        for half in range(2):
            S.dma("sp", rowb[0:120, :], spool_d[j, half * 8:(half + 1) * 8].rearrange("b n c -> (b n) c"), writes=[skey("rowb")])

            def f_tr(e):
                for gi in range(4):
                    ins = e.transpose(out=ps[:, 4, gi * 120:(gi + 1) * 120], in_=rowb[0:120, gi * 128:(gi + 1) * 128], identity=ident_f[0:120, 0:120])
                return ins
            S.op("pe", f_tr, reads=["rowb", "ident_f"], writes=[("ps", 4)])
            S.op("act", lambda e, half=half: e.copy(out=sprevT[:, :, half * 8:(half + 1) * 8, :], in_=ps[:, 4, 0:480].rearrange("p (g b n) -> p g b n", g=4, b=8)), reads=[("ps", 4)], writes=[("p_sprev", half)])
        ai = 0
        for tg in range(5):
            t0, tn = TG[tg]
            hkeys = [("hT", t) for t in range(t0 // 128, (t0 + tn) // 128)]
            for gi in range(4):
                def f_mm(e, gi=gi, t0=t0, tn=tn):
                    for k in range(8):
                        ins = e.matmul(ps[:, gi, 0:tn], lhsT=wpool(k, gi * 128, (gi + 1) * 128), rhs=hT[:, k, t0:t0 + tn], start=(k == 0), stop=(k == 7))
                    return ins
                S.op("pe", f_mm, reads=[("w", s)] + hkeys, writes=[("ps", gi)])
            for gi in range(4):
                w = 2 << gi
                a = ai % 2
                ai += 1
                if tg < 4:
                    A0, A1 = Ab[:, a, :], Ab[:, 1 - a, :]
                    S.op("act", lambda e, A0=A0, gi=gi: e.copy(out=A0[:, 15:527], in_=ps[:, gi, :]), reads=[("ps", gi)], writes=[("p_A", a)])
                    src = phalo[:, gi, :] if tg == 0 else ptl[:, gi, :]
                    S.op("pool", lambda e, A0=A0, src=src: e.tensor_copy(out=A0[:, 0:15], in_=src), reads=["phalo", ("p_tl", gi)], writes=[("p_Ah", a)])
                    S.op("pool", lambda e, A0=A0, gi=gi: e.tensor_copy(out=ptl[:, gi, :], in_=A0[:, 512:527]), reads=[("p_A", a), ("p_Ah", a)], writes=[("p_tl", gi)])
                    cur = A0
                    srcs = [("p_A", a), ("p_Ah", a)]
                    stepk = 1
                    tbuf = [SCR_T0, SCR_T1]
                    ti_ = 0
                    while stepk < w:
                        nxt = tbuf[ti_ % 2]
                        ti_ += 1
                        S.op("dve", lambda e, cur=cur, nxt=nxt, stepk=stepk: e.tensor_tensor(out=nxt[:, stepk:527], in0=cur[:, stepk:527], in1=cur[:, 0:527 - stepk], op=ALU.add),
                             reads=srcs, writes=[("p_T", ti_ % 2)])
                        srcs = [("p_T", ti_ % 2)]
                        cur = nxt
                        stepk *= 2
                    db = ai % 2
                    S.op("dve", lambda e, cur=cur, A0=A0, db=db, w=w: e.scalar_tensor_tensor(out=dTb[:, db, :], in0=cur[:, 15:527], scalar=1.0 / w, in1=A0[:, 15:527], op0=ALU.mult, op1=ALU.subtract),
                         reads=srcs + [("p_A", a)], writes=[("p_d", db)])
                    if tg == 0:
                        S.op("dve", lambda e, cur=cur, gi=gi: e.tensor_tensor(out=rc_tmp[:, :], in0=cur[:, 15:31], in1=cb[:, gi * 16:(gi + 1) * 16], op=ALU.mult), reads=srcs + ["cb"], writes=["p_rc"])
                        S.op("dve", lambda e, A0=A0, db=db: e.tensor_tensor(out=dTb[:, db, 0:16], in0=rc_tmp[:, :], in1=A0[:, 15:31], op=ALU.subtract), reads=["p_rc", ("p_A", a), ("p_d", db)], writes=[("p_d", db)])
                    rhs_d = dTb[:, db, :]
                else:
                    A0 = As[:, a, :, :]
                    S.op("act", lambda e, A0=A0, gi=gi: e.copy(out=A0[:, :, 15:23], in_=ps[:, gi, 0:128].rearrange("p (b n) -> p b n", b=16)), reads=[("ps", gi)], writes=[("p_As", a)] + UTK)
                    S.op("pool", lambda e, A0=A0, gi=gi: e.tensor_copy(out=A0[:, :, 0:15], in_=sprevT[:, gi, :, :]), reads=[("p_sprev", 0), ("p_sprev", 1)], writes=[("p_Ash", a)])
                    cur = A0
                    srcs = [("p_As", a), ("p_Ash", a)]
                    stepk = 1
                    tbuf = [SCR_S0, SCR_S1]
                    ti_ = 0
                    while stepk < w:
                        nxt = tbuf[ti_ % 2]
                        ti_ += 1
                        S.op("dve", lambda e, cur=cur, nxt=nxt, stepk=stepk: e.tensor_tensor(out=nxt[:, :, stepk:23], in0=cur[:, :, stepk:23], in1=cur[:, :, 0:23 - stepk], op=ALU.add),
                             reads=srcs, writes=[("p_TS", ti_ % 2)] + UTK)
                        srcs = [("p_TS", ti_ % 2)]
                        cur = nxt
                        stepk *= 2
                    db = ai % 2
                    S.op("dve", lambda e, cur=cur, A0=A0, db=db, w=w: e.scalar_tensor_tensor(out=dTb[:, db, 0:128].rearrange("p (b n) -> p b n", b=16), in0=cur[:, :, 15:23], scalar=1.0 / w, in1=A0[:, :, 15:23], op0=ALU.mult, op1=ALU.subtract),
                         reads=srcs + [("p_As", a)], writes=[("p_d", db)])
                    rhs_d = dTb[:, db, 0:128]
                ob = 6 + (ai % 2)
                S.op("pe", lambda e, gi=gi, rhs_d=rhs_d, ob=ob, tn=tn: e.matmul(ps[:, ob, 0:tn], lhsT=pw[:, gi, :], rhs=rhs_d, start=True, stop=True), reads=[("w", s), ("p_d", db)], writes=[("ps", ob)])
                S.op("act", lambda e, gi=gi, ob=ob, t0=t0, tn=tn: e.activation(out=apT[:, 4 + gi, t0:t0 + tn], in_=ps[:, ob, 0:tn], func=AF.Copy, scale=vT[:, gi, PS0:PS0 + 1]),
                     reads=[("ps", ob), "vT"] + [("ps", g_) for g_ in range(4)], writes=[("hT", t) for t in range(t0 // 128, (t0 + tn) // 128)])
        chunk_done()
        used[0] = 8
        if AB_STOP <= 4:
            return bail()
        ni = 0
        for tg in range(5):
            t0, tn = TG[tg]
            akeys = [("hT", t) for t in range(t0 // 128, (t0 + tn) // 128)]
            for c in range(4):
                bk = 4 + (ni % 2)
                ni += 1
                S.op("pe", lambda e, c=c, bk=bk, t0=t0, tn=tn: e.matmul(ps[:, bk, 0:tn], lhsT=Ef[:, c, :], rhs=den_acc[:, t0:t0 + tn], start=True, stop=True), reads=["den", "cb"], writes=[("ps", bk)])
                S.op("dve", lambda e, bk=bk, tn=tn: e.reciprocal(out=rden[:, 0:tn], in_=ps[:, bk, 0:tn]), reads=[("ps", bk)], writes=["p_rden", ("p_T", 0), ("p_T", 1)])
                S.op("dve", lambda e, c=c, t0=t0, tn=tn: e.tensor_tensor(out=apT[:, c, t0:t0 + tn], in0=O_acc[:, c, t0:t0 + tn], in1=rden[:, 0:tn], op=ALU.mult), reads=["p_rden", "Oacc"] + akeys, writes=akeys)
        out_proj(apT, "hT")
    for l in range(DEPTH):
        ffn(l, 0)
        if ENABLE_MIX:
            if l % 2 == 0 and ENABLE_AB:
                ab_mixer(l)
            if l % 2 == 1 and ENABLE_CONV:
                conv_mixer(l)
        ffn(l, 1)

    S.barrier()
    S.dma("sp", gfin, bass.AP(gfin_d.tensor, 0, [[0, 128], [1, D]]), writes=["gfin"])
    for t in range(NT):
        S.op("act", lambda e, t=t: e.activation(out=junk, in_=x_sb[:, t, :], func=AF.Square, accum_out=ss[:, t:t + 1]),
             reads=[("x", t)], writes=[("ss", t)])
    S.op("act", lambda e: e.activation(out=sd[:], in_=ss[:], func=AF.Sqrt, scale=1.0 / D, bias=eps_t[:]),
         reads=[("ss", t) for t in range(NT)] + ["eps"], writes=["sd"])
    S.op("dve", lambda e: e.reciprocal(out=rstd[:], in_=sd[:]), reads=["sd"], writes=["rstd"])
    for t in range(NT):
        b = t % 2
        S.op("dve", lambda e, t=t, b=b: e.scalar_tensor_tensor(out=ystage[:, b, :], in0=x_sb[:, t, :], scalar=rstd[:, t:t + 1], in1=gfin, op0=ALU.mult, op1=ALU.mult),
             reads=[("x", t), "rstd", "gfin"], writes=[("ys", b)])
        S.dma("sp", y_d[t * 128:(t + 1) * 128, :], ystage[:, b, :], reads=[("ys", b)], writes=[("y", t)])
        out_keys.append(("y", t))

    S.wait_keys("sp", out_keys)
    S.replay()
    return nc


_PROG_CACHE = {}


def _host_consts(half):
    c = np.zeros((128, 4096), np.float32)
    kk = np.arange(128)[:, None]
    qq = np.arange(128)[None, :]
    maskC = np.where(kk <= qq, 0.0, NEG).astype(np.float32)
    maskP = np.where(kk >= qq, 0.0, NEG).astype(np.float32)
    mp0 = maskP if half == 1 else np.full((128, 128), NEG, np.float32)
    c[:, 0:512] = np.concatenate([maskC, maskP, maskC, maskP], 1)
    c[:, 512:1024] = np.concatenate([maskC, mp0, maskC, mp0], 1)
    bk, tk = kk // 8, kk % 8
    bq, tq = qq // 8, qq % 8
    same = bk == bq
    conds = [tk <= tq, (tk <= tq) & ((tq - tk) % 4 == 0), tk == tq]
    for g in range(3):
        sm = np.where(same & conds[g], 0.0, NEG)
        c[:, 1024 + g * 256:1024 + (g + 1) * 256] = np.concatenate([sm, sm], 1)
    p = np.arange(128)[:, None, None, None]
    t = np.arange(8)[None, None, None, :]
    ones4 = np.ones((1, 1, 4, 1), bool)
    c[:, 1792:1824] = np.where((p >= t) & ones4, 0.0, NEG).reshape(128, 32)
    r4 = np.arange(4)[None, :, None, None]
    c[:, 1824:1952] = np.where(((t % 4) == r4) & (r4 + 4 * p >= t) & ones4, 0.0, NEG).reshape(128, 128)
    r8 = np.arange(8)[None, :, None, None]
    c[:, 1952:2208] = np.where((t == r8) & ones4 & (p >= 0), 0.0, NEG).reshape(128, 256)
    oh = np.zeros((128, 8, 64), np.float32)
    for hh in range(2):
        for h4 in range(4):
            oh[:, hh * 4 + h4, hh * 32 + h4] = 1.0
    c[:, 2208:2720] = oh.reshape(128, 512)
    o = 3072
    for gi in range(4):
        w = 2 << gi
        i = np.arange(16)
        c[:, o + gi * 16:o + (gi + 1) * 16] = (1.0 / np.minimum(w, i + 1)) if half == 0 else (1.0 / w)
    c[:, o + 64] = float(half)
    E = np.zeros((64, 4, 128), np.float32)
    for hh in range(2):
        for h4 in range(4):
            h = hh * 4 + h4
            E[hh * 32 + h4, h // 2, (h % 2) * 64:(h % 2) * 64 + 64] = 1.0
    c[0:64, o + 128:o + 640] = E.reshape(64, 512)
    return c


def _host_rope(half):
    inv = np.power(np.float32(10000.0), -np.arange(32, dtype=np.float32) / np.float32(32)).astype(np.float32)
    tab = np.zeros((3, NT, 128, 64), np.float32)
    p = np.arange(128)
    for g in range(3):
        d = DILS[g]
        cls = 16 // d
        for jt in range(NT):
            if jt < 16:
                r, n = jt // cls, jt % cls
                pos = half * NP + r + d * (128 * n + p)
            else:
                pos = 2048 + (p % 8)
            ang = pos.astype(np.float32)[:, None] * inv[None, :]
            tab[g, jt, :, 0:32] = np.cos(ang)
            tab[g, jt, :, 32:64] = np.sin(ang)
    return tab


def kernel(**inputs):
    x_prompt = np.asarray(inputs["x_prompt"], np.float32)
    x_sample = np.asarray(inputs["x_sample"], np.float32)
    B = x_prompt.shape[0]
    DEPTH = inputs["ffn1_norm"].shape[0]
    N_AB = (DEPTH + 1) // 2
    N_C = DEPTH // 2
    n_cores = 2 * B
    DB = x_sample.shape[0]
    assert DB == 16 * n_cores and x_prompt.shape[1] == 2 * NP
    key = (DEPTH, n_cores)
    if key not in _PROG_CACHE:
        _PROG_CACHE[key] = build_program(DEPTH, n_cores)
    nc = _PROG_CACHE[key]

    f32 = lambda a: np.ascontiguousarray(np.asarray(a, np.float32))
    vec_rows = [inputs["ffn1_norm"][l] for l in range(DEPTH)] + [inputs["mix_norm"][l] for l in range(DEPTH)] + \
               [inputs["ffn2_norm"][l] for l in range(DEPTH)]
    for j in range(N_AB):
        vec_rows.append(np.concatenate([np.asarray(inputs["pool_scale"][j]), np.asarray(inputs["pool_scale"][j])]))
    for j in range(N_C):
        for t in range(3):
            vec_rows.append(inputs["conv_w"][j][t])
    vecs = f32(np.stack([np.asarray(v, np.float32) for v in vec_rows]))
    shared = {
        "ffn1_w_gu": f32(inputs["ffn1_w_gu"]), "ffn2_w_gu": f32(inputs["ffn2_w_gu"]),
        "ffn1_w_down": f32(inputs["ffn1_w_down"]), "ffn2_w_down": f32(inputs["ffn2_w_down"]),
        "ab_w_in": f32(inputs["ab_w_in"]), "ab_w_out": f32(inputs["ab_w_out"]), "pool_w": f32(inputs["pool_w"]),
        "conv_w_in": f32(inputs["conv_w_in"]) if N_C else np.zeros((1, D, 3 * D), np.float32),
        "conv_w_out": f32(inputs["conv_w_out"]) if N_C else np.zeros((1, D, D), np.float32),
        "vecs": vecs, "gfin": f32(np.asarray(inputs["final_norm"]).reshape(1, D)),
        "ident": np.eye(128, dtype=np.float32),
    }
    in_maps = []
    for c in range(n_cores):
        b, half = c // 2, c % 2
        m = dict(shared)
        m["x"] = f32(np.concatenate([x_prompt[b, half * NP:(half + 1) * NP], x_sample[c * 16:(c + 1) * 16].reshape(128, D)], 0))
        m["cw0"] = f32(inputs["cache_win0"][:, c * 16:(c + 1) * 16])
        m["cw1"] = f32(inputs["cache_win1"][:, c * 16:(c + 1) * 16])
        m["cw2"] = f32(inputs["cache_win2"][:, c * 16:(c + 1) * 16])
        m["spool"] = f32(inputs["state_pool"][:, c * 16:(c + 1) * 16])
        m["sconv"] = f32(inputs["state_conv"][:, c * 16:(c + 1) * 16]) if N_C else np.zeros((1, 16, 2, D), np.float32)
        m["rope"] = _host_rope(half)
        m["consts"] = _host_consts(half)
        in_maps.append(m)
    res = run_bass_kernel_spmd(nc, in_maps, core_ids=list(range(n_cores)))
    R = res.results
    S_ = x_prompt.shape[1]
    y_prompt = np.zeros((B, S_, D), np.float32)
    y_sample = np.zeros((DB, 8, D), np.float32)
    for c in range(n_cores):
        b, half = c // 2, c % 2
        y_prompt[b, half * NP:(half + 1) * NP] = R[c]["y"][:NP]
        y_sample[c * 16:(c + 1) * 16] = R[c]["y"][NP:].reshape(16, 8, D)
    outs = [y_prompt, y_sample]
    for g in range(3):
        outs.append(np.stack([R[2 * b + 1]["kvrow%d" % g] for b in range(B)], 1))
    outs.append(np.stack([R[2 * b + 1]["poolp"] for b in range(B)], 1))
    outs.append(np.stack([R[2 * b + 1]["convp"][:N_C] for b in range(B)], 1))
    for g in range(3):
        outs.append(np.concatenate([R[c]["kvs%d" % g].reshape(N_AB, 16, 8, 2, 8, 64) for c in range(n_cores)], 1))
    outs.append(np.concatenate([R[c]["pools"] for c in range(n_cores)], 1))
    outs.append(np.concatenate([R[c]["convs"][:N_C] for c in range(n_cores)], 1))
    return tuple(outs)
```

```python
import numpy as np
import concourse.bass as bass
import concourse.mybir as mybir
from concourse.bass_utils import run_bass_kernel_spmd

F32 = mybir.dt.float32
BF16 = mybir.dt.bfloat16
AF = mybir.ActivationFunctionType
ALU = mybir.AluOpType

D = 1024
DFF = 2816
NPT = 16
NT = 17
T = NT * 128
NP = NPT * 128
TG = [(0, 512), (512, 512), (1024, 512), (1536, 512), (2048, 128)]
DILS = (1, 4, 16)
WINS = (128, 512, 2048)
EPS = 1e-6
NEG = -30000.0
SAME_ENGINE_SYNC = True
ENABLE_MIX = True
ENABLE_AB = True
ENABLE_CONV = True
import os as _os
if _os.environ.get("K_NOAB"):
    ENABLE_AB = False
if _os.environ.get("K_NOCONV"):
    ENABLE_CONV = False


class Sched:
    ENGS = ("pe", "act", "dve", "pool", "sp")

    def __init__(self, nc, n_dsem=20):
        self.nc = nc
        self.sem = {e: nc.alloc_semaphore("s_" + e) for e in self.ENGS}
        self.cnt = {e: 0 for e in self.ENGS}
        self.prog = {e: [] for e in self.ENGS}
        self.seen = {e: {} for e in self.ENGS}
        self.writer = {}
        self.readers = {}
        self.dsem = {q: [nc.alloc_semaphore("d_%s%d" % (q, i)) for i in range(n_dsem)] for q in ("sp", "pool")}
        self.dcnt = {q: [0] * n_dsem for q in ("sp", "pool")}
        self.dnext = {"sp": 0, "pool": 0}
        self.csem = []
        self.fence = []

    def _semh(self, sk):
        if isinstance(sk, str):
            return self.sem[sk]
        if sk[0] == "cc":
            return self.csem[sk[1]]
        return self.dsem[sk[0]][sk[1]]

    def _deps(self, eng, reads, writes):
        deps = {}

        def add(t):
            if t is not None and deps.get(t[0], 0) < t[1]:
                deps[t[0]] = t[1]
        for k in reads:
            add(self.writer.get(k))
        for k in writes:
            add(self.writer.get(k))
            for r in self.readers.get(k, ()):
                add(r)
        waits = []
        for sk, v in deps.items():
            if sk == eng and (eng == "pe" or not SAME_ENGINE_SYNC):
                continue
            if self.seen[eng].get(sk, 0) >= v:
                continue
            self.seen[eng][sk] = v
            waits.append((sk, v))
        return waits

    def _commit(self, tok, reads, writes):
        for k in writes:
            self.writer[k] = tok
            self.readers[k] = []
        for k in reads:
            self.readers.setdefault(k, []).append(tok)

    def op(self, eng, fn, reads=(), writes=()):
        waits = self._deps(eng, reads, writes)
        self.cnt[eng] += 1
        tok = (eng, self.cnt[eng])
        self.prog[eng].append((waits, fn, (eng, 1)))
        self._commit(tok, reads, writes)
        return tok

    def dma(self, q, out, in_, reads=(), writes=(), **kw):
        waits = self._deps(q, list(reads) + self.fence, writes)
        i = self.dnext[q]
        self.dnext[q] = (i + 1) % len(self.dsem[q])
        sk = (q, i)
        prev = self.dcnt[q][i]
        if prev > 0 and self.seen[q].get(sk, 0) < prev:
            self.seen[q][sk] = prev
            waits.append((sk, prev))
        self.dcnt[q][i] = prev + 16
        tok = (sk, prev + 16)
        self.prog[q].append((waits, lambda e: e.dma_start(out=out, in_=in_, **kw), (sk, 16)))
        self._commit(tok, reads, writes)
        return tok

    def collective(self, ins, outs, groups, reads=(), writes=()):
        waits = self._deps("pool", reads, writes)
        self.csem.append(self.nc.alloc_semaphore("cc%d" % len(self.csem)))
        sk = ("cc", len(self.csem) - 1)
        tok = (sk, 1)
        self.prog["pool"].append((waits, lambda e: e.collective_compute(
            "AllGather", ALU.bypass, replica_groups=groups, ins=[ins], outs=[outs]), (sk, 1)))
        self._commit(tok, reads, writes)
        return tok

    def barrier(self):
        engs = ("pe", "act", "dve")
        keys = [("bar", e) for e in ("pe", "act", "dve", "pool")]
        for e in ("pe", "act", "dve", "pool"):
            self.writer[("bar", e)] = (e, self.cnt[e]) if self.cnt[e] > 0 else None
        for e in engs:
            waits = self._deps(e, keys, ())
            self.prog[e].append((waits, None, None))
        self.fence = [("bar", e) for e in ("pe", "act", "dve")]

    def wait_keys(self, eng, keys):
        waits = self._deps(eng, keys, ())
        self.prog[eng].append((waits, None, None))

    def replay(self):
        nc = self.nc
        with nc.Block() as block:
            def mk(name):
                def run(e):
                    for waits, fn, inc in self.prog[name]:
                        for sk, v in waits:
                            e.wait_ge(self._semh(sk), v)
                        if fn is not None:
                            ins = fn(e)
                            ins.then_inc(self._semh(inc[0]), inc[1])
                return run
            block.tensor(mk("pe"))
            block.scalar(mk("act"))
            block.vector(mk("dve"))
            block.gpsimd(mk("pool"))
            block.sync(mk("sp"))


def build_program(DEPTH, n_cores):
    N_AB = (DEPTH + 1) // 2
    N_C = DEPTH // 2
    nc = bass.Bass("TRN2", target_bir_lowering=False)
    S = Sched(nc)
    groups = [[2 * i, 2 * i + 1] for i in range(n_cores // 2)]

    def din(name, shape, dt=F32):
        return nc.dram_tensor(name, list(shape), dt, kind="ExternalInput").ap()

    def dout(name, shape):
        return nc.dram_tensor(name, list(shape), F32, kind="ExternalOutput").ap()

    def dint(name, shape, dt):
        return nc.dram_tensor(name, list(shape), dt).ap()

    def sb(name, shape, dt):
        return nc.alloc_sbuf_tensor(name, list(shape), dt)

    x_d = din("x", [T, D])
    cw_d = [din("cw%d" % g, [N_AB, 16, WINS[g], 2, 8, 64]) for g in range(3)]
    spool_d = din("spool", [N_AB, 16, 15, 512])
    sconv_d = din("sconv", [max(N_C, 1), 16, 2, D])
    wgu_d = [din("ffn1_w_gu", [DEPTH, D, 2 * DFF]), din("ffn2_w_gu", [DEPTH, D, 2 * DFF])]
    wdn_d = [din("ffn1_w_down", [DEPTH, DFF, D]), din("ffn2_w_down", [DEPTH, DFF, D])]
    abin_d = din("ab_w_in", [N_AB, D, 5120])
    about_d = din("ab_w_out", [N_AB, D, D])
    poolw_d = din("pool_w", [N_AB, 4, 128, 128])
    cvin_d = din("conv_w_in", [max(N_C, 1), D, 3 * D])
    cvout_d = din("conv_w_out", [max(N_C, 1), D, D])
    NV = 3 * DEPTH + N_AB + 3 * N_C
    vecs_d = din("vecs", [NV, D])
    gfin_d = din("gfin", [1, D])
    ident_d = din("ident", [128, 128])
    rope_d = din("rope", [3, NT, 128, 64])
    consts_d = din("consts", [128, 4096])

    y_d = dout("y", [T, D])
    kvrow_d = [dout("kvrow%d" % g, [N_AB, min(WINS[g], NP), 2, 8, 64]) for g in range(3)]
    kvs_d = [dout("kvs%d" % g, [N_AB, 128, 2, 8, 64]) for g in range(3)]
    poolp_d = dout("poolp", [N_AB, 15, 512])
    pools_d = dout("pools", [N_AB, 16, 15, 512])
    convp_d = dout("convp", [max(N_C, 1), 2, D])
    convs_d = dout("convs", [max(N_C, 1), 16, 2, D])
    out_keys = []

    x_sb = sb("x_sb", [128, NT, D], F32)
    hT = sb("hT", [128, 8, T], BF16)
    NSLOT = 2
    wring = sb("wring", [128, NSLOT, 6144], BF16)
    big = sb("big", [128, 8704], F32)
    ident_bf = sb("ident_bf", [128, 128], BF16)
    ident_f = sb("ident_f", [128, 128], F32)
    vT = sb("vT", [128, 8, NV], F32)
    vrows = big[0:32, 0:D]
    ss = sb("ss", [128, NT], F32)
    sd = sb("sd", [128, NT], F32)
    rstd = sb("rstd", [128, NT], F32)
    eps_t = sb("eps_t", [128, 1], F32)
    xn = sb("xn", [128, 2, D], BF16)
    junk = xn[:, 0, :]
    gfin = big[:, 0:D]
    ystage = big[:, D:3 * D].rearrange("p (b n) -> p b n", b=2)
    scr = sb("scr", [128, 6144], F32)
    den_acc = sb("den_acc", [64, T], F32)
    CA = 2848
    ca = sb("ca", [128, CA], BF16)
    cb = sb("cb", [128, 640], F32)
    ps = nc.alloc_psum_tensor("ps", [128, 8, 512], F32)

    def psb(bank):
        return ps[:, bank, :].bitcast(BF16)

    chunks = []

    def slotv(s, a, b):
        return wring[:, s, a:b]

    def kview(ap2d):
        return ap2d.rearrange("(k p) n -> p k n", p=128)

    def add_ffn_chunks(l, f):
        for c in range(11):
            chunks.append([
                (lambda s: slotv(s, 0, 2048).rearrange("p (k n) -> p k n", k=8), kview(wgu_d[f][l])[:, :, c * 256:(c + 1) * 256]),
                (lambda s: slotv(s, 2048, 4096).rearrange("p (k n) -> p k n", k=8), kview(wgu_d[f][l])[:, :, DFF + c * 256:DFF + (c + 1) * 256]),
                (lambda s: slotv(s, 4096, 6144).rearrange("p (k n) -> p k n", k=2), kview(wdn_d[f][l][c * 256:(c + 1) * 256, :])),
            ])

    def k8(a, b, n):
        return lambda s: slotv(s, a, b).rearrange("p (k n) -> p k n", k=8)

    def add_pool_chunk(j):
        chunks.append([
            (k8(0, 4096, 512), kview(abin_d[j])[:, :, 4608:5120]),
            (lambda s: slotv(s, 4096, 4608).rearrange("p (g d) -> p g d", g=4), poolw_d[j].rearrange("g c d -> c g d")),
        ])

    def add_out_chunks(wd):
        for hf in range(2):
            chunks.append([(k8(0, 4096, 512), kview(wd)[:, :, hf * 512:(hf + 1) * 512])])

    def add_ab_chunks(j):
        add_pool_chunk(j)
        for g in (2, 1, 0):
            for hh in range(2):
                chunks.append([(k8(part * 2048, (part + 1) * 2048, 256),
                                kview(abin_d[j])[:, :, g * 1536 + part * 512 + hh * 256:g * 1536 + part * 512 + hh * 256 + 256]) for part in range(3)])
        add_pool_chunk(j)
        add_out_chunks(about_d[j])

    def add_conv_chunks(j):
        for rep in range(2):
            for sc in range(4):
                chunks.append([(k8(part * 2048, (part + 1) * 2048, 256),
                                kview(cvin_d[j])[:, :, part * 1024 + sc * 256:part * 1024 + sc * 256 + 256]) for part in range(3)])
        add_out_chunks(cvout_d[j])

    for l in range(DEPTH):
        add_ffn_chunks(l, 0)
        if ENABLE_MIX:
            if l % 2 == 0 and ENABLE_AB:
                add_ab_chunks(l // 2)
            if l % 2 == 1 and ENABLE_CONV:
                add_conv_chunks(l // 2)
        add_ffn_chunks(l, 1)

    wstate = {"next_load": 0, "next_use": 0}

    def issue_loads(upto):
        while wstate["next_load"] <= min(upto, len(chunks) - 1):
            i = wstate["next_load"]
            s = i % NSLOT
            for dst_fn, src in chunks[i]:
                S.dma("pool", dst_fn(s), src, writes=[("w", s)])
            wstate["next_load"] += 1

    def next_chunk():
        i = wstate["next_use"]
        wstate["next_use"] += 1
        assert i < wstate["next_load"], "chunk not loaded"
        return i % NSLOT

    def chunk_done():
        issue_loads(wstate["next_load"])

    S.op("dve", lambda e: e.memset(eps_t[:], EPS), writes=["eps"])
    for t in range(NT):
        S.dma("sp", x_sb[:, t, :], x_d[t * 128:(t + 1) * 128, :], writes=[("x", t)])
    S.dma("sp", ident_f[:], ident_d, writes=["ident_f"])
    S.dma("pool", ident_bf[:], ident_d, writes=["ident_bf"])
    S.dma("sp", vrows[0:NV, :], vecs_d, writes=["vrows"])
    S.dma("pool", ca[:], consts_d[:, 0:CA], writes=["ca"])
    S.dma("sp", cb[:], consts_d[:, 3072:3072 + 640], writes=["cb"])
    issue_loads(NSLOT - 1)
    def f_vt(e):
        for k in range(8):
            ins = e.transpose(out=ps[:, 0, k * NV:(k + 1) * NV], in_=vrows[0:NV, k * 128:(k + 1) * 128], identity=ident_f[0:NV, 0:NV])
        return ins
    S.op("pe", f_vt, reads=["vrows", "ident_f"], writes=[("ps", 0)])
    S.op("act", lambda e: e.copy(out=vT[:], in_=ps[:, 0, 0:8 * NV].rearrange("p (k v) -> p k v", k=8)), reads=[("ps", 0)], writes=["vT"])

    ALL_HT = [("hT", t) for t in range(NT)]
    nstate = {"i": 0}

    def norm_phase(vidx):
        for t in range(NT):
            S.op("act", lambda e, t=t: e.activation(out=junk, in_=x_sb[:, t, :], func=AF.Square, accum_out=ss[:, t:t + 1]),
                 reads=[("x", t)], writes=[("ss", t)])
        S.op("act", lambda e: e.activation(out=sd[:], in_=ss[:], func=AF.Sqrt, scale=1.0 / D, bias=eps_t[:]),
             reads=[("ss", t) for t in range(NT)] + ["eps"], writes=["sd"])
        S.op("dve", lambda e: e.reciprocal(out=rstd[:], in_=sd[:]), reads=["sd"], writes=["rstd"])
        for t in range(NT):
            i = nstate["i"]
            nstate["i"] += 1
            b = i % 2
            S.op("dve", lambda e, t=t, b=b: e.tensor_scalar(out=xn[:, b, :], in0=x_sb[:, t, :], scalar1=rstd[:, t:t + 1], scalar2=None, op0=ALU.mult),
                 reads=[("x", t), "rstd"], writes=[("xn", b)])

            def f_tr(e, b=b):
                for k in range(8):
                    ins = e.transpose(out=psb(b)[:, k * 128:(k + 1) * 128], in_=xn[:, b, k * 128:(k + 1) * 128], identity=ident_bf[:])
                return ins
            S.op("pe", f_tr, reads=[("xn", b), "ident_bf"], writes=[("ps", b)])
            g3 = vT[:, :, vidx:vidx + 1].broadcast_to([128, 8, 128])
            S.op("dve", lambda e, t=t, b=b, g3=g3: e.tensor_tensor(out=hT[:, :, t * 128:(t + 1) * 128], in0=psb(b).rearrange("p (k n) -> p k n", k=8), in1=g3, op=ALU.mult),
                 reads=[("ps", b), "vT"], writes=[("hT", t)])

    sg = big[:, 0:2048].rearrange("p (b h n) -> p b h n", b=2, h=2)
    hid = big[:, 2048:3072].bitcast(BF16).rearrange("p (b h n) -> p b h n", b=2, h=2)
    fstate = {"set": 0}

    def ffn(l, f):
        norm_phase(f * 2 * DEPTH + l if f == 0 else 2 * DEPTH + l)
        S.barrier()
        seq = [(c, tg) for c in range(11) for tg in range(5)]
        slots = {}

        def gu(i):
            c, tg = seq[i]
            if c not in slots:
                slots[c] = next_chunk()
            s = slots[c]
            t0, tn = TG[tg]
            b = i % 2
            hkeys = [("hT", t) for t in range(t0 // 128, (t0 + tn) // 128)]
            for bank, off in ((0, 0), (1, 128), (2, 2048), (3, 2048 + 128)):
                def f_mm(e, bank=bank, off=off, s=s, t0=t0, tn=tn):
                    for k in range(8):
                        base = (off // 2048) * 2048 + k * 256 + (off % 2048)
                        ins = e.matmul(ps[:, bank, 0:tn], lhsT=wring[:, s, base:base + 128], rhs=hT[:, k, t0:t0 + tn], start=(k == 0), stop=(k == 7))
                    return ins
                S.op("pe", f_mm, reads=[("w", s)] + hkeys, writes=[("ps", bank)])
            for h in range(2):
                S.op("act", lambda e, h=h, b=b, tn=tn: e.activation(out=sg[:, b, h, 0:tn], in_=ps[:, h, 0:tn], func=AF.Silu),
                     reads=[("ps", h)], writes=[("sg", b, h)])
                S.op("dve", lambda e, h=h, b=b, tn=tn: e.tensor_tensor(out=hid[:, b, h, 0:tn], in0=sg[:, b, h, 0:tn], in1=ps[:, 2 + h, 0:tn], op=ALU.mult),
                     reads=[("sg", b, h), ("ps", 2 + h)], writes=[("hid", b, h)])

        def down(i):
            c, tg = seq[i]
            s = slots[c]
            t0, tn = TG[tg]
            b = i % 2
            for tt in range(tn // 128):
                t = t0 // 128 + tt
                st = fstate["set"]
                fstate["set"] ^= 1
                b0 = 4 + 2 * st

                def f_dn(e, tt=tt, b0=b0, s=s, b=b):
                    for ncol in range(2):
                        for h in range(2):
                            ins = e.matmul(ps[:, b0 + ncol, :], lhsT=hid[:, b, h, tt * 128:(tt + 1) * 128],
                                           rhs=wring[:, s, 4096 + h * 1024 + ncol * 512:4096 + h * 1024 + (ncol + 1) * 512],
                                           start=(h == 0), stop=(h == 1))
                    return ins
                S.op("pe", f_dn, reads=[("w", s), ("hid", b, 0), ("hid", b, 1)], writes=[("ps", b0), ("ps", b0 + 1)])
                S.op("dve", lambda e, t=t, b0=b0: e.scalar_tensor_tensor(out=x_sb[:, t, :], in0=ps[:, b0:b0 + 2, :].rearrange("p a n -> p (a n)"), scalar=0.5, in1=x_sb[:, t, :], op0=ALU.mult, op1=ALU.add),
                     reads=[("ps", b0), ("ps", b0 + 1), ("x", t)], writes=[("x", t)])

        gu(0)
        for i in range(1, len(seq)):
            gu(i)
            down(i - 1)
            if seq[i - 1][1] == 4:
                chunk_done()
        down(len(seq) - 1)
        chunk_done()

    class Scr:
        off = 0

        def reset(self):
            self.off = 0

        def f32(self, n):
            a = scr[:, self.off:self.off + n]
            self.off += n
            assert self.off <= 6144, self.off
            return a

        def bf16(self, n):
            m = (n + 1) // 2
            a = scr[:, self.off:self.off + m].bitcast(BF16)
            self.off += m
            assert self.off <= 6144, self.off
            return a
    SC = Scr()
    ENGS4 = ("pe", "act", "dve", "pool")
    BARK = [("bar", e) for e in ENGS4]
    scr_keys = []

    def phase_barrier():
        if not _os.environ.get("K_PB_NOKEYS"):
            for e in ("pe", "act", "dve"):
                waits = S._deps(e, scr_keys, scr_keys)
                S.prog[e].append((waits, None, None))
        if not _os.environ.get("K_PB_NOBAR"):
            S.barrier()
        del scr_keys[:]
        SC.reset()

    def skey(k):
        scr_keys.append(k)
        return k

    halo_on = cb[:, 64:65]
    Ef = cb[0:64, 128:640].rearrange("p (c m) -> p c m", c=4)
    mrow_n = ca[:, 0:512]
    mrow_0 = ca[:, 512:1024]
    smask = [ca[:, 1024 + g * 256:1024 + (g + 1) * 256] for g in range(3)]
    cmask = [ca[:, 1792:1824], ca[:, 1824:1952], ca[:, 1952:2208]]
    oh = ca[:, 2208:2720].rearrange("p (h m) -> p h m", h=8)
    zlhs = ca[:, 2720:2848]
    CW0 = 3 * DEPTH + N_AB

    def out_proj(lhsT_all, tagk):
        cnt = 0
        for hf in range(2):
            s = next_chunk()
            for t in range(NT):
                bank = 6 + (cnt % 2)
                cnt += 1

                def f_mm(e, t=t, s=s, bank=bank):
                    for k in range(8):
                        ins = e.matmul(ps[:, bank, :], lhsT=lhsT_all[:, k, t * 128:(t + 1) * 128], rhs=wring[:, s, k * 512:(k + 1) * 512], start=(k == 0), stop=(k == 7))
                    return ins
                S.op("pe", f_mm, reads=[("w", s), (tagk, t)], writes=[("ps", bank)])
                S.op("dve", lambda e, t=t, hf=hf, bank=bank: e.tensor_tensor(out=x_sb[:, t, hf * 512:(hf + 1) * 512], in0=ps[:, bank, :], in1=x_sb[:, t, hf * 512:(hf + 1) * 512], op=ALU.add),
                     reads=[("ps", bank), ("x", t)], writes=[("x", t)])
            chunk_done()

    cx_in = [dint("cx_in%d" % j, [2, D], F32) for j in range(N_C)]
    cx_out = [dint("cx_out%d" % j, [4, D], F32) for j in range(N_C)]

    def conv_mixer(l):
        j = l // 2
        norm_phase(DEPTH + l)
        phase_barrier()
        mT = big[:, :].bitcast(BF16).rearrange("p (k n) -> p k n", k=8)
        urows = SC.f32(2048).rearrange("p (a n) -> p a n", a=2)
        rowbuf = SC.f32(1024)
        t1 = rowbuf.rearrange("p (a n) -> p a n", a=2)
        RBK = ["rowbuf", ("c_t1", 0), ("c_t1", 1)]
        Ub = SC.f32(2 * 514).rearrange("p (a n) -> p a n", a=2)
        yb = SC.f32(512)
        tails = SC.f32(16).rearrange("p (f n) -> p f n", f=8)
        halo = SC.f32(16).rearrange("p (f n) -> p f n", f=8)
        sprev = SC.f32(256).rearrange("p (f b n) -> p f b n", f=8, b=16)
        Us = SC.f32(2 * 160).rearrange("p (a b n) -> p a b n", a=2, b=16)
        ys = SC.f32(128).rearrange("p (b n) -> p b n", b=16)
        wi = CW0 + 3 * j
        for sc in range(4):
            s = next_chunk()
            for ti, t in enumerate((15, 16)):
                for pi, part in enumerate((1, 2)):
                    bank = ti * 2 + pi

                    def f_mm(e, t=t, s=s, bank=bank, part=part):
                        for k in range(8):
                            ins = e.matmul(ps[:, bank, 0:256], lhsT=hT[:, k, t * 128:(t + 1) * 128], rhs=wring[:, s, part * 2048 + k * 256:part * 2048 + (k + 1) * 256], start=(k == 0), stop=(k == 7))
                        return ins
                    S.op("pe", f_mm, reads=[("w", s), ("hT", t)], writes=[("ps", bank)])
                S.op("act", lambda e, ti=ti: e.copy(out=t1[:, ti, 0:256], in_=ps[:, ti * 2, 0:256]), reads=[("ps", ti * 2)], writes=[("c_t1", ti)])
                S.op("dve", lambda e, ti=ti, sc=sc: e.tensor_tensor(out=urows[:, ti, sc * 256:(sc + 1) * 256], in0=t1[:, ti, 0:256], in1=ps[:, ti * 2 + 1, 0:256], op=ALU.mult),
                     reads=[("c_t1", ti), ("ps", ti * 2 + 1)], writes=[skey(("urows", ti, sc))])
            chunk_done()
        ur0 = [("urows", 0, sc) for sc in range(4)]
        ur1 = [("urows", 1, sc) for sc in range(4)]
        S.dma("sp", convp_d[j], urows[126:128, 0, :], reads=ur0, writes=[("convp", j)])
        S.dma("sp", convs_d[j, :, 0, :], urows[6:128:8, 1, :], reads=ur1, writes=[("convs0", j)])
        S.dma("sp", convs_d[j, :, 1, :], urows[7:128:8, 1, :], reads=ur1, writes=[("convs1", j)])
        out_keys.extend([("convp", j), ("convs0", j), ("convs1", j)])
        S.dma("sp", cx_in[j], urows[126:128, 0, :], reads=ur0, writes=[("cx_in", j)])
        S.collective(cx_in[j], cx_out[j], groups, reads=[("cx_in", j)], writes=[("cx_out", j)])
        S.dma("sp", rowbuf[0:2, :], cx_out[j][0:2, :], reads=[("cx_out", j)], writes=[skey("rowbuf")] + RBK[1:])

        def f_tr2(e):
            for f in range(8):
                ins = e.transpose(out=ps[:, 4, f * 2:(f + 1) * 2], in_=rowbuf[0:2, f * 128:(f + 1) * 128], identity=ident_f[0:2, 0:2])
            return ins
        S.op("pe", f_tr2, reads=["rowbuf", "ident_f"], writes=[("ps", 4)])
        S.op("dve", lambda e: e.tensor_scalar(out=halo[:, :, :], in0=ps[:, 4, 0:16].rearrange("p (f n) -> p f n", f=8), scalar1=halo_on, scalar2=None, op0=ALU.mult),
             reads=[("ps", 4), "cb"], writes=["c_halo"])
        S.dma("sp", rowbuf[0:32, :], sconv_d[j].rearrange("b n d -> (b n) d"), reads=[("ps", 4)], writes=[skey("rowbuf")] + RBK[1:])

        def f_tr32(e):
            for f in range(8):
                ins = e.transpose(out=ps[:, 5, f * 32:(f + 1) * 32], in_=rowbuf[0:32, f * 128:(f + 1) * 128], identity=ident_f[0:32, 0:32])
            return ins
        S.op("pe", f_tr32, reads=["rowbuf", "ident_f"], writes=[("ps", 5)])
        S.op("act", lambda e: e.copy(out=sprev[:, :, :, :], in_=ps[:, 5, 0:256].rearrange("p (f b n) -> p f b n", f=8, b=16)), reads=[("ps", 5)], writes=["c_sprev"])
        ui = 0
        for sc in range(4):
            s = next_chunk()
            for tg in range(5):
                t0, tn = TG[tg]
                hkeys = [("hT", t) for t in range(t0 // 128, (t0 + tn) // 128)]
                for fh in range(2):
                    for part in range(3):
                        bank = fh * 3 + part

                        def f_mm(e, s=s, bank=bank, part=part, fh=fh, t0=t0, tn=tn):
                            for k in range(8):
                                base = part * 2048 + k * 256 + fh * 128
                                ins = e.matmul(ps[:, bank, 0:tn], lhsT=wring[:, s, base:base + 128], rhs=hT[:, k, t0:t0 + tn], start=(k == 0), stop=(k == 7))
                            return ins
                        S.op("pe", f_mm, reads=[("w", s)] + hkeys, writes=[("ps", bank)])
                for fh in range(2):
                    f = sc * 2 + fh
                    bgb, bgc, bv = fh * 3, fh * 3 + 1, fh * 3 + 2
                    w0, w1, w2 = (vT[:, f, wi + tt:wi + tt + 1] for tt in range(3))
                    ub = ui % 2
                    ui += 1
                    S.op("act", lambda e, ub=ub, bgc=bgc, tn=tn: e.copy(out=t1[:, ub, 0:tn], in_=ps[:, bgc, 0:tn]), reads=[("ps", bgc)], writes=[("c_t1", ub), "rowbuf"])
                    mkeys = [("mT", t) for t in range(t0 // 128, (t0 + tn) // 128)]
                    if tg < 4:
                        S.op("dve", lambda e, ub=ub, bv=bv: e.tensor_tensor(out=Ub[:, ub, 2:514], in0=t1[:, ub, 0:512], in1=ps[:, bv, :], op=ALU.mult),
                             reads=[("c_t1", ub), ("ps", bv)], writes=[("c_U", ub)])
                        src = halo[:, f, :] if tg == 0 else tails[:, f, :]
                        S.op("pool", lambda e, ub=ub, src=src: e.tensor_copy(out=Ub[:, ub, 0:2], in_=src), reads=["c_halo", ("c_tail", f)], writes=[("c_Uh", ub)])
                        S.op("pool", lambda e, ub=ub, f=f: e.tensor_copy(out=tails[:, f, :], in_=Ub[:, ub, 512:514]), reads=[("c_U", ub), ("c_Uh", ub)], writes=[("c_tail", f)])
                        S.op("dve", lambda e, ub=ub, w0=w0: e.tensor_scalar(out=yb[:, :], in0=Ub[:, ub, 0:512], scalar1=w0, scalar2=None, op0=ALU.mult),
                             reads=[("c_U", ub), ("c_Uh", ub), "vT"], writes=["c_y"])
                        S.op("dve", lambda e, ub=ub, w1=w1: e.scalar_tensor_tensor(out=yb[:, :], in0=Ub[:, ub, 1:513], scalar=w1, in1=yb[:, :], op0=ALU.mult, op1=ALU.add),
                             reads=["c_y"], writes=["c_y"])
                        S.op("dve", lambda e, ub=ub, w2=w2: e.scalar_tensor_tensor(out=yb[:, :], in0=Ub[:, ub, 2:514], scalar=w2, in1=yb[:, :], op0=ALU.mult, op1=ALU.add),
                             reads=["c_y"], writes=["c_y"])
                        S.op("dve", lambda e, f=f, t0=t0, bgb=bgb: e.tensor_tensor(out=mT[:, f, t0:t0 + 512], in0=yb[:, :], in1=ps[:, bgb, :], op=ALU.mult),
                             reads=["c_y", ("ps", bgb)], writes=mkeys)
                    else:
                        v3 = lambda ap: ap.rearrange("p (b n) -> p b n", b=16)
                        S.op("dve", lambda e, ub=ub, bv=bv: e.tensor_tensor(out=Us[:, ub, :, 2:10], in0=v3(t1[:, ub, 0:128]), in1=v3(ps[:, bv, 0:128]), op=ALU.mult),
                             reads=[("c_t1", ub), ("ps", bv)], writes=[("c_Us", ub)])
                        S.op("pool", lambda e, ub=ub, f=f: e.tensor_copy(out=Us[:, ub, :, 0:2], in_=sprev[:, f, :, :]), reads=["c_sprev"], writes=[("c_Ush", ub)])
                        S.op("dve", lambda e, ub=ub, w0=w0: e.tensor_scalar(out=ys[:, :, :], in0=Us[:, ub, :, 0:8], scalar1=w0, scalar2=None, op0=ALU.mult),
                             reads=[("c_Us", ub), ("c_Ush", ub), "vT"], writes=["c_ys"])
                        S.op("dve", lambda e, ub=ub, w1=w1: e.scalar_tensor_tensor(out=ys[:, :, :], in0=Us[:, ub, :, 1:9], scalar=w1, in1=ys[:, :, :], op0=ALU.mult, op1=ALU.add),
                             reads=["c_ys"], writes=["c_ys"])
                        S.op("dve", lambda e, ub=ub, w2=w2: e.scalar_tensor_tensor(out=ys[:, :, :], in0=Us[:, ub, :, 2:10], scalar=w2, in1=ys[:, :, :], op0=ALU.mult, op1=ALU.add),
                             reads=["c_ys"], writes=["c_ys"])
                        S.op("dve", lambda e, f=f, bgb=bgb: e.tensor_tensor(out=v3(mT[:, f, 2048:2176]), in0=ys[:, :, :], in1=v3(ps[:, bgb, 0:128]), op=ALU.mult),
                             reads=["c_ys", ("ps", bgb)], writes=mkeys)
            chunk_done()
        out_proj(mT, "mT")

    NHT = (1, 4, 16)
    recs = [dint("recs%d" % j, [6 * NT * 128, 768], BF16) for j in range(N_AB)]
    ex_in = [[[dint("ex_in%d_%d_%d" % (j, g, hh), [NHT[g] * 128, 256], F32) for hh in range(2)] for g in range(3)] for j in range(N_AB)]
    ex_out = [[[dint("ex_out%d_%d_%d" % (j, g, hh), [2 * NHT[g] * 128, 256], F32) for hh in range(2)] for g in range(3)] for j in range(N_AB)]
    pt_in = [dint("pt_in%d" % j, [128, 60], F32) for j in range(N_AB)]
    pt_out = [dint("pt_out%d" % j, [256, 60], F32) for j in range(N_AB)]

    def tile_geom(g, jt):
        d = DILS[g]
        cls = 16 // d
        r, n = jt // cls, jt % cls
        start = r + d * 128 * n
        return d, r, n, start

    def ab_mixer(l):
        j = l // 2
        norm_phase(DEPTH + l)
        phase_barrier()
        O_acc = big[:, :].rearrange("p (c n) -> p c n", c=4)
        S.op("pool", lambda e: e.memset(big[:, :], 0.0), writes=["Oacc"])
        S.op("pool", lambda e: e.memset(den_acc[:, :], 0.0), writes=["den"])
        ptail = SC.f32(64).rearrange("p (g n) -> p g n", g=4)
        phalo = cb[:, 66:126].rearrange("p (g n) -> p g n", g=4)
        s = next_chunk()

        def f_pt(e, s=s):
            for gi in range(4):
                for k in range(8):
                    ins = e.matmul(ps[:, 2, gi * 16:(gi + 1) * 16], lhsT=wring[:, s, k * 512 + gi * 128:k * 512 + (gi + 1) * 128], rhs=hT[:, k, NP - 16:NP], start=(k == 0), stop=(k == 7))
            return ins
        S.op("pe", f_pt, reads=[("w", s), ("hT", 15)], writes=[("ps", 2)])
        chunk_done()
        S.op("act", lambda e: e.copy(out=ptail[:, :, :], in_=ps[:, 2, 0:64].rearrange("p (g n) -> p g n", g=4)), reads=[("ps", 2)], writes=[skey("ptail")])
        S.dma("sp", pt_in[j].rearrange("p (g n) -> p g n", g=4), ptail[:, :, 1:16], reads=["ptail"], writes=[("pt_in", j)])
        S.collective(pt_in[j], pt_out[j], groups, reads=[("pt_in", j)], writes=[("pt_out", j)])
        S.dma("sp", phalo[:, :, :], pt_out[j][0:128, :].rearrange("p (g n) -> p g n", g=4), reads=[("pt_out", j)], writes=[skey("phalo_raw")])
        S.op("dve", lambda e: e.tensor_scalar(out=phalo[:, :, :], in0=phalo[:, :, :], scalar1=halo_on, scalar2=None, op0=ALU.mult), reads=["phalo_raw", "cb"], writes=["phalo"])

        AB_STOP = int(_os.environ.get("K_AB_STOP", "9"))
        used = [1]

        def bail():
            for _ in range(10 - used[0]):
                next_chunk()
                chunk_done()
        if AB_STOP <= 1:
            return bail()
        ropeG = SC.f32(NT * 64).rearrange("p (t n) -> p t n", t=NT)
        tmp = SC.f32(1024).rearrange("p (a n) -> p a n", a=4)
        rot = SC.f32(1024).rearrange("p (a n) -> p a n", a=2)
        rot_bf2 = SC.bf16(1024).rearrange("p (a n) -> p a n", a=2)
        vf = SC.f32(512).rearrange("p (a n) -> p a n", a=2)
        rec = SC.bf16(2 * 768).rearrange("p (a n) -> p a n", a=2)
        ai = 0
        pA = 0
        pend = [None]
        for g in (2, 1, 0):
            d = DILS[g]
            rows_g = min(WINS[g], NP)
            S.dma("sp", ropeG[:, :, :], rope_d[g].rearrange("t p n -> p t n"), writes=[skey("ropeG")])
            for hh in range(2):
                pid = g * 2 + hh
                s = next_chunk()
                for jt in range(NT):
                    if jt < 16:
                        d_, r, n, start = tile_geom(g, jt)
                        tok = lambda k, start=start, d=d: hT[:, k, start:start + d * 127 + 1:d]
                        hk = ALL_HT[:16]
                    else:
                        tok = lambda k: hT[:, k, NP:T]
                        hk = [("hT", 16)]
                    b = ai % 2
                    ai += 1
                    bA = 4 + 2 * (pA % 2)
                    pA += 1

                    def f_mm(e, s=s, bA=bA, tok=tok):
                        for part in range(3):
                            for k in range(8):
                                o = ps[:, bA + part // 2, (part % 2) * 256:(part % 2) * 256 + 256]
                                ins = e.matmul(o, lhsT=tok(k), rhs=wring[:, s, part * 2048 + k * 256:part * 2048 + (k + 1) * 256], start=(k == 0), stop=(k == 7))
                        return ins
                    S.op("pe", f_mm, reads=[("w", s)] + hk, writes=[("ps", bA), ("ps", bA + 1)])
                    qk4 = ps[:, bA, :].rearrange("p (a h f) -> p a h f", a=8, h=2)
                    cosb = ropeG[:, jt, 0:32].unsqueeze(1).broadcast_to([128, 8, 32])
                    sinb = ropeG[:, jt, 32:64].unsqueeze(1).broadcast_to([128, 8, 32])
                    t4 = lambda i: tmp[:, i, :].rearrange("p (a f) -> p a f", a=8)
                    for i, (hsel, tb) in enumerate(((0, cosb), (1, sinb), (1, cosb), (0, sinb))):
                        S.op("dve", lambda e, i=i, hsel=hsel, tb=tb, qk4=qk4: e.tensor_tensor(out=t4(i), in0=qk4[:, :, hsel, :], in1=tb, op=ALU.mult),
                             reads=[("ps", bA), "ropeG"], writes=[("a_tmp", i)])
                    rot4 = rot[:, b, :].rearrange("p (a h f) -> p a h f", a=8, h=2)
                    S.op("dve", lambda e, rot4=rot4: e.tensor_tensor(out=rot4[:, :, 0, :], in0=t4(0), in1=t4(1), op=ALU.subtract),
                         reads=[("a_tmp", 0), ("a_tmp", 1)], writes=[skey(("a_rot0", b))])
                    S.op("dve", lambda e, rot4=rot4: e.tensor_tensor(out=rot4[:, :, 1, :], in0=t4(2), in1=t4(3), op=ALU.add),
                         reads=[("a_tmp", 2), ("a_tmp", 3)], writes=[skey(("a_rot1", b))])
                    S.op("act", lambda e, b=b: e.copy(out=rot_bf2[:, b, :], in_=rot[:, b, :]), reads=[("a_rot0", b), ("a_rot1", b)], writes=[("a_rotbf", b)])
                    S.op("act", lambda e, b=b, bA=bA: e.copy(out=vf[:, b, :], in_=ps[:, bA + 1, 0:256]), reads=[("ps", bA + 1)], writes=[skey(("a_vf", b))])
                    S.op("act", lambda e, b=b: e.copy(out=rec[:, b, 512:768], in_=vf[:, b, :]), reads=[("a_vf", b)], writes=[skey(("a_recv", b))])
                    kk = rot[:, b, 256:512].rearrange("p (h f) -> p h f", h=4)
                    vv = vf[:, b, :].rearrange("p (h f) -> p h f", h=4)
                    if jt == 16:
                        okk, okv = ("kvsK", j, g, hh), ("kvsV", j, g, hh)
                        S.dma("sp", kvs_d[g][j, :, 0, hh * 4:hh * 4 + 4, :], kk, reads=[("a_rot0", b), ("a_rot1", b)], writes=[okk])
                        S.dma("sp", kvs_d[g][j, :, 1, hh * 4:hh * 4 + 4, :], vv, reads=[("a_vf", b)], writes=[okv])
                        out_keys.extend([okk, okv])
                    elif start + d * 127 >= NP - rows_g and n == 16 // d - 1:
                        r0 = start - (NP - rows_g)
                        okk, okv = ("kvrK", j, g, hh, jt), ("kvrV", j, g, hh, jt)
                        S.dma("sp", kvrow_d[g][j, r0:r0 + d * 127 + 1:d, 0, hh * 4:hh * 4 + 4, :], kk, reads=[("a_rot0", b), ("a_rot1", b)], writes=[okk])
                        S.dma("sp", kvrow_d[g][j, r0:r0 + d * 127 + 1:d, 1, hh * 4:hh * 4 + 4, :], vv, reads=[("a_vf", b)], writes=[okv])
                        out_keys.extend([okk, okv])
                    bT = pA % 2
                    is_halo = jt < 16 and n == 16 // d - 1
                    hi = r if jt < 16 else 0

                    def back(b=b, bT=bT, jt=jt, pid=pid, g=g, hh=hh, is_halo=is_halo, hi=hi):
                        def f_tr(e):
                            for c4 in range(4):
                                ins = e.transpose(out=psb(bT)[:, c4 * 128:(c4 + 1) * 128], in_=rot_bf2[:, b, c4 * 128:(c4 + 1) * 128], identity=ident_bf[:])
                            return ins
                        S.op("pe", f_tr, reads=[("a_rotbf", b), "ident_bf"], writes=[("ps", bT)])
                        S.op("act", lambda e: e.copy(out=rec[:, b, 0:512], in_=psb(bT)[:, 0:512]), reads=[("ps", bT)], writes=[skey(("a_recqk", b))])
                        rrow = (pid * NT + jt) * 128
                        S.dma("sp", recs[j][rrow:rrow + 128, :], rec[:, b, :], reads=[("a_recqk", b), ("a_recv", b)], writes=[("rec", j, pid, jt)])
                        if is_halo:
                            erow = hi * 128
                            S.dma("sp", ex_in[j][g][hh][erow:erow + 128, :], rec[:, b, 256:768].bitcast(F32), reads=[("a_recqk", b), ("a_recv", b)], writes=[("ex_in", j, g, hh, hi)])
                    if pend[0] is not None:
                        pend[0]()
                    pend[0] = back
                pend[0]()
                pend[0] = None
                chunk_done()
                S.collective(ex_in[j][g][hh], ex_out[j][g][hh], groups, reads=[("ex_in", j, g, hh, hi_) for hi_ in range(NHT[g])], writes=[("ex_out", j, g, hh)])

        used[0] = 7
        if AB_STOP <= 2:
            return bail()
        phase_barrier()
        qk_c = SC.bf16(3 * 512).rearrange("p (a n) -> p a n", a=3)
        v_c = SC.bf16(3 * 256).rearrange("p (a n) -> p a n", a=3)
        k_p = SC.bf16(3 * 256).rearrange("p (a n) -> p a n", a=3)
        v_p = SC.bf16(3 * 256).rearrange("p (a n) -> p a n", a=3)
        pT = SC.bf16(2 * 1024).rearrange("p (a n) -> p a n", a=2)
        kc = SC.bf16(2 * 1024).rearrange("p (a b n) -> p a b n", a=2, b=4)
        vc = SC.bf16(2 * 1024).rearrange("p (a b n) -> p a b n", a=2, b=4)
        kTs = SC.bf16(2048).rearrange("p (a b c n) -> p a b c n", a=2, b=4, c=2)
        pTc = SC.bf16(2 * 128).rearrange("p (a n) -> p a n", a=2)
        bi = 0
        ci = 0
        pendB = [None]
        for g in (0, 1, 2):
            d = DILS[g]
            nblk = (1, 4, 8)[g]
            for hh in range(2):
                pid = g * 2 + hh
                order = [jt for jt in range(16) if jt % (16 // d) != 0] + [jt for jt in range(16) if jt % (16 // d) == 0] + [16]
                if _os.environ.get("K_PB_TILES") is not None:
                    order = [int(v) for v in _os.environ["K_PB_TILES"].split(",") if v != ""]
                for jt in order:
                    b = bi % 2
                    b3 = bi % 3
                    bi += 1
                    rrow = (pid * NT + jt) * 128
                    S.dma("sp", qk_c[:, b3, :], recs[j][rrow:rrow + 128, 0:512], reads=[("rec", j, pid, jt)], writes=[skey(("b_qk", b3))])
                    S.dma("sp", v_c[:, b3, :], recs[j][rrow:rrow + 128, 512:768], reads=[("rec", j, pid, jt)], writes=[skey(("b_v", b3))])
                    blocks = [(qk_c[:, b3, 256:512], v_c[:, b3, :], [("b_qk", b3), ("b_v", b3)])]
                    if jt < 16:
                        d_, r, n, start = tile_geom(g, jt)
                        if n > 0:
                            prow = (pid * NT + jt - 1) * 128
                            S.dma("sp", k_p[:, b3, :], recs[j][prow:prow + 128, 256:512], reads=[("rec", j, pid, jt - 1)], writes=[skey(("b_kp", b3))])
                            S.dma("sp", v_p[:, b3, :], recs[j][prow:prow + 128, 512:768], reads=[("rec", j, pid, jt - 1)], writes=[skey(("b_vp", b3))])
                        else:
                            erow = r * 128
                            S.dma("sp", k_p[:, b3, :].bitcast(F32), ex_out[j][g][hh][erow:erow + 128, 0:128], reads=[("ex_out", j, g, hh)], writes=[skey(("b_kp", b3))])
                            S.dma("sp", v_p[:, b3, :].bitcast(F32), ex_out[j][g][hh][erow:erow + 128, 128:256], reads=[("ex_out", j, g, hh)], writes=[skey(("b_vp", b3))])
                        blocks.append((k_p[:, b3, :], v_p[:, b3, :], [("b_kp", b3), ("b_vp", b3)]))
                        mrow = mrow_0 if n == 0 else mrow_n
                    nb = len(blocks)
                    bkeys = [k for blk in blocks for k in blk[2]]

                    samp = (jt == 16)
                    if samp:
                        pidx = lambda h4, cp: (h4 % 2) * 512 + (h4 // 2) * 128
                    else:
                        pidx = lambda h4, cp: (h4 % 2) * 512 + (h4 // 2) * 256 + cp * 128

                    sbk = 0 if b == 0 else 6

                    def f_sc(e, b3=b3, blocks=blocks, nb=nb, g=g, samp=samp, pidx=pidx, sbk=sbk, mrow=(mrow if jt < 16 else None)):
                        for bank in range(2):
                            if samp:
                                e.matmul(ps[:, sbk + bank, 0:256], lhsT=ident_bf[:], rhs=smask[g], start=True, stop=False)
                            else:
                                e.matmul(ps[:, sbk + bank, :], lhsT=ident_bf[:], rhs=mrow, start=True, stop=False)
                        for h4 in (0, 2, 1, 3):
                            pb = (h4 % 2) * 64
                            for cp in range(nb):
                                c0 = pidx(h4, cp) % 512
                                kT = blocks[cp][0]
                                ins = e.matmul(ps[:, sbk + h4 % 2, c0:c0 + 128], lhsT=kT[pb:pb + 64, (h4 // 2) * 128:(h4 // 2) * 128 + 128], rhs=qk_c[pb:pb + 64, b3, (h4 // 2) * 128:(h4 // 2) * 128 + 128], start=False, stop=True)
                        return ins
                    S.op("pe", f_sc, reads=bkeys + ["ca", "ident_bf"], writes=[("ps", sbk), ("ps", sbk + 1)])
                    if not samp:
                        S.op("act", lambda e, b=b, sbk=sbk: e.activation(out=pT[:, b, :], in_=ps[:, sbk:sbk + 2, :].rearrange("p a n -> p (a n)"), func=AF.Exp, scale=0.125),
                             reads=[("ps", sbk), ("ps", sbk + 1)], writes=[("b_pT", b)])
                    else:
                        S.op("act", lambda e, b=b, sbk=sbk: e.activation(out=pT[:, b, :].rearrange("p (a n) -> p a n", a=2)[:, :, 0:256], in_=ps[:, sbk:sbk + 2, 0:256], func=AF.Exp, scale=0.125),
                             reads=[("ps", sbk), ("ps", sbk + 1)], writes=[("b_pT", b)])

                    def f_pv(e, b=b, blocks=blocks, nb=nb, hh=hh, pidx=pidx):
                        for hs in range(2):
                            for cc in range(2):
                                h4 = 2 * cc + hs
                                o = ps[:, 2, (hs * 2 + cc) * 128:(hs * 2 + cc + 1) * 128]
                                for cp in range(nb):
                                    V = blocks[cp][1]
                                    ins = e.matmul(o, lhsT=V[:, cc * 128:(cc + 1) * 128], rhs=pT[:, b, pidx(h4, cp):pidx(h4, cp) + 128], start=(cp == 0), stop=(cp == nb - 1))
                        first = True
                        for h4 in range(4):
                            for cp in range(nb):
                                ins = e.matmul(ps[0:64, 3, 0:128], lhsT=oh[:, hh * 4 + h4, :], rhs=pT[:, b, pidx(h4, cp):pidx(h4, cp) + 128], start=first, stop=(h4 == 3 and cp == nb - 1))
                                first = False
                        return ins
                    pv_reads = [("b_pT", b)] + bkeys + ["ca"]
                    if jt < 16:
                        cols = slice(start, start + d * 127 + 1, d)
                    else:
                        cols = slice(NP, T)
                    oa0 = O_acc[0:64, hh * 2:hh * 2 + 2, cols]
                    oa1 = O_acc[64:128, hh * 2:hh * 2 + 2, cols]

                    def accum(oa0=oa0, oa1=oa1, cols=cols):
                        S.op("dve", lambda e, oa0=oa0: e.tensor_tensor(out=oa0, in0=ps[0:64, 2, 0:256].rearrange("p (c n) -> p c n", c=2), in1=oa0, op=ALU.add),
                             reads=[("ps", 2), "Oacc"], writes=["Oacc"])
                        S.op("dve", lambda e, oa1=oa1: e.tensor_tensor(out=oa1, in0=ps[64:128, 2, 256:512].rearrange("p (c n) -> p c n", c=2), in1=oa1, op=ALU.add),
                             reads=[("ps", 2), "Oacc"], writes=["Oacc"])
                        S.op("dve", lambda e, cols=cols: e.tensor_tensor(out=den_acc[:, cols], in0=ps[0:64, 3, 0:128], in1=den_acc[:, cols], op=ALU.add),
                             reads=[("ps", 3), "den"], writes=["den"])
                    def fin(f_pv=f_pv, pv_reads=pv_reads, accum=accum):
                        S.op("pe", f_pv, reads=pv_reads, writes=[("ps", 2), ("ps", 3)])
                        accum()
                    if pendB[0] is not None:
                        pendB[0]()
                        pendB[0] = None
                    if samp:
                        fin()
                    else:
                        pendB[0] = fin
                    if samp and not _os.environ.get("K_NOCACHE"):
                        def f_z(e):
                            e.matmul(ps[:, 2, :], lhsT=zlhs, rhs=mrow_n, start=True, stop=False)
                            return e.matmul(ps[0:64, 3, 0:128], lhsT=zlhs[:, 0:64], rhs=ident_bf[:], start=True, stop=False)
                        S.op("pe", f_z, reads=["ca", "ident_bf"], writes=[("ps", 2), ("ps", 3)])
                    if jt == 16 and not _os.environ.get("K_NOCACHE"):
                        stp = [(0, nblk)] if nblk <= 4 else [(0, 4), (4, 8)]
                        steps = [(sb_, b0, b1) for sb_ in range(16) for (b0, b1) in stp]
                        NS = len(steps)

                        def cw_view(i):
                            sb_, b0, b1 = steps[i]
                            return cw_d[g][j, sb_].rearrange("(p r) kv h f -> p r kv h f", r=d), b0, b1

                        def loadK(i):
                            if i >= NS:
                                return
                            cwv, b0, b1 = cw_view(i)
                            S.dma("pool", kc[:, i % 2, 0:b1 - b0, :].rearrange("p r (h f) -> p r h f", h=4), cwv[:, b0:b1, 0, hh * 4:hh * 4 + 4, :], writes=[skey(("c_kc", i % 2))])

                        def loadV(i):
                            if i >= NS:
                                return
                            cwv, b0, b1 = cw_view(i)
                            S.dma("pool", vc[:, i % 2, 0:b1 - b0, :].rearrange("p r (h f) -> p r h f", h=4), cwv[:, b0:b1, 1, hh * 4:hh * 4 + 4, :], writes=[skey(("c_vc", i % 2))])

                        def stA(i):
                            if i >= NS:
                                return
                            sb_, b0, b1 = steps[i]
                            nbk = b1 - b0
                            c2 = i % 2

                            def f_ct(e):
                                for bl in range(nbk):
                                    for cc in range(2):
                                        ins = e.transpose(out=psb(4)[:, (bl * 2 + cc) * 128:(bl * 2 + cc + 1) * 128], in_=kc[:, c2, bl, cc * 128:(cc + 1) * 128], identity=ident_bf[:])
                                return ins
                            S.op("pe", f_ct, reads=[("c_kc", c2), "ident_bf"], writes=[("ps", 4)])
                            S.op("act", lambda e: e.copy(out=kTs[:, c2, 0:nbk, :, :], in_=psb(4)[:, 0:nbk * 256].rearrange("p (b c n) -> p b c n", b=nbk, c=2)), reads=[("ps", 4)], writes=[("c_kT", c2)])

                        def stB(i):
                            sb_, b0, b1 = steps[i]
                            nbk = b1 - b0
                            c2 = i % 2

                            def f_cs(e, g=g, b3=b3):
                                cm = cmask[g][:, b0 * 32:(b0 + nbk) * 32].rearrange("p (r h t) -> p r h t", r=nbk, h=4)[:, :, 0:2, :]
                                for par in range(2):
                                    e.matmul(ps[:, 5 + par, 0:nbk * 16].rearrange("p (r h t) -> p r h t", r=nbk, h=2), lhsT=ident_bf[:], rhs=cm, start=True, stop=False)
                                for par in range(2):
                                    pb = par * 64
                                    for bl in range(nbk):
                                        for hp in range(2):
                                            ins = e.matmul(ps[:, 5 + par, (bl * 2 + hp) * 8:(bl * 2 + hp + 1) * 8], lhsT=kTs[pb:pb + 64, c2, bl, hp, :],
                                                           rhs=qk_c[pb:pb + 64, b3, hp * 128 + sb_ * 8:hp * 128 + sb_ * 8 + 8], start=False, stop=True)
                                return ins
                            S.op("pe", f_cs, reads=[("c_kT", c2), ("b_qk", b3), "ca", "ident_bf"], writes=[("ps", 5), ("ps", 6)])
                            for par in range(2):
                                S.op("act", lambda e, par=par: e.activation(out=pTc[:, c2, par * 64:par * 64 + nbk * 16], in_=ps[:, 5 + par, 0:nbk * 16], func=AF.Exp, scale=0.125),
                                     reads=[("ps", 5 + par)], writes=[("c_pT", c2, par)])

                        def stC(i):
                            if i < 0:
                                return
                            sb_, b0, b1 = steps[i]
                            nbk = b1 - b0
                            c2 = i % 2

                            def f_cpv(e, hh=hh):
                                for hs in range(2):
                                    for cc in range(2):
                                        for bl in range(nbk):
                                            ins = e.matmul(ps[:, 2, (hs * 2 + cc) * 128 + sb_ * 8:(hs * 2 + cc) * 128 + sb_ * 8 + 8], lhsT=vc[:, c2, bl, cc * 128:(cc + 1) * 128],
                                                           rhs=pTc[:, c2, hs * 64 + (bl * 2 + cc) * 8:hs * 64 + (bl * 2 + cc + 1) * 8], start=False, stop=True)
                                for bl in range(nbk):
                                    for h4 in range(4):
                                        ins = e.matmul(ps[0:64, 3, sb_ * 8:sb_ * 8 + 8], lhsT=oh[:, hh * 4 + h4, :], rhs=pTc[:, c2, (h4 % 2) * 64 + (bl * 2 + h4 // 2) * 8:(h4 % 2) * 64 + (bl * 2 + h4 // 2 + 1) * 8], start=False, stop=True)
                                return ins
                            S.op("pe", f_cpv, reads=[("c_pT", c2, 0), ("c_pT", c2, 1), ("c_vc", c2), "ca"], writes=[("ps", 2), ("ps", 3)])

                        loadK(0)
                        loadK(1)
                        loadV(0)
                        stA(0)
                        for i in range(NS):
                            loadK(i + 2)
                            stA(i + 1)
                            stB(i)
                            stC(i - 1)
                            loadV(i + 1)
                        stC(NS - 1)
                        accum()

        if AB_STOP <= 3:
            return bail()
        phase_barrier()
        apT = hT
        Ab = SC.f32(2 * 527).rearrange("p (a n) -> p a n", a=2)
        ptl = SC.f32(60).rearrange("p (g n) -> p g n", g=4)
        dTb = SC.bf16(2 * 512).rearrange("p (a n) -> p a n", a=2)
        uoff = SC.off
        As = SC.f32(2 * 368).rearrange("p (a b n) -> p a b n", a=2, b=16)
        SCR_S0 = SC.f32(368).rearrange("p (b n) -> p b n", b=16)
        SCR_S1 = SC.f32(368).rearrange("p (b n) -> p b n", b=16)
        utok = scr[:, uoff:uoff + 1024].rearrange("p (a n) -> p a n", a=2)
        UTK = [("utok", 0), ("utok", 1)]
        sprevT = SC.f32(4 * 240).rearrange("p (g b n) -> p g b n", g=4, b=16)
        rowb = SC.f32(512)
        SCR_T0 = SC.f32(527)
        SCR_T1 = SC.f32(527)
        rden = SCR_T0[:, 0:512]
        rc_tmp = SC.f32(16)
        PS0 = 3 * DEPTH + j
        s = next_chunk()
        wpool = lambda k, a, b_: wring[:, s, k * 512 + a:k * 512 + b_]
        pw = wring[:, s, 4096:4608].rearrange("p (g d) -> p g d", g=4)
        for ti, t in enumerate((15, 16)):
            def f_mm(e, t=t, ti=ti):
                for k in range(8):
                    ins = e.matmul(ps[:, 6 + ti, :], lhsT=hT[:, k, t * 128:(t + 1) * 128], rhs=wpool(k, 0, 512), start=(k == 0), stop=(k == 7))
                return ins
            S.op("pe", f_mm, reads=[("w", s), ("hT", t)], writes=[("ps", 6 + ti)])
            S.op("act", lambda e, ti=ti: e.copy(out=utok[:, ti, :], in_=ps[:, 6 + ti, :]), reads=[("ps", 6 + ti)], writes=[skey(("utok", ti))])
        S.dma("sp", poolp_d[j], utok[113:128, 0, :], reads=[("utok", 0)], writes=[("poolp", j)])
        S.dma("sp", pools_d[j, :, 7:15, :], utok[:, 1, :], reads=[("utok", 1)], writes=[("pools_n", j)])
        S.dma("sp", pools_d[j, :, 0:7, :], spool_d[j, :, 8:15, :], writes=[("pools_o", j)])
        out_keys.extend([("poolp", j), ("pools_n", j), ("pools_o", j)])
        for half in range(2):
            S.dma("sp", rowb[0:120, :], spool_d[j, half * 8:(half + 1) * 8].rearrange("b n c -> (b n) c"), writes=[skey("rowb")])

            def f_tr(e):
                for gi in range(4):
                    ins = e.transpose(out=ps[:, 4, gi * 120:(gi + 1) * 120], in_=rowb[0:120, gi * 128:(gi + 1) * 128], identity=ident_f[0:120, 0:120])
                return ins
            S.op("pe", f_tr, reads=["rowb", "ident_f"], writes=[("ps", 4)])
            S.op("act", lambda e, half=half: e.copy(out=sprevT[:, :, half * 8:(half + 1) * 8, :], in_=ps[:, 4, 0:480].rearrange("p (g b n) -> p g b n", g=4, b=8)), reads=[("ps", 4)], writes=[("p_sprev", half)])
        ai = 0
        for tg in range(5):
            t0, tn = TG[tg]
            hkeys = [("hT", t) for t in range(t0 // 128, (t0 + tn) // 128)]
            for gi in range(4):
                def f_mm(e, gi=gi, t0=t0, tn=tn):
                    for k in range(8):
                        ins = e.matmul(ps[:, gi, 0:tn], lhsT=wpool(k, gi * 128, (gi + 1) * 128), rhs=hT[:, k, t0:t0 + tn], start=(k == 0), stop=(k == 7))
                    return ins
                S.op("pe", f_mm, reads=[("w", s)] + hkeys, writes=[("ps", gi)])
            for gi in range(4):
                w = 2 << gi
                a = ai % 2
                ai += 1
                if tg < 4:
                    A0, A1 = Ab[:, a, :], Ab[:, 1 - a, :]
                    S.op("act", lambda e, A0=A0, gi=gi: e.copy(out=A0[:, 15:527], in_=ps[:, gi, :]), reads=[("ps", gi)], writes=[("p_A", a)])
                    src = phalo[:, gi, :] if tg == 0 else ptl[:, gi, :]
                    S.op("pool", lambda e, A0=A0, src=src: e.tensor_copy(out=A0[:, 0:15], in_=src), reads=["phalo", ("p_tl", gi)], writes=[("p_Ah", a)])
                    S.op("pool", lambda e, A0=A0, gi=gi: e.tensor_copy(out=ptl[:, gi, :], in_=A0[:, 512:527]), reads=[("p_A", a), ("p_Ah", a)], writes=[("p_tl", gi)])
                    cur = A0
                    srcs = [("p_A", a), ("p_Ah", a)]
                    stepk = 1
                    tbuf = [SCR_T0, SCR_T1]
                    ti_ = 0
                    while stepk < w:
                        nxt = tbuf[ti_ % 2]
                        ti_ += 1
                        S.op("dve", lambda e, cur=cur, nxt=nxt, stepk=stepk: e.tensor_tensor(out=nxt[:, stepk:527], in0=cur[:, stepk:527], in1=cur[:, 0:527 - stepk], op=ALU.add),
                             reads=srcs, writes=[("p_T", ti_ % 2)])
                        srcs = [("p_T", ti_ % 2)]
                        cur = nxt
                        stepk *= 2
                    db = ai % 2
                    S.op("dve", lambda e, cur=cur, A0=A0, db=db, w=w: e.scalar_tensor_tensor(out=dTb[:, db, :], in0=cur[:, 15:527], scalar=1.0 / w, in1=A0[:, 15:527], op0=ALU.mult, op1=ALU.subtract),
                         reads=srcs + [("p_A", a)], writes=[("p_d", db)])
                    if tg == 0:
                        S.op("dve", lambda e, cur=cur, gi=gi: e.tensor_tensor(out=rc_tmp[:, :], in0=cur[:, 15:31], in1=cb[:, gi * 16:(gi + 1) * 16], op=ALU.mult), reads=srcs + ["cb"], writes=["p_rc"])
                        S.op("dve", lambda e, A0=A0, db=db: e.tensor_tensor(out=dTb[:, db, 0:16], in0=rc_tmp[:, :], in1=A0[:, 15:31], op=ALU.subtract), reads=["p_rc", ("p_A", a), ("p_d", db)], writes=[("p_d", db)])
                    rhs_d = dTb[:, db, :]
                else:
                    A0 = As[:, a, :, :]
                    S.op("act", lambda e, A0=A0, gi=gi: e.copy(out=A0[:, :, 15:23], in_=ps[:, gi, 0:128].rearrange("p (b n) -> p b n", b=16)), reads=[("ps", gi)], writes=[("p_As", a)] + UTK)
                    S.op("pool", lambda e, A0=A0, gi=gi: e.tensor_copy(out=A0[:, :, 0:15], in_=sprevT[:, gi, :, :]), reads=[("p_sprev", 0), ("p_sprev", 1)], writes=[("p_Ash", a)])
                    cur = A0
                    srcs = [("p_As", a), ("p_Ash", a)]
                    stepk = 1
                    tbuf = [SCR_S0, SCR_S1]
                    ti_ = 0
                    while stepk < w:
                        nxt = tbuf[ti_ % 2]
                        ti_ += 1
                        S.op("dve", lambda e, cur=cur, nxt=nxt, stepk=stepk: e.tensor_tensor(out=nxt[:, :, stepk:23], in0=cur[:, :, stepk:23], in1=cur[:, :, 0:23 - stepk], op=ALU.add),
                             reads=srcs, writes=[("p_TS", ti_ % 2)] + UTK)
                        srcs = [("p_TS", ti_ % 2)]
                        cur = nxt
                        stepk *= 2
                    db = ai % 2
                    S.op("dve", lambda e, cur=cur, A0=A0, db=db, w=w: e.scalar_tensor_tensor(out=dTb[:, db, 0:128].rearrange("p (b n) -> p b n", b=16), in0=cur[:, :, 15:23], scalar=1.0 / w, in1=A0[:, :, 15:23], op0=ALU.mult, op1=ALU.subtract),
                         reads=srcs + [("p_As", a)], writes=[("p_d", db)])
                    rhs_d = dTb[:, db, 0:128]
                ob = 6 + (ai % 2)
                S.op("pe", lambda e, gi=gi, rhs_d=rhs_d, ob=ob, tn=tn: e.matmul(ps[:, ob, 0:tn], lhsT=pw[:, gi, :], rhs=rhs_d, start=True, stop=True), reads=[("w", s), ("p_d", db)], writes=[("ps", ob)])
                S.op("act", lambda e, gi=gi, ob=ob, t0=t0, tn=tn: e.activation(out=apT[:, 4 + gi, t0:t0 + tn], in_=ps[:, ob, 0:tn], func=AF.Copy, scale=vT[:, gi, PS0:PS0 + 1]),
                     reads=[("ps", ob), "vT"] + [("ps", g_) for g_ in range(4)], writes=[("hT", t) for t in range(t0 // 128, (t0 + tn) // 128)])
        chunk_done()
        used[0] = 8
        if AB_STOP <= 4:
            return bail()
        ni = 0
        for tg in range(5):
            t0, tn = TG[tg]
            akeys = [("hT", t) for t in range(t0 // 128, (t0 + tn) // 128)]
            for c in range(4):
                bk = 4 + (ni % 2)
                ni += 1
                S.op("pe", lambda e, c=c, bk=bk, t0=t0, tn=tn: e.matmul(ps[:, bk, 0:tn], lhsT=Ef[:, c, :], rhs=den_acc[:, t0:t0 + tn], start=True, stop=True), reads=["den", "cb"], writes=[("ps", bk)])
                S.op("dve", lambda e, bk=bk, tn=tn: e.reciprocal(out=rden[:, 0:tn], in_=ps[:, bk, 0:tn]), reads=[("ps", bk)], writes=["p_rden", ("p_T", 0), ("p_T", 1)])
                S.op("dve", lambda e, c=c, t0=t0, tn=tn: e.tensor_tensor(out=apT[:, c, t0:t0 + tn], in0=O_acc[:, c, t0:t0 + tn], in1=rden[:, 0:tn], op=ALU.mult), reads=["p_rden", "Oacc"] + akeys, writes=akeys)
        out_proj(apT, "hT")
    for l in range(DEPTH):
        ffn(l, 0)
        if ENABLE_MIX:
            if l % 2 == 0 and ENABLE_AB:
                ab_mixer(l)
            if l % 2 == 1 and ENABLE_CONV:
                conv_mixer(l)
        ffn(l, 1)

    S.barrier()
    S.dma("sp", gfin, bass.AP(gfin_d.tensor, 0, [[0, 128], [1, D]]), writes=["gfin"])
    for t in range(NT):
        S.op("act", lambda e, t=t: e.activation(out=junk, in_=x_sb[:, t, :], func=AF.Square, accum_out=ss[:, t:t + 1]),
             reads=[("x", t)], writes=[("ss", t)])
    S.op("act", lambda e: e.activation(out=sd[:], in_=ss[:], func=AF.Sqrt, scale=1.0 / D, bias=eps_t[:]),
         reads=[("ss", t) for t in range(NT)] + ["eps"], writes=["sd"])
    S.op("dve", lambda e: e.reciprocal(out=rstd[:], in_=sd[:]), reads=["sd"], writes=["rstd"])
    for t in range(NT):
        b = t % 2
        S.op("dve", lambda e, t=t, b=b: e.scalar_tensor_tensor(out=ystage[:, b, :], in0=x_sb[:, t, :], scalar=rstd[:, t:t + 1], in1=gfin, op0=ALU.mult, op1=ALU.mult),
             reads=[("x", t), "rstd", "gfin"], writes=[("ys", b)])
        S.dma("sp", y_d[t * 128:(t + 1) * 128, :], ystage[:, b, :], reads=[("ys", b)], writes=[("y", t)])
        out_keys.append(("y", t))

    S.wait_keys("sp", out_keys)
    S.replay()
    return nc


_PROG_CACHE = {}


def _host_consts(half):
    c = np.zeros((128, 4096), np.float32)
    kk = np.arange(128)[:, None]
    qq = np.arange(128)[None, :]
    maskC = np.where(kk <= qq, 0.0, NEG).astype(np.float32)
    maskP = np.where(kk >= qq, 0.0, NEG).astype(np.float32)
    mp0 = maskP if half == 1 else np.full((128, 128), NEG, np.float32)
    c[:, 0:512] = np.concatenate([maskC, maskP, maskC, maskP], 1)
    c[:, 512:1024] = np.concatenate([maskC, mp0, maskC, mp0], 1)
    bk, tk = kk // 8, kk % 8
    bq, tq = qq // 8, qq % 8
    same = bk == bq
    conds = [tk <= tq, (tk <= tq) & ((tq - tk) % 4 == 0), tk == tq]
    for g in range(3):
        sm = np.where(same & conds[g], 0.0, NEG)
        c[:, 1024 + g * 256:1024 + (g + 1) * 256] = np.concatenate([sm, sm], 1)
    p = np.arange(128)[:, None, None, None]
    t = np.arange(8)[None, None, None, :]
    ones4 = np.ones((1, 1, 4, 1), bool)
    c[:, 1792:1824] = np.where((p >= t) & ones4, 0.0, NEG).reshape(128, 32)
    r4 = np.arange(4)[None, :, None, None]
    c[:, 1824:1952] = np.where(((t % 4) == r4) & (r4 + 4 * p >= t) & ones4, 0.0, NEG).reshape(128, 128)
    r8 = np.arange(8)[None, :, None, None]
    c[:, 1952:2208] = np.where((t == r8) & ones4 & (p >= 0), 0.0, NEG).reshape(128, 256)
    oh = np.zeros((128, 8, 64), np.float32)
    for hh in range(2):
        for h4 in range(4):
            oh[:, hh * 4 + h4, hh * 32 + h4] = 1.0
    c[:, 2208:2720] = oh.reshape(128, 512)
    o = 3072
    for gi in range(4):
        w = 2 << gi
        i = np.arange(16)
        c[:, o + gi * 16:o + (gi + 1) * 16] = (1.0 / np.minimum(w, i + 1)) if half == 0 else (1.0 / w)
    c[:, o + 64] = float(half)
    E = np.zeros((64, 4, 128), np.float32)
    for hh in range(2):
        for h4 in range(4):
            h = hh * 4 + h4
            E[hh * 32 + h4, h // 2, (h % 2) * 64:(h % 2) * 64 + 64] = 1.0
    c[0:64, o + 128:o + 640] = E.reshape(64, 512)
    return c


def _host_rope(half):
    inv = np.power(np.float32(10000.0), -np.arange(32, dtype=np.float32) / np.float32(32)).astype(np.float32)
    tab = np.zeros((3, NT, 128, 64), np.float32)
    p = np.arange(128)
    for g in range(3):
        d = DILS[g]
        cls = 16 // d
        for jt in range(NT):
            if jt < 16:
                r, n = jt // cls, jt % cls
                pos = half * NP + r + d * (128 * n + p)
            else:
                pos = 2048 + (p % 8)
            ang = pos.astype(np.float32)[:, None] * inv[None, :]
            tab[g, jt, :, 0:32] = np.cos(ang)
            tab[g, jt, :, 32:64] = np.sin(ang)
    return tab


def kernel(**inputs):
    x_prompt = np.asarray(inputs["x_prompt"], np.float32)
    x_sample = np.asarray(inputs["x_sample"], np.float32)
    B = x_prompt.shape[0]
    DEPTH = inputs["ffn1_norm"].shape[0]
    N_AB = (DEPTH + 1) // 2
    N_C = DEPTH // 2
    n_cores = 2 * B
    DB = x_sample.shape[0]
    assert DB == 16 * n_cores and x_prompt.shape[1] == 2 * NP
    key = (DEPTH, n_cores)
    if key not in _PROG_CACHE:
        _PROG_CACHE[key] = build_program(DEPTH, n_cores)
    nc = _PROG_CACHE[key]

    f32 = lambda a: np.ascontiguousarray(np.asarray(a, np.float32))
    vec_rows = [inputs["ffn1_norm"][l] for l in range(DEPTH)] + [inputs["mix_norm"][l] for l in range(DEPTH)] + \
               [inputs["ffn2_norm"][l] for l in range(DEPTH)]
    for j in range(N_AB):
        vec_rows.append(np.concatenate([np.asarray(inputs["pool_scale"][j]), np.asarray(inputs["pool_scale"][j])]))
    for j in range(N_C):
        for t in range(3):
            vec_rows.append(inputs["conv_w"][j][t])
    vecs = f32(np.stack([np.asarray(v, np.float32) for v in vec_rows]))
    shared = {
        "ffn1_w_gu": f32(inputs["ffn1_w_gu"]), "ffn2_w_gu": f32(inputs["ffn2_w_gu"]),
        "ffn1_w_down": f32(inputs["ffn1_w_down"]), "ffn2_w_down": f32(inputs["ffn2_w_down"]),
        "ab_w_in": f32(inputs["ab_w_in"]), "ab_w_out": f32(inputs["ab_w_out"]), "pool_w": f32(inputs["pool_w"]),
        "conv_w_in": f32(inputs["conv_w_in"]) if N_C else np.zeros((1, D, 3 * D), np.float32),
        "conv_w_out": f32(inputs["conv_w_out"]) if N_C else np.zeros((1, D, D), np.float32),
        "vecs": vecs, "gfin": f32(np.asarray(inputs["final_norm"]).reshape(1, D)),
        "ident": np.eye(128, dtype=np.float32),
    }
    in_maps = []
    for c in range(n_cores):
        b, half = c // 2, c % 2
        m = dict(shared)
        m["x"] = f32(np.concatenate([x_prompt[b, half * NP:(half + 1) * NP], x_sample[c * 16:(c + 1) * 16].reshape(128, D)], 0))
        m["cw0"] = f32(inputs["cache_win0"][:, c * 16:(c + 1) * 16])
        m["cw1"] = f32(inputs["cache_win1"][:, c * 16:(c + 1) * 16])
        m["cw2"] = f32(inputs["cache_win2"][:, c * 16:(c + 1) * 16])
        m["spool"] = f32(inputs["state_pool"][:, c * 16:(c + 1) * 16])
        m["sconv"] = f32(inputs["state_conv"][:, c * 16:(c + 1) * 16]) if N_C else np.zeros((1, 16, 2, D), np.float32)
        m["rope"] = _host_rope(half)
        m["consts"] = _host_consts(half)
        in_maps.append(m)
    res = run_bass_kernel_spmd(nc, in_maps, core_ids=list(range(n_cores)))
    R = res.results
    S_ = x_prompt.shape[1]
    y_prompt = np.zeros((B, S_, D), np.float32)
    y_sample = np.zeros((DB, 8, D), np.float32)
    for c in range(n_cores):
        b, half = c // 2, c % 2
        y_prompt[b, half * NP:(half + 1) * NP] = R[c]["y"][:NP]
        y_sample[c * 16:(c + 1) * 16] = R[c]["y"][NP:].reshape(16, 8, D)
    outs = [y_prompt, y_sample]
    for g in range(3):
        outs.append(np.stack([R[2 * b + 1]["kvrow%d" % g] for b in range(B)], 1))
    outs.append(np.stack([R[2 * b + 1]["poolp"] for b in range(B)], 1))
    outs.append(np.stack([R[2 * b + 1]["convp"][:N_C] for b in range(B)], 1))
    for g in range(3):
        outs.append(np.concatenate([R[c]["kvs%d" % g].reshape(N_AB, 16, 8, 2, 8, 64) for c in range(n_cores)], 1))
    outs.append(np.concatenate([R[c]["pools"] for c in range(n_cores)], 1))
    outs.append(np.concatenate([R[c]["convs"][:N_C] for c in range(n_cores)], 1))
    return tuple(outs)
```

```python
import numpy as np
import concourse.bass as bass
import concourse.mybir as mybir
from concourse.bass_utils import run_bass_kernel_spmd

F32 = mybir.dt.float32
BF16 = mybir.dt.bfloat16
AF = mybir.ActivationFunctionType
ALU = mybir.AluOpType

D = 1024
DFF = 2816
NPT = 16
NT = 17
T = NT * 128
NP = NPT * 128
TG = [(0, 512), (512, 512), (1024, 512), (1536, 512), (2048, 128)]
DILS = (1, 4, 16)
WINS = (128, 512, 2048)
EPS = 1e-6
NEG = -30000.0
SAME_ENGINE_SYNC = True
ENABLE_MIX = True
ENABLE_AB = True
ENABLE_CONV = True
import os as _os
if _os.environ.get("K_NOAB"):
    ENABLE_AB = False
if _os.environ.get("K_NOCONV"):
    ENABLE_CONV = False


class Sched:
    ENGS = ("pe", "act", "dve", "pool", "sp")

    def __init__(self, nc, n_dsem=20):
        self.nc = nc
        self.sem = {e: nc.alloc_semaphore("s_" + e) for e in self.ENGS}
        self.cnt = {e: 0 for e in self.ENGS}
        self.prog = {e: [] for e in self.ENGS}
        self.seen = {e: {} for e in self.ENGS}
        self.writer = {}
        self.readers = {}
        self.dsem = {q: [nc.alloc_semaphore("d_%s%d" % (q, i)) for i in range(n_dsem)] for q in ("sp", "pool")}
        self.dcnt = {q: [0] * n_dsem for q in ("sp", "pool")}
        self.dnext = {"sp": 0, "pool": 0}
        self.csem = []
        self.fence = []

    def _semh(self, sk):
        if isinstance(sk, str):
            return self.sem[sk]
        if sk[0] == "cc":
            return self.csem[sk[1]]
        return self.dsem[sk[0]][sk[1]]

    def _deps(self, eng, reads, writes):
        deps = {}

        def add(t):
            if t is not None and deps.get(t[0], 0) < t[1]:
                deps[t[0]] = t[1]
        for k in reads:
            add(self.writer.get(k))
        for k in writes:
            add(self.writer.get(k))
            for r in self.readers.get(k, ()):
                add(r)
        waits = []
        for sk, v in deps.items():
            if sk == eng and (eng == "pe" or not SAME_ENGINE_SYNC):
                continue
            if self.seen[eng].get(sk, 0) >= v:
                continue
            self.seen[eng][sk] = v
            waits.append((sk, v))
        return waits

    def _commit(self, tok, reads, writes):
        for k in writes:
            self.writer[k] = tok
            self.readers[k] = []
        for k in reads:
            self.readers.setdefault(k, []).append(tok)

    def op(self, eng, fn, reads=(), writes=()):
        waits = self._deps(eng, reads, writes)
        self.cnt[eng] += 1
        tok = (eng, self.cnt[eng])
        self.prog[eng].append((waits, fn, (eng, 1)))
        self._commit(tok, reads, writes)
        return tok

    def dma(self, q, out, in_, reads=(), writes=(), **kw):
        waits = self._deps(q, list(reads) + self.fence, writes)
        i = self.dnext[q]
        self.dnext[q] = (i + 1) % len(self.dsem[q])
        sk = (q, i)
        prev = self.dcnt[q][i]
        if prev > 0 and self.seen[q].get(sk, 0) < prev:
            self.seen[q][sk] = prev
            waits.append((sk, prev))
        self.dcnt[q][i] = prev + 16
        tok = (sk, prev + 16)
        self.prog[q].append((waits, lambda e: e.dma_start(out=out, in_=in_, **kw), (sk, 16)))
        self._commit(tok, reads, writes)
        return tok

    def collective(self, ins, outs, groups, reads=(), writes=()):
        waits = self._deps("pool", reads, writes)
        self.csem.append(self.nc.alloc_semaphore("cc%d" % len(self.csem)))
        sk = ("cc", len(self.csem) - 1)
        tok = (sk, 1)
        self.prog["pool"].append((waits, lambda e: e.collective_compute(
            "AllGather", ALU.bypass, replica_groups=groups, ins=[ins], outs=[outs]), (sk, 1)))
        self._commit(tok, reads, writes)
        return tok

    def barrier(self):
        engs = ("pe", "act", "dve")
        keys = [("bar", e) for e in ("pe", "act", "dve", "pool")]
        for e in ("pe", "act", "dve", "pool"):
            self.writer[("bar", e)] = (e, self.cnt[e]) if self.cnt[e] > 0 else None
        for e in engs:
            waits = self._deps(e, keys, ())
            self.prog[e].append((waits, None, None))
        self.fence = [("bar", e) for e in ("pe", "act", "dve")]

    def wait_keys(self, eng, keys):
        waits = self._deps(eng, keys, ())
        self.prog[eng].append((waits, None, None))

    def replay(self):
        nc = self.nc
        with nc.Block() as block:
            def mk(name):
                def run(e):
                    for waits, fn, inc in self.prog[name]:
                        for sk, v in waits:
                            e.wait_ge(self._semh(sk), v)
                        if fn is not None:
                            ins = fn(e)
                            ins.then_inc(self._semh(inc[0]), inc[1])
                return run
            block.tensor(mk("pe"))
            block.scalar(mk("act"))
            block.vector(mk("dve"))
            block.gpsimd(mk("pool"))
            block.sync(mk("sp"))


def build_program(DEPTH, n_cores):
    N_AB = (DEPTH + 1) // 2
    N_C = DEPTH // 2
    nc = bass.Bass("TRN2", target_bir_lowering=False)
    S = Sched(nc)
    groups = [[2 * i, 2 * i + 1] for i in range(n_cores // 2)]

    def din(name, shape, dt=F32):
        return nc.dram_tensor(name, list(shape), dt, kind="ExternalInput").ap()

    def dout(name, shape):
        return nc.dram_tensor(name, list(shape), F32, kind="ExternalOutput").ap()

    def dint(name, shape, dt):
        return nc.dram_tensor(name, list(shape), dt).ap()

    def sb(name, shape, dt):
        return nc.alloc_sbuf_tensor(name, list(shape), dt)

    x_d = din("x", [T, D])
    cw_d = [din("cw%d" % g, [N_AB, 16, WINS[g], 2, 8, 64]) for g in range(3)]
    spool_d = din("spool", [N_AB, 16, 15, 512])
    sconv_d = din("sconv", [max(N_C, 1), 16, 2, D])
    wgu_d = [din("ffn1_w_gu", [DEPTH, D, 2 * DFF]), din("ffn2_w_gu", [DEPTH, D, 2 * DFF])]
    wdn_d = [din("ffn1_w_down", [DEPTH, DFF, D]), din("ffn2_w_down", [DEPTH, DFF, D])]
    abin_d = din("ab_w_in", [N_AB, D, 5120])
    about_d = din("ab_w_out", [N_AB, D, D])
    poolw_d = din("pool_w", [N_AB, 4, 128, 128])
    cvin_d = din("conv_w_in", [max(N_C, 1), D, 3 * D])
    cvout_d = din("conv_w_out", [max(N_C, 1), D, D])
    NV = 3 * DEPTH + N_AB + 3 * N_C
    vecs_d = din("vecs", [NV, D])
    gfin_d = din("gfin", [1, D])
    ident_d = din("ident", [128, 128])
    rope_d = din("rope", [3, NT, 128, 64])
    consts_d = din("consts", [128, 4096])

    y_d = dout("y", [T, D])
    kvrow_d = [dout("kvrow%d" % g, [N_AB, min(WINS[g], NP), 2, 8, 64]) for g in range(3)]
    kvs_d = [dout("kvs%d" % g, [N_AB, 128, 2, 8, 64]) for g in range(3)]
    poolp_d = dout("poolp", [N_AB, 15, 512])
    pools_d = dout("pools", [N_AB, 16, 15, 512])
    convp_d = dout("convp", [max(N_C, 1), 2, D])
    convs_d = dout("convs", [max(N_C, 1), 16, 2, D])
    out_keys = []

    x_sb = sb("x_sb", [128, NT, D], F32)
    hT = sb("hT", [128, 8, T], BF16)
    NSLOT = 2
    wring = sb("wring", [128, NSLOT, 6144], BF16)
    big = sb("big", [128, 8704], F32)
    ident_bf = sb("ident_bf", [128, 128], BF16)
    ident_f = sb("ident_f", [128, 128], F32)
    vT = sb("vT", [128, 8, NV], F32)
    vrows = big[0:32, 0:D]
    ss = sb("ss", [128, NT], F32)
    sd = sb("sd", [128, NT], F32)
    rstd = sb("rstd", [128, NT], F32)
    eps_t = sb("eps_t", [128, 1], F32)
    xn = sb("xn", [128, 2, D], BF16)
    gfin = big[:, 0:D]
    ystage = big[:, D:3 * D].rearrange("p (b n) -> p b n", b=2)
    scr = sb("scr", [128, 6144], F32)
    den_acc = sb("den_acc", [64, T], F32)
    CA = 2848
    ca = sb("ca", [128, CA], BF16)
    cb = sb("cb", [128, 640], F32)
    ps = nc.alloc_psum_tensor("ps", [128, 8, 512], F32)

    def psb(bank):
        return ps[:, bank, :].bitcast(BF16)
    junk = ps[:, 6:8, :].rearrange("p a n -> p (a n)")

    chunks = []

    def slotv(s, a, b):
        return wring[:, s, a:b]

    def kview(ap2d):
        return ap2d.rearrange("(k p) n -> p k n", p=128)

    def add_ffn_chunks(l, f):
        for c in range(11):
            chunks.append([
                (lambda s: slotv(s, 0, 2048).rearrange("p (k n) -> p k n", k=8), kview(wgu_d[f][l])[:, :, c * 256:(c + 1) * 256]),
                (lambda s: slotv(s, 2048, 4096).rearrange("p (k n) -> p k n", k=8), kview(wgu_d[f][l])[:, :, DFF + c * 256:DFF + (c + 1) * 256]),
                (lambda s: slotv(s, 4096, 6144).rearrange("p (k n) -> p k n", k=2), kview(wdn_d[f][l][c * 256:(c + 1) * 256, :])),
            ])

    def k8(a, b, n):
        return lambda s: slotv(s, a, b).rearrange("p (k n) -> p k n", k=8)

    def add_pool_chunk(j):
        chunks.append([
            (k8(0, 4096, 512), kview(abin_d[j])[:, :, 4608:5120]),
            (lambda s: slotv(s, 4096, 4608).rearrange("p (g d) -> p g d", g=4), poolw_d[j].rearrange("g c d -> c g d")),
        ])

    def add_out_chunks(wd):
        for hf in range(2):
            chunks.append([(k8(0, 4096, 512), kview(wd)[:, :, hf * 512:(hf + 1) * 512])])

    def add_ab_chunks(j):
        add_pool_chunk(j)
        for g in (2, 1, 0):
            for hh in range(2):
                chunks.append([(k8(part * 2048, (part + 1) * 2048, 256),
                                kview(abin_d[j])[:, :, g * 1536 + part * 512 + hh * 256:g * 1536 + part * 512 + hh * 256 + 256]) for part in range(3)])
        add_pool_chunk(j)
        add_out_chunks(about_d[j])

    def add_conv_chunks(j):
        for rep in range(2):
            for sc in range(4):
                chunks.append([(k8(part * 2048, (part + 1) * 2048, 256),
                                kview(cvin_d[j])[:, :, part * 1024 + sc * 256:part * 1024 + sc * 256 + 256]) for part in range(3)])
        add_out_chunks(cvout_d[j])

    for l in range(DEPTH):
        add_ffn_chunks(l, 0)
        if ENABLE_MIX:
            if l % 2 == 0 and ENABLE_AB:
                add_ab_chunks(l // 2)
            if l % 2 == 1 and ENABLE_CONV:
                add_conv_chunks(l // 2)
        add_ffn_chunks(l, 1)

    wstate = {"next_load": 0, "next_use": 0}

    def issue_loads(upto):
        while wstate["next_load"] <= min(upto, len(chunks) - 1):
            i = wstate["next_load"]
            s = i % NSLOT
            for dst_fn, src in chunks[i]:
                S.dma("pool", dst_fn(s), src, writes=[("w", s)])
            wstate["next_load"] += 1

    def next_chunk():
        i = wstate["next_use"]
        wstate["next_use"] += 1
        assert i < wstate["next_load"], "chunk not loaded"
        return i % NSLOT

    def chunk_done():
        issue_loads(wstate["next_load"])

    S.op("dve", lambda e: e.memset(eps_t[:], EPS), writes=["eps"])
    for t in range(NT):
        S.dma("sp", x_sb[:, t, :], x_d[t * 128:(t + 1) * 128, :], writes=[("x", t)])
    S.dma("sp", ident_f[:], ident_d, writes=["ident_f"])
    S.dma("pool", ident_bf[:], ident_d, writes=["ident_bf"])
    S.dma("sp", vrows[0:NV, :], vecs_d, writes=["vrows"])
    S.dma("pool", ca[:], consts_d[:, 0:CA], writes=["ca"])
    S.dma("sp", cb[:], consts_d[:, 3072:3072 + 640], writes=["cb"])
    issue_loads(NSLOT - 1)
    def f_vt(e):
        for k in range(8):
            ins = e.transpose(out=ps[:, 0, k * NV:(k + 1) * NV], in_=vrows[0:NV, k * 128:(k + 1) * 128], identity=ident_f[0:NV, 0:NV])
        return ins
    S.op("pe", f_vt, reads=["vrows", "ident_f"], writes=[("ps", 0)])
    S.op("act", lambda e: e.copy(out=vT[:], in_=ps[:, 0, 0:8 * NV].rearrange("p (k v) -> p k v", k=8)), reads=[("ps", 0)], writes=["vT"])

    ALL_HT = [("hT", t) for t in range(NT)]
    nstate = {"i": 0}

    def norm_phase(vidx):
        for (t0, tn) in TG:
            ta, tb = t0 // 128, (t0 + tn) // 128
            for t in range(ta, tb):
                S.op("act", lambda e, t=t: e.activation(out=junk, in_=x_sb[:, t, :], func=AF.Square, accum_out=ss[:, t:t + 1]),
                     reads=[("x", t)], writes=[("ss", t), ("ps", 6), ("ps", 7)])
            S.op("act", lambda e, ta=ta, tb=tb: e.activation(out=sd[:, ta:tb], in_=ss[:, ta:tb], func=AF.Sqrt, scale=1.0 / D, bias=eps_t[:]),
                 reads=[("ss", t) for t in range(ta, tb)] + ["eps"], writes=[("sd", ta)])
            S.op("dve", lambda e, ta=ta, tb=tb: e.reciprocal(out=rstd[:, ta:tb], in_=sd[:, ta:tb]), reads=[("sd", ta)], writes=[("rstd", ta)])
            for t in range(ta, tb):
                i = nstate["i"]
                nstate["i"] += 1
                b = i % 2
                S.op("dve", lambda e, t=t, b=b: e.tensor_scalar(out=xn[:, b, :], in0=x_sb[:, t, :], scalar1=rstd[:, t:t + 1], scalar2=None, op0=ALU.mult),
                     reads=[("x", t), ("rstd", ta)], writes=[("xn", b)])

                def f_tr(e, b=b):
                    for k in range(8):
                        ins = e.transpose(out=psb(b)[:, k * 128:(k + 1) * 128], in_=xn[:, b, k * 128:(k + 1) * 128], identity=ident_bf[:])
                    return ins
                S.op("pe", f_tr, reads=[("xn", b), "ident_bf"], writes=[("ps", b)])
                g3 = vT[:, :, vidx:vidx + 1].broadcast_to([128, 8, 128])
                S.op("dve", lambda e, t=t, b=b, g3=g3: e.tensor_tensor(out=hT[:, :, t * 128:(t + 1) * 128], in0=psb(b).rearrange("p (k n) -> p k n", k=8), in1=g3, op=ALU.mult),
                     reads=[("ps", b), "vT"], writes=[("hT", t)])

    sg = big[:, 0:2048].rearrange("p (b h n) -> p b h n", b=2, h=2)
    hid = big[:, 2048:3072].bitcast(BF16).rearrange("p (b h n) -> p b h n", b=2, h=2)
    fstate = {"set": 0}

    def ffn(l, f):
        S.barrier()
        norm_phase(f * 2 * DEPTH + l if f == 0 else 2 * DEPTH + l)
        seq = [(c, tg) for c in range(11) for tg in range(5)]
        slots = {}

        def gu(i):
            c, tg = seq[i]
            if c not in slots:
                slots[c] = next_chunk()
            s = slots[c]
            t0, tn = TG[tg]
            b = i % 2
            hkeys = [("hT", t) for t in range(t0 // 128, (t0 + tn) // 128)]
            for bank, off in ((0, 0), (1, 128), (2, 2048), (3, 2048 + 128)):
                def f_mm(e, bank=bank, off=off, s=s, t0=t0, tn=tn):
                    for k in range(8):
                        base = (off // 2048) * 2048 + k * 256 + (off % 2048)
                        ins = e.matmul(ps[:, bank, 0:tn], lhsT=wring[:, s, base:base + 128], rhs=hT[:, k, t0:t0 + tn], start=(k == 0), stop=(k == 7))
                    return ins
                S.op("pe", f_mm, reads=[("w", s)] + hkeys, writes=[("ps", bank)])
            for h in range(2):
                S.op("act", lambda e, h=h, b=b, tn=tn: e.activation(out=sg[:, b, h, 0:tn], in_=ps[:, h, 0:tn], func=AF.Silu),
                     reads=[("ps", h)], writes=[("sg", b, h)])
                S.op("dve", lambda e, h=h, b=b, tn=tn: e.tensor_tensor(out=hid[:, b, h, 0:tn], in0=sg[:, b, h, 0:tn], in1=ps[:, 2 + h, 0:tn], op=ALU.mult),
                     reads=[("sg", b, h), ("ps", 2 + h)], writes=[("hid", b, h)])

        def down(i):
            c, tg = seq[i]
            s = slots[c]
            t0, tn = TG[tg]
            b = i % 2
            for tt in range(tn // 128):
                t = t0 // 128 + tt
                st = fstate["set"]
                fstate["set"] ^= 1
                b0 = 4 + 2 * st

                def f_dn(e, tt=tt, b0=b0, s=s, b=b):
                    for ncol in range(2):
                        for h in range(2):
                            ins = e.matmul(ps[:, b0 + ncol, :], lhsT=hid[:, b, h, tt * 128:(tt + 1) * 128],
                                           rhs=wring[:, s, 4096 + h * 1024 + ncol * 512:4096 + h * 1024 + (ncol + 1) * 512],
                                           start=(h == 0), stop=(h == 1))
                    return ins
                S.op("pe", f_dn, reads=[("w", s), ("hid", b, 0), ("hid", b, 1)], writes=[("ps", b0), ("ps", b0 + 1)])
                S.op("dve", lambda e, t=t, b0=b0: e.scalar_tensor_tensor(out=x_sb[:, t, :], in0=ps[:, b0:b0 + 2, :].rearrange("p a n -> p (a n)"), scalar=0.5, in1=x_sb[:, t, :], op0=ALU.mult, op1=ALU.add),
                     reads=[("ps", b0), ("ps", b0 + 1), ("x", t)], writes=[("x", t)])

        gu(0)
        for i in range(1, len(seq)):
            gu(i)
            down(i - 1)
            if seq[i - 1][1] == 4:
                chunk_done()
        down(len(seq) - 1)
        chunk_done()

    class Scr:
        off = 0

        def reset(self):
            self.off = 0

        def f32(self, n):
            a = scr[:, self.off:self.off + n]
            self.off += n
            assert self.off <= 6144, self.off
            return a

        def bf16(self, n):
            m = (n + 1) // 2
            a = scr[:, self.off:self.off + m].bitcast(BF16)
            self.off += m
            assert self.off <= 6144, self.off
            return a
    SC = Scr()
    ENGS4 = ("pe", "act", "dve", "pool")
    BARK = [("bar", e) for e in ENGS4]
    scr_keys = []

    def phase_barrier():
        if not _os.environ.get("K_PB_NOKEYS"):
            for e in ("pe", "act", "dve"):
                waits = S._deps(e, scr_keys, scr_keys)
                S.prog[e].append((waits, None, None))
        if not _os.environ.get("K_PB_NOBAR"):
            S.barrier()
        del scr_keys[:]
        SC.reset()

    def skey(k):
        scr_keys.append(k)
        return k

    halo_on = cb[:, 64:65]
    Ef = cb[0:64, 128:640].rearrange("p (c m) -> p c m", c=4)
    mrow_n = ca[:, 0:512]
    mrow_0 = ca[:, 512:1024]
    smask = [ca[:, 1024 + g * 256:1024 + (g + 1) * 256] for g in range(3)]
    cmask = [ca[:, 1792:1824], ca[:, 1824:1952], ca[:, 1952:2208]]
    oh = ca[:, 2208:2720].rearrange("p (h m) -> p h m", h=8)
    zlhs = ca[:, 2720:2848]
    CW0 = 3 * DEPTH + N_AB

    def out_proj(lhsT_all, tagk):
        cnt = 0
        for hf in range(2):
            s = next_chunk()
            for t in range(NT):
                bank = 6 + (cnt % 2)
                cnt += 1

                def f_mm(e, t=t, s=s, bank=bank):
                    for k in range(8):
                        ins = e.matmul(ps[:, bank, :], lhsT=lhsT_all[:, k, t * 128:(t + 1) * 128], rhs=wring[:, s, k * 512:(k + 1) * 512], start=(k == 0), stop=(k == 7))
                    return ins
                S.op("pe", f_mm, reads=[("w", s), (tagk, t)], writes=[("ps", bank)])
                S.op("dve", lambda e, t=t, hf=hf, bank=bank: e.tensor_tensor(out=x_sb[:, t, hf * 512:(hf + 1) * 512], in0=ps[:, bank, :], in1=x_sb[:, t, hf * 512:(hf + 1) * 512], op=ALU.add),
                     reads=[("ps", bank), ("x", t)], writes=[("x", t)])
            chunk_done()

    cx_in = [dint("cx_in%d" % j, [2, D], F32) for j in range(N_C)]
    cx_out = [dint("cx_out%d" % j, [4, D], F32) for j in range(N_C)]

    def conv_mixer(l):
        j = l // 2
        phase_barrier()
        norm_phase(DEPTH + l)
        mT = big[:, :].bitcast(BF16).rearrange("p (k n) -> p k n", k=8)
        urows = SC.f32(2048).rearrange("p (a n) -> p a n", a=2)
        rowbuf = SC.f32(1024)
        t1 = rowbuf.rearrange("p (a n) -> p a n", a=2)
        RBK = ["rowbuf", ("c_t1", 0), ("c_t1", 1)]
        Ub = SC.f32(2 * 514).rearrange("p (a n) -> p a n", a=2)
        yb = SC.f32(512)
        tails = SC.f32(16).rearrange("p (f n) -> p f n", f=8)
        halo = SC.f32(16).rearrange("p (f n) -> p f n", f=8)
        sprev = SC.f32(256).rearrange("p (f b n) -> p f b n", f=8, b=16)
        Us = SC.f32(2 * 160).rearrange("p (a b n) -> p a b n", a=2, b=16)
        ys = SC.f32(128).rearrange("p (b n) -> p b n", b=16)
        wi = CW0 + 3 * j
        for sc in range(4):
            s = next_chunk()
            for ti, t in enumerate((15, 16)):
                for pi, part in enumerate((1, 2)):
                    bank = ti * 2 + pi

                    def f_mm(e, t=t, s=s, bank=bank, part=part):
                        for k in range(8):
                            ins = e.matmul(ps[:, bank, 0:256], lhsT=hT[:, k, t * 128:(t + 1) * 128], rhs=wring[:, s, part * 2048 + k * 256:part * 2048 + (k + 1) * 256], start=(k == 0), stop=(k == 7))
                        return ins
                    S.op("pe", f_mm, reads=[("w", s), ("hT", t)], writes=[("ps", bank)])
                S.op("act", lambda e, ti=ti: e.copy(out=t1[:, ti, 0:256], in_=ps[:, ti * 2, 0:256]), reads=[("ps", ti * 2)], writes=[("c_t1", ti)])
                S.op("dve", lambda e, ti=ti, sc=sc: e.tensor_tensor(out=urows[:, ti, sc * 256:(sc + 1) * 256], in0=t1[:, ti, 0:256], in1=ps[:, ti * 2 + 1, 0:256], op=ALU.mult),
                     reads=[("c_t1", ti), ("ps", ti * 2 + 1)], writes=[skey(("urows", ti, sc))])
            chunk_done()
        ur0 = [("urows", 0, sc) for sc in range(4)]
        ur1 = [("urows", 1, sc) for sc in range(4)]
        S.dma("sp", convp_d[j], urows[126:128, 0, :], reads=ur0, writes=[("convp", j)])
        S.dma("sp", convs_d[j, :, 0, :], urows[6:128:8, 1, :], reads=ur1, writes=[("convs0", j)])
        S.dma("sp", convs_d[j, :, 1, :], urows[7:128:8, 1, :], reads=ur1, writes=[("convs1", j)])
        out_keys.extend([("convp", j), ("convs0", j), ("convs1", j)])
        S.dma("sp", cx_in[j], urows[126:128, 0, :], reads=ur0, writes=[("cx_in", j)])
        S.collective(cx_in[j], cx_out[j], groups, reads=[("cx_in", j)], writes=[("cx_out", j)])
        S.dma("sp", rowbuf[0:2, :], cx_out[j][0:2, :], reads=[("cx_out", j)], writes=[skey("rowbuf")] + RBK[1:])

        def f_tr2(e):
            for f in range(8):
                ins = e.transpose(out=ps[:, 4, f * 2:(f + 1) * 2], in_=rowbuf[0:2, f * 128:(f + 1) * 128], identity=ident_f[0:2, 0:2])
            return ins
        S.op("pe", f_tr2, reads=["rowbuf", "ident_f"], writes=[("ps", 4)])
        S.op("dve", lambda e: e.tensor_scalar(out=halo[:, :, :], in0=ps[:, 4, 0:16].rearrange("p (f n) -> p f n", f=8), scalar1=halo_on, scalar2=None, op0=ALU.mult),
             reads=[("ps", 4), "cb"], writes=["c_halo"])
        S.dma("sp", rowbuf[0:32, :], sconv_d[j].rearrange("b n d -> (b n) d"), reads=[("ps", 4)], writes=[skey("rowbuf")] + RBK[1:])

        def f_tr32(e):
            for f in range(8):
                ins = e.transpose(out=ps[:, 5, f * 32:(f + 1) * 32], in_=rowbuf[0:32, f * 128:(f + 1) * 128], identity=ident_f[0:32, 0:32])
            return ins
        S.op("pe", f_tr32, reads=["rowbuf", "ident_f"], writes=[("ps", 5)])
        S.op("act", lambda e: e.copy(out=sprev[:, :, :, :], in_=ps[:, 5, 0:256].rearrange("p (f b n) -> p f b n", f=8, b=16)), reads=[("ps", 5)], writes=["c_sprev"])
        ui = 0
        for sc in range(4):
            s = next_chunk()
            for tg in range(5):
                t0, tn = TG[tg]
                hkeys = [("hT", t) for t in range(t0 // 128, (t0 + tn) // 128)]
                for fh in range(2):
                    for part in range(3):
                        bank = fh * 3 + part

                        def f_mm(e, s=s, bank=bank, part=part, fh=fh, t0=t0, tn=tn):
                            for k in range(8):
                                base = part * 2048 + k * 256 + fh * 128
                                ins = e.matmul(ps[:, bank, 0:tn], lhsT=wring[:, s, base:base + 128], rhs=hT[:, k, t0:t0 + tn], start=(k == 0), stop=(k == 7))
                            return ins
                        S.op("pe", f_mm, reads=[("w", s)] + hkeys, writes=[("ps", bank)])
                for fh in range(2):
                    f = sc * 2 + fh
                    bgb, bgc, bv = fh * 3, fh * 3 + 1, fh * 3 + 2
                    w0, w1, w2 = (vT[:, f, wi + tt:wi + tt + 1] for tt in range(3))
                    ub = ui % 2
                    ui += 1
                    S.op("act", lambda e, ub=ub, bgc=bgc, tn=tn: e.copy(out=t1[:, ub, 0:tn], in_=ps[:, bgc, 0:tn]), reads=[("ps", bgc)], writes=[("c_t1", ub), "rowbuf"])
                    mkeys = [("mT", t) for t in range(t0 // 128, (t0 + tn) // 128)]
                    if tg < 4:
                        S.op("dve", lambda e, ub=ub, bv=bv: e.tensor_tensor(out=Ub[:, ub, 2:514], in0=t1[:, ub, 0:512], in1=ps[:, bv, :], op=ALU.mult),
                             reads=[("c_t1", ub), ("ps", bv)], writes=[("c_U", ub)])
                        src = halo[:, f, :] if tg == 0 else tails[:, f, :]
                        S.op("pool", lambda e, ub=ub, src=src: e.tensor_copy(out=Ub[:, ub, 0:2], in_=src), reads=["c_halo", ("c_tail", f)], writes=[("c_Uh", ub)])
                        S.op("pool", lambda e, ub=ub, f=f: e.tensor_copy(out=tails[:, f, :], in_=Ub[:, ub, 512:514]), reads=[("c_U", ub), ("c_Uh", ub)], writes=[("c_tail", f)])
                        S.op("dve", lambda e, ub=ub, w0=w0: e.tensor_scalar(out=yb[:, :], in0=Ub[:, ub, 0:512], scalar1=w0, scalar2=None, op0=ALU.mult),
                             reads=[("c_U", ub), ("c_Uh", ub), "vT"], writes=["c_y"])
                        S.op("dve", lambda e, ub=ub, w1=w1: e.scalar_tensor_tensor(out=yb[:, :], in0=Ub[:, ub, 1:513], scalar=w1, in1=yb[:, :], op0=ALU.mult, op1=ALU.add),
                             reads=["c_y"], writes=["c_y"])
                        S.op("dve", lambda e, ub=ub, w2=w2: e.scalar_tensor_tensor(out=yb[:, :], in0=Ub[:, ub, 2:514], scalar=w2, in1=yb[:, :], op0=ALU.mult, op1=ALU.add),
                             reads=["c_y"], writes=["c_y"])
                        S.op("dve", lambda e, f=f, t0=t0, bgb=bgb: e.tensor_tensor(out=mT[:, f, t0:t0 + 512], in0=yb[:, :], in1=ps[:, bgb, :], op=ALU.mult),
                             reads=["c_y", ("ps", bgb)], writes=mkeys)
                    else:
                        v3 = lambda ap: ap.rearrange("p (b n) -> p b n", b=16)
                        S.op("dve", lambda e, ub=ub, bv=bv: e.tensor_tensor(out=Us[:, ub, :, 2:10], in0=v3(t1[:, ub, 0:128]), in1=v3(ps[:, bv, 0:128]), op=ALU.mult),
                             reads=[("c_t1", ub), ("ps", bv)], writes=[("c_Us", ub)])
                        S.op("pool", lambda e, ub=ub, f=f: e.tensor_copy(out=Us[:, ub, :, 0:2], in_=sprev[:, f, :, :]), reads=["c_sprev"], writes=[("c_Ush", ub)])
                        S.op("dve", lambda e, ub=ub, w0=w0: e.tensor_scalar(out=ys[:, :, :], in0=Us[:, ub, :, 0:8], scalar1=w0, scalar2=None, op0=ALU.mult),
                             reads=[("c_Us", ub), ("c_Ush", ub), "vT"], writes=["c_ys"])
                        S.op("dve", lambda e, ub=ub, w1=w1: e.scalar_tensor_tensor(out=ys[:, :, :], in0=Us[:, ub, :, 1:9], scalar=w1, in1=ys[:, :, :], op0=ALU.mult, op1=ALU.add),
                             reads=["c_ys"], writes=["c_ys"])
                        S.op("dve", lambda e, ub=ub, w2=w2: e.scalar_tensor_tensor(out=ys[:, :, :], in0=Us[:, ub, :, 2:10], scalar=w2, in1=ys[:, :, :], op0=ALU.mult, op1=ALU.add),
                             reads=["c_ys"], writes=["c_ys"])
                        S.op("dve", lambda e, f=f, bgb=bgb: e.tensor_tensor(out=v3(mT[:, f, 2048:2176]), in0=ys[:, :, :], in1=v3(ps[:, bgb, 0:128]), op=ALU.mult),
                             reads=["c_ys", ("ps", bgb)], writes=mkeys)
            chunk_done()
        out_proj(mT, "mT")

    NHT = (1, 4, 16)
    recs = [dint("recs%d" % j, [6 * NT * 128, 768], BF16) for j in range(N_AB)]
    ex_in = [[[dint("ex_in%d_%d_%d" % (j, g, hh), [NHT[g] * 128, 256], F32) for hh in range(2)] for g in range(3)] for j in range(N_AB)]
    ex_out = [[[dint("ex_out%d_%d_%d" % (j, g, hh), [2 * NHT[g] * 128, 256], F32) for hh in range(2)] for g in range(3)] for j in range(N_AB)]
    pt_in = [dint("pt_in%d" % j, [128, 60], F32) for j in range(N_AB)]
    pt_out = [dint("pt_out%d" % j, [256, 60], F32) for j in range(N_AB)]

    def tile_geom(g, jt):
        d = DILS[g]
        cls = 16 // d
        r, n = jt // cls, jt % cls
        start = r + d * 128 * n
        return d, r, n, start

    def ab_mixer(l):
        j = l // 2
        phase_barrier()
        norm_phase(DEPTH + l)
        O_acc = big[:, :].rearrange("p (c n) -> p c n", c=4)
        S.op("pool", lambda e: e.memset(big[:, :], 0.0), writes=["Oacc"])
        S.op("pool", lambda e: e.memset(den_acc[:, :], 0.0), writes=["den"])
        ptail = SC.f32(64).rearrange("p (g n) -> p g n", g=4)
        phalo = cb[:, 66:126].rearrange("p (g n) -> p g n", g=4)
        s = next_chunk()

        def f_pt(e, s=s):
            for gi in range(4):
                for k in range(8):
                    ins = e.matmul(ps[:, 2, gi * 16:(gi + 1) * 16], lhsT=wring[:, s, k * 512 + gi * 128:k * 512 + (gi + 1) * 128], rhs=hT[:, k, NP - 16:NP], start=(k == 0), stop=(k == 7))
            return ins
        S.op("pe", f_pt, reads=[("w", s), ("hT", 15)], writes=[("ps", 2)])
        chunk_done()
        S.op("act", lambda e: e.copy(out=ptail[:, :, :], in_=ps[:, 2, 0:64].rearrange("p (g n) -> p g n", g=4)), reads=[("ps", 2)], writes=[skey("ptail")])
        S.dma("sp", pt_in[j].rearrange("p (g n) -> p g n", g=4), ptail[:, :, 1:16], reads=["ptail"], writes=[("pt_in", j)])
        S.collective(pt_in[j], pt_out[j], groups, reads=[("pt_in", j)], writes=[("pt_out", j)])
        S.dma("sp", phalo[:, :, :], pt_out[j][0:128, :].rearrange("p (g n) -> p g n", g=4), reads=[("pt_out", j)], writes=[skey("phalo_raw")])
        S.op("dve", lambda e: e.tensor_scalar(out=phalo[:, :, :], in0=phalo[:, :, :], scalar1=halo_on, scalar2=None, op0=ALU.mult), reads=["phalo_raw", "cb"], writes=["phalo"])

        AB_STOP = int(_os.environ.get("K_AB_STOP", "9"))
        used = [1]

        def bail():
            for _ in range(10 - used[0]):
                next_chunk()
                chunk_done()
        if AB_STOP <= 1:
            return bail()
        ropeG = SC.f32(NT * 64).rearrange("p (t n) -> p t n", t=NT)
        tmp = SC.f32(1024).rearrange("p (a n) -> p a n", a=4)
        rot = SC.f32(1024).rearrange("p (a n) -> p a n", a=2)
        rot_bf2 = SC.bf16(1024).rearrange("p (a n) -> p a n", a=2)
        vf = SC.f32(512).rearrange("p (a n) -> p a n", a=2)
        rec = SC.bf16(2 * 768).rearrange("p (a n) -> p a n", a=2)
        ai = 0
        pA = 0
        pend = [None]
        for g in (2, 1, 0):
            d = DILS[g]
            rows_g = min(WINS[g], NP)
            S.dma("sp", ropeG[:, :, :], rope_d[g].rearrange("t p n -> p t n"), writes=[skey("ropeG")])
            for hh in range(2):
                pid = g * 2 + hh
                s = next_chunk()
                for jt in range(NT):
                    if jt < 16:
                        d_, r, n, start = tile_geom(g, jt)
                        tok = lambda k, start=start, d=d: hT[:, k, start:start + d * 127 + 1:d]
                        hk = ALL_HT[:16]
                    else:
                        tok = lambda k: hT[:, k, NP:T]
                        hk = [("hT", 16)]
                    b = ai % 2
                    ai += 1
                    bA = 4 + 2 * (pA % 2)
                    pA += 1

                    def f_mm(e, s=s, bA=bA, tok=tok):
                        for part in range(3):
                            for k in range(8):
                                o = ps[:, bA + part // 2, (part % 2) * 256:(part % 2) * 256 + 256]
                                ins = e.matmul(o, lhsT=tok(k), rhs=wring[:, s, part * 2048 + k * 256:part * 2048 + (k + 1) * 256], start=(k == 0), stop=(k == 7))
                        return ins
                    S.op("pe", f_mm, reads=[("w", s)] + hk, writes=[("ps", bA), ("ps", bA + 1)])
                    qk4 = ps[:, bA, :].rearrange("p (a h f) -> p a h f", a=8, h=2)
                    cosb = ropeG[:, jt, 0:32].unsqueeze(1).broadcast_to([128, 8, 32])
                    sinb = ropeG[:, jt, 32:64].unsqueeze(1).broadcast_to([128, 8, 32])
                    t4 = lambda i: tmp[:, i, :].rearrange("p (a f) -> p a f", a=8)
                    for i, (hsel, tb) in enumerate(((0, cosb), (1, sinb), (1, cosb), (0, sinb))):
                        S.op("dve", lambda e, i=i, hsel=hsel, tb=tb, qk4=qk4: e.tensor_tensor(out=t4(i), in0=qk4[:, :, hsel, :], in1=tb, op=ALU.mult),
                             reads=[("ps", bA), "ropeG"], writes=[("a_tmp", i)])
                    rot4 = rot[:, b, :].rearrange("p (a h f) -> p a h f", a=8, h=2)
                    S.op("dve", lambda e, rot4=rot4: e.tensor_tensor(out=rot4[:, :, 0, :], in0=t4(0), in1=t4(1), op=ALU.subtract),
                         reads=[("a_tmp", 0), ("a_tmp", 1)], writes=[skey(("a_rot0", b))])
                    S.op("dve", lambda e, rot4=rot4: e.tensor_tensor(out=rot4[:, :, 1, :], in0=t4(2), in1=t4(3), op=ALU.add),
                         reads=[("a_tmp", 2), ("a_tmp", 3)], writes=[skey(("a_rot1", b))])
                    S.op("act", lambda e, b=b: e.copy(out=rot_bf2[:, b, :], in_=rot[:, b, :]), reads=[("a_rot0", b), ("a_rot1", b)], writes=[("a_rotbf", b)])
                    S.op("act", lambda e, b=b, bA=bA: e.copy(out=vf[:, b, :], in_=ps[:, bA + 1, 0:256]), reads=[("ps", bA + 1)], writes=[skey(("a_vf", b))])
                    S.op("act", lambda e, b=b: e.copy(out=rec[:, b, 512:768], in_=vf[:, b, :]), reads=[("a_vf", b)], writes=[skey(("a_recv", b))])
                    kk = rot[:, b, 256:512].rearrange("p (h f) -> p h f", h=4)
                    vv = vf[:, b, :].rearrange("p (h f) -> p h f", h=4)
                    if jt == 16:
                        okk, okv = ("kvsK", j, g, hh), ("kvsV", j, g, hh)
                        S.dma("sp", kvs_d[g][j, :, 0, hh * 4:hh * 4 + 4, :], kk, reads=[("a_rot0", b), ("a_rot1", b)], writes=[okk])
                        S.dma("sp", kvs_d[g][j, :, 1, hh * 4:hh * 4 + 4, :], vv, reads=[("a_vf", b)], writes=[okv])
                        out_keys.extend([okk, okv])
                    elif start + d * 127 >= NP - rows_g and n == 16 // d - 1:
                        r0 = start - (NP - rows_g)
                        okk, okv = ("kvrK", j, g, hh, jt), ("kvrV", j, g, hh, jt)
                        S.dma("sp", kvrow_d[g][j, r0:r0 + d * 127 + 1:d, 0, hh * 4:hh * 4 + 4, :], kk, reads=[("a_rot0", b), ("a_rot1", b)], writes=[okk])
                        S.dma("sp", kvrow_d[g][j, r0:r0 + d * 127 + 1:d, 1, hh * 4:hh * 4 + 4, :], vv, reads=[("a_vf", b)], writes=[okv])
                        out_keys.extend([okk, okv])
                    bT = pA % 2
                    is_halo = jt < 16 and n == 16 // d - 1
                    hi = r if jt < 16 else 0

                    def back(b=b, bT=bT, jt=jt, pid=pid, g=g, hh=hh, is_halo=is_halo, hi=hi):
                        def f_tr(e):
                            for c4 in range(4):
                                ins = e.transpose(out=psb(bT)[:, c4 * 128:(c4 + 1) * 128], in_=rot_bf2[:, b, c4 * 128:(c4 + 1) * 128], identity=ident_bf[:])
                            return ins
                        S.op("pe", f_tr, reads=[("a_rotbf", b), "ident_bf"], writes=[("ps", bT)])
                        S.op("act", lambda e: e.copy(out=rec[:, b, 0:512], in_=psb(bT)[:, 0:512]), reads=[("ps", bT)], writes=[skey(("a_recqk", b))])
                        rrow = (pid * NT + jt) * 128
                        S.dma("sp", recs[j][rrow:rrow + 128, :], rec[:, b, :], reads=[("a_recqk", b), ("a_recv", b)], writes=[("rec", j, pid, jt)])
                        if is_halo:
                            erow = hi * 128
                            S.dma("sp", ex_in[j][g][hh][erow:erow + 128, :], rec[:, b, 256:768].bitcast(F32), reads=[("a_recqk", b), ("a_recv", b)], writes=[("ex_in", j, g, hh, hi)])
                    if pend[0] is not None:
                        pend[0]()
                    pend[0] = back
                pend[0]()
                pend[0] = None
                chunk_done()
                S.collective(ex_in[j][g][hh], ex_out[j][g][hh], groups, reads=[("ex_in", j, g, hh, hi_) for hi_ in range(NHT[g])], writes=[("ex_out", j, g, hh)])

        used[0] = 7
        if AB_STOP <= 2:
            return bail()
        phase_barrier()
        qk_c = SC.bf16(3 * 512).rearrange("p (a n) -> p a n", a=3)
        v_c = SC.bf16(3 * 256).rearrange("p (a n) -> p a n", a=3)
        k_p = SC.bf16(3 * 256).rearrange("p (a n) -> p a n", a=3)
        v_p = SC.bf16(3 * 256).rearrange("p (a n) -> p a n", a=3)
        pT = SC.bf16(2 * 1024).rearrange("p (a n) -> p a n", a=2)
        kc = SC.bf16(2 * 1024).rearrange("p (a b n) -> p a b n", a=2, b=4)
        vc = SC.bf16(2 * 1024).rearrange("p (a b n) -> p a b n", a=2, b=4)
        kTs = SC.bf16(2048).rearrange("p (a b c n) -> p a b c n", a=2, b=4, c=2)
        pTc = SC.bf16(2 * 128).rearrange("p (a n) -> p a n", a=2)
        bi = 0
        ci = 0
        pendB = [None]
        for g in (0, 1, 2):
            d = DILS[g]
            nblk = (1, 4, 8)[g]
            for hh in range(2):
                pid = g * 2 + hh
                order = [jt for jt in range(16) if jt % (16 // d) != 0] + [jt for jt in range(16) if jt % (16 // d) == 0] + [16]
                if _os.environ.get("K_PB_TILES") is not None:
                    order = [int(v) for v in _os.environ["K_PB_TILES"].split(",") if v != ""]
                for jt in order:
                    b = bi % 2
                    b3 = bi % 3
                    bi += 1
                    rrow = (pid * NT + jt) * 128
                    S.dma("sp", qk_c[:, b3, :], recs[j][rrow:rrow + 128, 0:512], reads=[("rec", j, pid, jt)], writes=[skey(("b_qk", b3))])
                    S.dma("sp", v_c[:, b3, :], recs[j][rrow:rrow + 128, 512:768], reads=[("rec", j, pid, jt)], writes=[skey(("b_v", b3))])
                    blocks = [(qk_c[:, b3, 256:512], v_c[:, b3, :], [("b_qk", b3), ("b_v", b3)])]
                    if jt < 16:
                        d_, r, n, start = tile_geom(g, jt)
                        if n > 0:
                            prow = (pid * NT + jt - 1) * 128
                            S.dma("sp", k_p[:, b3, :], recs[j][prow:prow + 128, 256:512], reads=[("rec", j, pid, jt - 1)], writes=[skey(("b_kp", b3))])
                            S.dma("sp", v_p[:, b3, :], recs[j][prow:prow + 128, 512:768], reads=[("rec", j, pid, jt - 1)], writes=[skey(("b_vp", b3))])
                        else:
                            erow = r * 128
                            S.dma("sp", k_p[:, b3, :].bitcast(F32), ex_out[j][g][hh][erow:erow + 128, 0:128], reads=[("ex_out", j, g, hh)], writes=[skey(("b_kp", b3))])
                            S.dma("sp", v_p[:, b3, :].bitcast(F32), ex_out[j][g][hh][erow:erow + 128, 128:256], reads=[("ex_out", j, g, hh)], writes=[skey(("b_vp", b3))])
                        blocks.append((k_p[:, b3, :], v_p[:, b3, :], [("b_kp", b3), ("b_vp", b3)]))
                        mrow = mrow_0 if n == 0 else mrow_n
                    nb = len(blocks)
                    bkeys = [k for blk in blocks for k in blk[2]]

                    samp = (jt == 16)
                    if samp:
                        pidx = lambda h4, cp: (h4 % 2) * 512 + (h4 // 2) * 128
                    else:
                        pidx = lambda h4, cp: (h4 % 2) * 512 + (h4 // 2) * 256 + cp * 128

                    sbk = 0 if b == 0 else 6

                    def f_sc(e, b3=b3, blocks=blocks, nb=nb, g=g, samp=samp, pidx=pidx, sbk=sbk, mrow=(mrow if jt < 16 else None)):
                        for bank in range(2):
                            if samp:
                                e.matmul(ps[:, sbk + bank, 0:256], lhsT=ident_bf[:], rhs=smask[g], start=True, stop=False)
                            else:
                                e.matmul(ps[:, sbk + bank, :], lhsT=ident_bf[:], rhs=mrow, start=True, stop=False)
                        for h4 in (0, 2, 1, 3):
                            pb = (h4 % 2) * 64
                            for cp in range(nb):
                                c0 = pidx(h4, cp) % 512
                                kT = blocks[cp][0]
                                ins = e.matmul(ps[:, sbk + h4 % 2, c0:c0 + 128], lhsT=kT[pb:pb + 64, (h4 // 2) * 128:(h4 // 2) * 128 + 128], rhs=qk_c[pb:pb + 64, b3, (h4 // 2) * 128:(h4 // 2) * 128 + 128], start=False, stop=True)
                        return ins
                    S.op("pe", f_sc, reads=bkeys + ["ca", "ident_bf"], writes=[("ps", sbk), ("ps", sbk + 1)])
                    if not samp:
                        S.op("act", lambda e, b=b, sbk=sbk: e.activation(out=pT[:, b, :], in_=ps[:, sbk:sbk + 2, :].rearrange("p a n -> p (a n)"), func=AF.Exp, scale=0.125),
                             reads=[("ps", sbk), ("ps", sbk + 1)], writes=[("b_pT", b)])
                    else:
                        S.op("act", lambda e, b=b, sbk=sbk: e.activation(out=pT[:, b, :].rearrange("p (a n) -> p a n", a=2)[:, :, 0:256], in_=ps[:, sbk:sbk + 2, 0:256], func=AF.Exp, scale=0.125),
                             reads=[("ps", sbk), ("ps", sbk + 1)], writes=[("b_pT", b)])

                    def f_pv(e, b=b, blocks=blocks, nb=nb, hh=hh, pidx=pidx):
                        for hs in range(2):
                            for cc in range(2):
                                h4 = 2 * cc + hs
                                o = ps[:, 2, (hs * 2 + cc) * 128:(hs * 2 + cc + 1) * 128]
                                for cp in range(nb):
                                    V = blocks[cp][1]
                                    ins = e.matmul(o, lhsT=V[:, cc * 128:(cc + 1) * 128], rhs=pT[:, b, pidx(h4, cp):pidx(h4, cp) + 128], start=(cp == 0), stop=(cp == nb - 1))
                        first = True
                        for h4 in range(4):
                            for cp in range(nb):
                                ins = e.matmul(ps[0:64, 3, 0:128], lhsT=oh[:, hh * 4 + h4, :], rhs=pT[:, b, pidx(h4, cp):pidx(h4, cp) + 128], start=first, stop=(h4 == 3 and cp == nb - 1))
                                first = False
                        return ins
                    pv_reads = [("b_pT", b)] + bkeys + ["ca"]
                    if jt < 16:
                        cols = slice(start, start + d * 127 + 1, d)
                    else:
                        cols = slice(NP, T)
                    oa0 = O_acc[0:64, hh * 2:hh * 2 + 2, cols]
                    oa1 = O_acc[64:128, hh * 2:hh * 2 + 2, cols]

                    def accum(oa0=oa0, oa1=oa1, cols=cols):
                        S.op("dve", lambda e, oa0=oa0: e.tensor_tensor(out=oa0, in0=ps[0:64, 2, 0:256].rearrange("p (c n) -> p c n", c=2), in1=oa0, op=ALU.add),
                             reads=[("ps", 2), "Oacc"], writes=["Oacc"])
                        S.op("dve", lambda e, oa1=oa1: e.tensor_tensor(out=oa1, in0=ps[64:128, 2, 256:512].rearrange("p (c n) -> p c n", c=2), in1=oa1, op=ALU.add),
                             reads=[("ps", 2), "Oacc"], writes=["Oacc"])
                        S.op("dve", lambda e, cols=cols: e.tensor_tensor(out=den_acc[:, cols], in0=ps[0:64, 3, 0:128], in1=den_acc[:, cols], op=ALU.add),
                             reads=[("ps", 3), "den"], writes=["den"])
                    def fin(f_pv=f_pv, pv_reads=pv_reads, accum=accum):
                        S.op("pe", f_pv, reads=pv_reads, writes=[("ps", 2), ("ps", 3)])
                        accum()
                    if pendB[0] is not None:
                        pendB[0]()
                        pendB[0] = None
                    if samp:
                        fin()
                    else:
                        pendB[0] = fin
                    if samp and not _os.environ.get("K_NOCACHE"):
                        def f_z(e):
                            e.matmul(ps[:, 2, :], lhsT=zlhs, rhs=mrow_n, start=True, stop=False)
                            return e.matmul(ps[0:64, 3, 0:128], lhsT=zlhs[:, 0:64], rhs=ident_bf[:], start=True, stop=False)
                        S.op("pe", f_z, reads=["ca", "ident_bf"], writes=[("ps", 2), ("ps", 3)])
                    if jt == 16 and not _os.environ.get("K_NOCACHE"):
                        stp = [(0, nblk)] if nblk <= 4 else [(0, 4), (4, 8)]
                        steps = [(sb_, b0, b1) for sb_ in range(16) for (b0, b1) in stp]
                        NS = len(steps)

                        def cw_view(i):
                            sb_, b0, b1 = steps[i]
                            return cw_d[g][j, sb_].rearrange("(p r) kv h f -> p r kv h f", r=d), b0, b1

                        def loadK(i):
                            if i >= NS:
                                return
                            cwv, b0, b1 = cw_view(i)
                            S.dma("pool", kc[:, i % 2, 0:b1 - b0, :].rearrange("p r (h f) -> p r h f", h=4), cwv[:, b0:b1, 0, hh * 4:hh * 4 + 4, :], writes=[skey(("c_kc", i % 2))])

                        def loadV(i):
                            if i >= NS:
                                return
                            cwv, b0, b1 = cw_view(i)
                            S.dma("pool", vc[:, i % 2, 0:b1 - b0, :].rearrange("p r (h f) -> p r h f", h=4), cwv[:, b0:b1, 1, hh * 4:hh * 4 + 4, :], writes=[skey(("c_vc", i % 2))])

                        def stA(i):
                            if i >= NS:
                                return
                            sb_, b0, b1 = steps[i]
                            nbk = b1 - b0
                            c2 = i % 2

                            def f_ct(e):
                                for bl in range(nbk):
                                    for cc in range(2):
                                        ins = e.transpose(out=psb(4)[:, (bl * 2 + cc) * 128:(bl * 2 + cc + 1) * 128], in_=kc[:, c2, bl, cc * 128:(cc + 1) * 128], identity=ident_bf[:])
                                return ins
                            S.op("pe", f_ct, reads=[("c_kc", c2), "ident_bf"], writes=[("ps", 4)])
                            S.op("act", lambda e: e.copy(out=kTs[:, c2, 0:nbk, :, :], in_=psb(4)[:, 0:nbk * 256].rearrange("p (b c n) -> p b c n", b=nbk, c=2)), reads=[("ps", 4)], writes=[("c_kT", c2)])

                        def stB(i):
                            sb_, b0, b1 = steps[i]
                            nbk = b1 - b0
                            c2 = i % 2

                            def f_cs(e, g=g, b3=b3):
                                cm = cmask[g][:, b0 * 32:(b0 + nbk) * 32].rearrange("p (r h t) -> p r h t", r=nbk, h=4)[:, :, 0:2, :]
                                for par in range(2):
                                    e.matmul(ps[:, 5 + par, 0:nbk * 16].rearrange("p (r h t) -> p r h t", r=nbk, h=2), lhsT=ident_bf[:], rhs=cm, start=True, stop=False)
                                for par in range(2):
                                    pb = par * 64
                                    for bl in range(nbk):
                                        for hp in range(2):
                                            ins = e.matmul(ps[:, 5 + par, (bl * 2 + hp) * 8:(bl * 2 + hp + 1) * 8], lhsT=kTs[pb:pb + 64, c2, bl, hp, :],
                                                           rhs=qk_c[pb:pb + 64, b3, hp * 128 + sb_ * 8:hp * 128 + sb_ * 8 + 8], start=False, stop=True)
                                return ins
                            S.op("pe", f_cs, reads=[("c_kT", c2), ("b_qk", b3), "ca", "ident_bf"], writes=[("ps", 5), ("ps", 6)])
                            for par in range(2):
                                S.op("act", lambda e, par=par: e.activation(out=pTc[:, c2, par * 64:par * 64 + nbk * 16], in_=ps[:, 5 + par, 0:nbk * 16], func=AF.Exp, scale=0.125),
                                     reads=[("ps", 5 + par)], writes=[("c_pT", c2, par)])

                        def stC(i):
                            if i < 0:
                                return
                            sb_, b0, b1 = steps[i]
                            nbk = b1 - b0
                            c2 = i % 2

                            def f_cpv(e, hh=hh):
                                for hs in range(2):
                                    for cc in range(2):
                                        for bl in range(nbk):
                                            ins = e.matmul(ps[:, 2, (hs * 2 + cc) * 128 + sb_ * 8:(hs * 2 + cc) * 128 + sb_ * 8 + 8], lhsT=vc[:, c2, bl, cc * 128:(cc + 1) * 128],
                                                           rhs=pTc[:, c2, hs * 64 + (bl * 2 + cc) * 8:hs * 64 + (bl * 2 + cc + 1) * 8], start=False, stop=True)
                                for bl in range(nbk):
                                    for h4 in range(4):
                                        ins = e.matmul(ps[0:64, 3, sb_ * 8:sb_ * 8 + 8], lhsT=oh[:, hh * 4 + h4, :], rhs=pTc[:, c2, (h4 % 2) * 64 + (bl * 2 + h4 // 2) * 8:(h4 % 2) * 64 + (bl * 2 + h4 // 2 + 1) * 8], start=False, stop=True)
                                return ins
                            S.op("pe", f_cpv, reads=[("c_pT", c2, 0), ("c_pT", c2, 1), ("c_vc", c2), "ca"], writes=[("ps", 2), ("ps", 3)])

                        loadK(0)
                        loadK(1)
                        loadV(0)
                        stA(0)
                        for i in range(NS):
                            loadK(i + 2)
                            stA(i + 1)
                            stB(i)
                            stC(i - 1)
                            loadV(i + 1)
                        stC(NS - 1)
                        accum()

        if AB_STOP <= 3:
            return bail()
        phase_barrier()
        apT = hT
        Ab = SC.f32(2 * 527).rearrange("p (a n) -> p a n", a=2)
        ptl = SC.f32(60).rearrange("p (g n) -> p g n", g=4)
        dTb = SC.bf16(2 * 512).rearrange("p (a n) -> p a n", a=2)
        uoff = SC.off
        As = SC.f32(2 * 368).rearrange("p (a b n) -> p a b n", a=2, b=16)
        SCR_S0 = SC.f32(368).rearrange("p (b n) -> p b n", b=16)
        SCR_S1 = SC.f32(368).rearrange("p (b n) -> p b n", b=16)
        utok = scr[:, uoff:uoff + 1024].rearrange("p (a n) -> p a n", a=2)
        UTK = [("utok", 0), ("utok", 1)]
        sprevT = SC.f32(4 * 240).rearrange("p (g b n) -> p g b n", g=4, b=16)
        rowb = SC.f32(512)
        SCR_T0 = SC.f32(527)
        SCR_T1 = SC.f32(527)
        rden = SCR_T0[:, 0:512]
        rc_tmp = SC.f32(16)
        PS0 = 3 * DEPTH + j
        s = next_chunk()
        wpool = lambda k, a, b_: wring[:, s, k * 512 + a:k * 512 + b_]
        pw = wring[:, s, 4096:4608].rearrange("p (g d) -> p g d", g=4)
        for ti, t in enumerate((15, 16)):
            def f_mm(e, t=t, ti=ti):
                for k in range(8):
                    ins = e.matmul(ps[:, 6 + ti, :], lhsT=hT[:, k, t * 128:(t + 1) * 128], rhs=wpool(k, 0, 512), start=(k == 0), stop=(k == 7))
                return ins
            S.op("pe", f_mm, reads=[("w", s), ("hT", t)], writes=[("ps", 6 + ti)])
            S.op("act", lambda e, ti=ti: e.copy(out=utok[:, ti, :], in_=ps[:, 6 + ti, :]), reads=[("ps", 6 + ti)], writes=[skey(("utok", ti))])
        S.dma("sp", poolp_d[j], utok[113:128, 0, :], reads=[("utok", 0)], writes=[("poolp", j)])
        S.dma("sp", pools_d[j, :, 7:15, :], utok[:, 1, :], reads=[("utok", 1)], writes=[("pools_n", j)])
        S.dma("sp", pools_d[j, :, 0:7, :], spool_d[j, :, 8:15, :], writes=[("pools_o", j)])
        out_keys.extend([("poolp", j), ("pools_n", j), ("pools_o", j)])
        for half in range(2):
            S.dma("sp", rowb[0:120, :], spool_d[j, half * 8:(half + 1) * 8].rearrange("b n c -> (b n) c"), writes=[skey("rowb")])

            def f_tr(e):
                for gi in range(4):
                    ins = e.transpose(out=ps[:, 4, gi * 120:(gi + 1) * 120], in_=rowb[0:120, gi * 128:(gi + 1) * 128], identity=ident_f[0:120, 0:120])
                return ins
            S.op("pe", f_tr, reads=["rowb", "ident_f"], writes=[("ps", 4)])
            S.op("act", lambda e, half=half: e.copy(out=sprevT[:, :, half * 8:(half + 1) * 8, :], in_=ps[:, 4, 0:480].rearrange("p (g b n) -> p g b n", g=4, b=8)), reads=[("ps", 4)], writes=[("p_sprev", half)])
        ai = 0
        for tg in range(5):
            t0, tn = TG[tg]
            hkeys = [("hT", t) for t in range(t0 // 128, (t0 + tn) // 128)]
            for gi in range(4):
                def f_mm(e, gi=gi, t0=t0, tn=tn):
                    for k in range(8):
                        ins = e.matmul(ps[:, gi, 0:tn], lhsT=wpool(k, gi * 128, (gi + 1) * 128), rhs=hT[:, k, t0:t0 + tn], start=(k == 0), stop=(k == 7))
                    return ins
                S.op("pe", f_mm, reads=[("w", s)] + hkeys, writes=[("ps", gi)])
            for gi in range(4):
                w = 2 << gi
                a = ai % 2
                ai += 1
                if tg < 4:
                    A0, A1 = Ab[:, a, :], Ab[:, 1 - a, :]
                    S.op("act", lambda e, A0=A0, gi=gi: e.copy(out=A0[:, 15:527], in_=ps[:, gi, :]), reads=[("ps", gi)], writes=[("p_A", a)])
                    src = phalo[:, gi, :] if tg == 0 else ptl[:, gi, :]
                    S.op("pool", lambda e, A0=A0, src=src: e.tensor_copy(out=A0[:, 0:15], in_=src), reads=["phalo", ("p_tl", gi)], writes=[("p_Ah", a)])
                    S.op("pool", lambda e, A0=A0, gi=gi: e.tensor_copy(out=ptl[:, gi, :], in_=A0[:, 512:527]), reads=[("p_A", a), ("p_Ah", a)], writes=[("p_tl", gi)])
                    cur = A0
                    srcs = [("p_A", a), ("p_Ah", a)]
                    stepk = 1
                    tbuf = [SCR_T0, SCR_T1]
                    ti_ = 0
                    while stepk < w:
                        nxt = tbuf[ti_ % 2]
                        ti_ += 1
                        S.op("dve", lambda e, cur=cur, nxt=nxt, stepk=stepk: e.tensor_tensor(out=nxt[:, stepk:527], in0=cur[:, stepk:527], in1=cur[:, 0:527 - stepk], op=ALU.add),
                             reads=srcs, writes=[("p_T", ti_ % 2)])
                        srcs = [("p_T", ti_ % 2)]
                        cur = nxt
                        stepk *= 2
                    db = ai % 2
                    S.op("dve", lambda e, cur=cur, A0=A0, db=db, w=w: e.scalar_tensor_tensor(out=dTb[:, db, :], in0=cur[:, 15:527], scalar=1.0 / w, in1=A0[:, 15:527], op0=ALU.mult, op1=ALU.subtract),
                         reads=srcs + [("p_A", a)], writes=[("p_d", db)])
                    if tg == 0:
                        S.op("dve", lambda e, cur=cur, gi=gi: e.tensor_tensor(out=rc_tmp[:, :], in0=cur[:, 15:31], in1=cb[:, gi * 16:(gi + 1) * 16], op=ALU.mult), reads=srcs + ["cb"], writes=["p_rc"])
                        S.op("dve", lambda e, A0=A0, db=db: e.tensor_tensor(out=dTb[:, db, 0:16], in0=rc_tmp[:, :], in1=A0[:, 15:31], op=ALU.subtract), reads=["p_rc", ("p_A", a), ("p_d", db)], writes=[("p_d", db)])
                    rhs_d = dTb[:, db, :]
                else:
                    A0 = As[:, a, :, :]
                    S.op("act", lambda e, A0=A0, gi=gi: e.copy(out=A0[:, :, 15:23], in_=ps[:, gi, 0:128].rearrange("p (b n) -> p b n", b=16)), reads=[("ps", gi)], writes=[("p_As", a)] + UTK)
                    S.op("pool", lambda e, A0=A0, gi=gi: e.tensor_copy(out=A0[:, :, 0:15], in_=sprevT[:, gi, :, :]), reads=[("p_sprev", 0), ("p_sprev", 1)], writes=[("p_Ash", a)])
                    cur = A0
                    srcs = [("p_As", a), ("p_Ash", a)]
                    stepk = 1
                    tbuf = [SCR_S0, SCR_S1]
                    ti_ = 0
                    while stepk < w:
                        nxt = tbuf[ti_ % 2]
                        ti_ += 1
                        S.op("dve", lambda e, cur=cur, nxt=nxt, stepk=stepk: e.tensor_tensor(out=nxt[:, :, stepk:23], in0=cur[:, :, stepk:23], in1=cur[:, :, 0:23 - stepk], op=ALU.add),
                             reads=srcs, writes=[("p_TS", ti_ % 2)] + UTK)
                        srcs = [("p_TS", ti_ % 2)]
                        cur = nxt
                        stepk *= 2
                    db = ai % 2
                    S.op("dve", lambda e, cur=cur, A0=A0, db=db, w=w: e.scalar_tensor_tensor(out=dTb[:, db, 0:128].rearrange("p (b n) -> p b n", b=16), in0=cur[:, :, 15:23], scalar=1.0 / w, in1=A0[:, :, 15:23], op0=ALU.mult, op1=ALU.subtract),
                         reads=srcs + [("p_As", a)], writes=[("p_d", db)])
                    rhs_d = dTb[:, db, 0:128]
                ob = 6 + (ai % 2)
                S.op("pe", lambda e, gi=gi, rhs_d=rhs_d, ob=ob, tn=tn: e.matmul(ps[:, ob, 0:tn], lhsT=pw[:, gi, :], rhs=rhs_d, start=True, stop=True), reads=[("w", s), ("p_d", db)], writes=[("ps", ob)])
                S.op("act", lambda e, gi=gi, ob=ob, t0=t0, tn=tn: e.activation(out=apT[:, 4 + gi, t0:t0 + tn], in_=ps[:, ob, 0:tn], func=AF.Copy, scale=vT[:, gi, PS0:PS0 + 1]),
                     reads=[("ps", ob), "vT"] + [("ps", g_) for g_ in range(4)], writes=[("hT", t) for t in range(t0 // 128, (t0 + tn) // 128)])
        chunk_done()
        used[0] = 8
        if AB_STOP <= 4:
            return bail()
        ni = 0
        for tg in range(5):
            t0, tn = TG[tg]
            akeys = [("hT", t) for t in range(t0 // 128, (t0 + tn) // 128)]
            for c in range(4):
                bk = 4 + (ni % 2)
                ni += 1
                S.op("pe", lambda e, c=c, bk=bk, t0=t0, tn=tn: e.matmul(ps[:, bk, 0:tn], lhsT=Ef[:, c, :], rhs=den_acc[:, t0:t0 + tn], start=True, stop=True), reads=["den", "cb"], writes=[("ps", bk)])
                S.op("dve", lambda e, bk=bk, tn=tn: e.reciprocal(out=rden[:, 0:tn], in_=ps[:, bk, 0:tn]), reads=[("ps", bk)], writes=["p_rden", ("p_T", 0), ("p_T", 1)])
                S.op("dve", lambda e, c=c, t0=t0, tn=tn: e.tensor_tensor(out=apT[:, c, t0:t0 + tn], in0=O_acc[:, c, t0:t0 + tn], in1=rden[:, 0:tn], op=ALU.mult), reads=["p_rden", "Oacc"] + akeys, writes=akeys)
        out_proj(apT, "hT")
    for l in range(DEPTH):
        ffn(l, 0)
        if ENABLE_MIX:
            if l % 2 == 0 and ENABLE_AB:
                ab_mixer(l)
            if l % 2 == 1 and ENABLE_CONV:
                conv_mixer(l)
        ffn(l, 1)

    S.barrier()
    S.dma("sp", gfin, bass.AP(gfin_d.tensor, 0, [[0, 128], [1, D]]), writes=["gfin"])
    for t in range(NT):
        S.op("act", lambda e, t=t: e.activation(out=junk, in_=x_sb[:, t, :], func=AF.Square, accum_out=ss[:, t:t + 1]),
             reads=[("x", t)], writes=[("ss", t), ("ps", 6), ("ps", 7)])
    S.op("act", lambda e: e.activation(out=sd[:], in_=ss[:], func=AF.Sqrt, scale=1.0 / D, bias=eps_t[:]),
         reads=[("ss", t) for t in range(NT)] + ["eps"], writes=["sd"])
    S.op("dve", lambda e: e.reciprocal(out=rstd[:], in_=sd[:]), reads=["sd"], writes=["rstd"])
    for t in range(NT):
        b = t % 2
        S.op("dve", lambda e, t=t, b=b: e.scalar_tensor_tensor(out=ystage[:, b, :], in0=x_sb[:, t, :], scalar=rstd[:, t:t + 1], in1=gfin, op0=ALU.mult, op1=ALU.mult),
             reads=[("x", t), "rstd", "gfin"], writes=[("ys", b)])
        S.dma("sp", y_d[t * 128:(t + 1) * 128, :], ystage[:, b, :], reads=[("ys", b)], writes=[("y", t)])
        out_keys.append(("y", t))

    S.wait_keys("sp", out_keys)
    S.replay()
    return nc


_PROG_CACHE = {}


def _host_consts(half):
    c = np.zeros((128, 4096), np.float32)
    kk = np.arange(128)[:, None]
    qq = np.arange(128)[None, :]
    maskC = np.where(kk <= qq, 0.0, NEG).astype(np.float32)
    maskP = np.where(kk >= qq, 0.0, NEG).astype(np.float32)
    mp0 = maskP if half == 1 else np.full((128, 128), NEG, np.float32)
    c[:, 0:512] = np.concatenate([maskC, maskP, maskC, maskP], 1)
    c[:, 512:1024] = np.concatenate([maskC, mp0, maskC, mp0], 1)
    bk, tk = kk // 8, kk % 8
    bq, tq = qq // 8, qq % 8
    same = bk == bq
    conds = [tk <= tq, (tk <= tq) & ((tq - tk) % 4 == 0), tk == tq]
    for g in range(3):
        sm = np.where(same & conds[g], 0.0, NEG)
        c[:, 1024 + g * 256:1024 + (g + 1) * 256] = np.concatenate([sm, sm], 1)
    p = np.arange(128)[:, None, None, None]
    t = np.arange(8)[None, None, None, :]
    ones4 = np.ones((1, 1, 4, 1), bool)
    c[:, 1792:1824] = np.where((p >= t) & ones4, 0.0, NEG).reshape(128, 32)
    r4 = np.arange(4)[None, :, None, None]
    c[:, 1824:1952] = np.where(((t % 4) == r4) & (r4 + 4 * p >= t) & ones4, 0.0, NEG).reshape(128, 128)
    r8 = np.arange(8)[None, :, None, None]
    c[:, 1952:2208] = np.where((t == r8) & ones4 & (p >= 0), 0.0, NEG).reshape(128, 256)
    oh = np.zeros((128, 8, 64), np.float32)
    for hh in range(2):
        for h4 in range(4):
            oh[:, hh * 4 + h4, hh * 32 + h4] = 1.0
    c[:, 2208:2720] = oh.reshape(128, 512)
    o = 3072
    for gi in range(4):
        w = 2 << gi
        i = np.arange(16)
        c[:, o + gi * 16:o + (gi + 1) * 16] = (1.0 / np.minimum(w, i + 1)) if half == 0 else (1.0 / w)
    c[:, o + 64] = float(half)
    E = np.zeros((64, 4, 128), np.float32)
    for hh in range(2):
        for h4 in range(4):
            h = hh * 4 + h4
            E[hh * 32 + h4, h // 2, (h % 2) * 64:(h % 2) * 64 + 64] = 1.0
    c[0:64, o + 128:o + 640] = E.reshape(64, 512)
    return c


def _host_rope(half):
    inv = np.power(np.float32(10000.0), -np.arange(32, dtype=np.float32) / np.float32(32)).astype(np.float32)
    tab = np.zeros((3, NT, 128, 64), np.float32)
    p = np.arange(128)
    for g in range(3):
        d = DILS[g]
        cls = 16 // d
        for jt in range(NT):
            if jt < 16:
                r, n = jt // cls, jt % cls
                pos = half * NP + r + d * (128 * n + p)
            else:
                pos = 2048 + (p % 8)
            ang = pos.astype(np.float32)[:, None] * inv[None, :]
            tab[g, jt, :, 0:32] = np.cos(ang)
            tab[g, jt, :, 32:64] = np.sin(ang)
    return tab


def kernel(**inputs):
    x_prompt = np.asarray(inputs["x_prompt"], np.float32)
    x_sample = np.asarray(inputs["x_sample"], np.float32)
    B = x_prompt.shape[0]
    DEPTH = inputs["ffn1_norm"].shape[0]
    N_AB = (DEPTH + 1) // 2
    N_C = DEPTH // 2
    n_cores = 2 * B
    DB = x_sample.shape[0]
    assert DB == 16 * n_cores and x_prompt.shape[1] == 2 * NP
    key = (DEPTH, n_cores)
    if key not in _PROG_CACHE:
        _PROG_CACHE[key] = build_program(DEPTH, n_cores)
    nc = _PROG_CACHE[key]

    f32 = lambda a: np.ascontiguousarray(np.asarray(a, np.float32))
    vec_rows = [inputs["ffn1_norm"][l] for l in range(DEPTH)] + [inputs["mix_norm"][l] for l in range(DEPTH)] + \
               [inputs["ffn2_norm"][l] for l in range(DEPTH)]
    for j in range(N_AB):
        vec_rows.append(np.concatenate([np.asarray(inputs["pool_scale"][j]), np.asarray(inputs["pool_scale"][j])]))
    for j in range(N_C):
        for t in range(3):
            vec_rows.append(inputs["conv_w"][j][t])
    vecs = f32(np.stack([np.asarray(v, np.float32) for v in vec_rows]))
    shared = {
        "ffn1_w_gu": f32(inputs["ffn1_w_gu"]), "ffn2_w_gu": f32(inputs["ffn2_w_gu"]),
        "ffn1_w_down": f32(inputs["ffn1_w_down"]), "ffn2_w_down": f32(inputs["ffn2_w_down"]),
        "ab_w_in": f32(inputs["ab_w_in"]), "ab_w_out": f32(inputs["ab_w_out"]), "pool_w": f32(inputs["pool_w"]),
        "conv_w_in": f32(inputs["conv_w_in"]) if N_C else np.zeros((1, D, 3 * D), np.float32),
        "conv_w_out": f32(inputs["conv_w_out"]) if N_C else np.zeros((1, D, D), np.float32),
        "vecs": vecs, "gfin": f32(np.asarray(inputs["final_norm"]).reshape(1, D)),
        "ident": np.eye(128, dtype=np.float32),
    }
    in_maps = []
    for c in range(n_cores):
        b, half = c // 2, c % 2
        m = dict(shared)
        m["x"] = f32(np.concatenate([x_prompt[b, half * NP:(half + 1) * NP], x_sample[c * 16:(c + 1) * 16].reshape(128, D)], 0))
        m["cw0"] = f32(inputs["cache_win0"][:, c * 16:(c + 1) * 16])
        m["cw1"] = f32(inputs["cache_win1"][:, c * 16:(c + 1) * 16])
        m["cw2"] = f32(inputs["cache_win2"][:, c * 16:(c + 1) * 16])
        m["spool"] = f32(inputs["state_pool"][:, c * 16:(c + 1) * 16])
        m["sconv"] = f32(inputs["state_conv"][:, c * 16:(c + 1) * 16]) if N_C else np.zeros((1, 16, 2, D), np.float32)
        m["rope"] = _host_rope(half)
        m["consts"] = _host_consts(half)
        in_maps.append(m)
    res = run_bass_kernel_spmd(nc, in_maps, core_ids=list(range(n_cores)))
    R = res.results
    S_ = x_prompt.shape[1]
    y_prompt = np.zeros((B, S_, D), np.float32)
    y_sample = np.zeros((DB, 8, D), np.float32)
    for c in range(n_cores):
        b, half = c // 2, c % 2
        y_prompt[b, half * NP:(half + 1) * NP] = R[c]["y"][:NP]
        y_sample[c * 16:(c + 1) * 16] = R[c]["y"][NP:].reshape(16, 8, D)
    outs = [y_prompt, y_sample]
    for g in range(3):
        outs.append(np.stack([R[2 * b + 1]["kvrow%d" % g] for b in range(B)], 1))
    outs.append(np.stack([R[2 * b + 1]["poolp"] for b in range(B)], 1))
    outs.append(np.stack([R[2 * b + 1]["convp"][:N_C] for b in range(B)], 1))
    for g in range(3):
        outs.append(np.concatenate([R[c]["kvs%d" % g].reshape(N_AB, 16, 8, 2, 8, 64) for c in range(n_cores)], 1))
    outs.append(np.concatenate([R[c]["pools"] for c in range(n_cores)], 1))
    outs.append(np.concatenate([R[c]["convs"][:N_C] for c in range(n_cores)], 1))
    return tuple(outs)
```

```python
import numpy as np
import concourse.bass as bass
import concourse.mybir as mybir
from concourse.bass_utils import run_bass_kernel_spmd

F32 = mybir.dt.float32
BF16 = mybir.dt.bfloat16
AF = mybir.ActivationFunctionType
ALU = mybir.AluOpType

D = 1024
DFF = 2816
NPT = 16
NT = 17
T = NT * 128
NP = NPT * 128
TG = [(0, 512), (512, 512), (1024, 512), (1536, 512), (2048, 128)]
DILS = (1, 4, 16)
WINS = (128, 512, 2048)
EPS = 1e-6
NEG = -30000.0
SAME_ENGINE_SYNC = True
ENABLE_MIX = True
ENABLE_AB = True
ENABLE_CONV = True
import os as _os
if _os.environ.get("K_NOAB"):
    ENABLE_AB = False
if _os.environ.get("K_NOCONV"):
    ENABLE_CONV = False


class Sched:
    ENGS = ("pe", "act", "dve", "pool", "sp")

    def __init__(self, nc, n_dsem=20):
        self.nc = nc
        self.sem = {e: nc.alloc_semaphore("s_" + e) for e in self.ENGS}
        self.cnt = {e: 0 for e in self.ENGS}
        self.prog = {e: [] for e in self.ENGS}
        self.seen = {e: {} for e in self.ENGS}
        self.writer = {}
        self.readers = {}
        self.dsem = {q: [nc.alloc_semaphore("d_%s%d" % (q, i)) for i in range(n_dsem)] for q in ("sp", "pool")}
        self.dcnt = {q: [0] * n_dsem for q in ("sp", "pool")}
        self.dnext = {"sp": 0, "pool": 0}
        self.csem = []
        self.fence = []

    def _semh(self, sk):
        if isinstance(sk, str):
            return self.sem[sk]
        if sk[0] == "cc":
            return self.csem[sk[1]]
        return self.dsem[sk[0]][sk[1]]

    def _deps(self, eng, reads, writes):
        deps = {}

        def add(t):
            if t is not None and deps.get(t[0], 0) < t[1]:
                deps[t[0]] = t[1]
        for k in reads:
            add(self.writer.get(k))
        for k in writes:
            add(self.writer.get(k))
            for r in self.readers.get(k, ()):
                add(r)
        waits = []
        for sk, v in deps.items():
            if sk == eng and (eng == "pe" or not SAME_ENGINE_SYNC):
                continue
            if self.seen[eng].get(sk, 0) >= v:
                continue
            self.seen[eng][sk] = v
            waits.append((sk, v))
        return waits

    def _commit(self, tok, reads, writes):
        for k in writes:
            self.writer[k] = tok
            self.readers[k] = []
        for k in reads:
            self.readers.setdefault(k, []).append(tok)

    def op(self, eng, fn, reads=(), writes=()):
        waits = self._deps(eng, reads, writes)
        self.cnt[eng] += 1
        tok = (eng, self.cnt[eng])
        self.prog[eng].append((waits, fn, (eng, 1)))
        self._commit(tok, reads, writes)
        return tok

    def dma(self, q, out, in_, reads=(), writes=(), **kw):
        waits = self._deps(q, list(reads) + self.fence, writes)
        i = self.dnext[q]
        self.dnext[q] = (i + 1) % len(self.dsem[q])
        sk = (q, i)
        prev = self.dcnt[q][i]
        if prev > 0 and self.seen[q].get(sk, 0) < prev:
            self.seen[q][sk] = prev
            waits.append((sk, prev))
        self.dcnt[q][i] = prev + 16
        tok = (sk, prev + 16)
        self.prog[q].append((waits, lambda e: e.dma_start(out=out, in_=in_, **kw), (sk, 16)))
        self._commit(tok, reads, writes)
        return tok

    def collective(self, ins, outs, groups, reads=(), writes=()):
        waits = self._deps("pool", reads, writes)
        self.csem.append(self.nc.alloc_semaphore("cc%d" % len(self.csem)))
        sk = ("cc", len(self.csem) - 1)
        tok = (sk, 1)
        self.prog["pool"].append((waits, lambda e: e.collective_compute(
            "AllGather", ALU.bypass, replica_groups=groups, ins=[ins], outs=[outs]), (sk, 1)))
        self._commit(tok, reads, writes)
        return tok

    def barrier(self):
        engs = ("pe", "act", "dve")
        keys = [("bar", e) for e in ("pe", "act", "dve", "pool")]
        for e in ("pe", "act", "dve", "pool"):
            self.writer[("bar", e)] = (e, self.cnt[e]) if self.cnt[e] > 0 else None
        for e in engs:
            waits = self._deps(e, keys, ())
            self.prog[e].append((waits, None, None))
        self.fence = [("bar", e) for e in ("pe", "act", "dve")]

    def wait_keys(self, eng, keys):
        waits = self._deps(eng, keys, ())
        self.prog[eng].append((waits, None, None))

    def replay(self):
        nc = self.nc
        with nc.Block() as block:
            def mk(name):
                def run(e):
                    for waits, fn, inc in self.prog[name]:
                        for sk, v in waits:
                            e.wait_ge(self._semh(sk), v)
                        if fn is not None:
                            ins = fn(e)
                            ins.then_inc(self._semh(inc[0]), inc[1])
                return run
            block.tensor(mk("pe"))
            block.scalar(mk("act"))
            block.vector(mk("dve"))
            block.gpsimd(mk("pool"))
            block.sync(mk("sp"))


def build_program(DEPTH, n_cores):
    N_AB = (DEPTH + 1) // 2
    N_C = DEPTH // 2
    nc = bass.Bass("TRN2", target_bir_lowering=False)
    S = Sched(nc)
    groups = [[2 * i, 2 * i + 1] for i in range(n_cores // 2)]

    def din(name, shape, dt=F32):
        return nc.dram_tensor(name, list(shape), dt, kind="ExternalInput").ap()

    def dout(name, shape):
        return nc.dram_tensor(name, list(shape), F32, kind="ExternalOutput").ap()

    def dint(name, shape, dt):
        return nc.dram_tensor(name, list(shape), dt).ap()

    def sb(name, shape, dt):
        return nc.alloc_sbuf_tensor(name, list(shape), dt)

    x_d = din("x", [T, D])
    cw_d = [din("cw%d" % g, [N_AB, 16, WINS[g], 2, 8, 64]) for g in range(3)]
    spool_d = din("spool", [N_AB, 16, 15, 512])
    sconv_d = din("sconv", [max(N_C, 1), 16, 2, D])
    wgu_d = [din("ffn1_w_gu", [DEPTH, D, 2 * DFF]), din("ffn2_w_gu", [DEPTH, D, 2 * DFF])]
    wdn_d = [din("ffn1_w_down", [DEPTH, DFF, D]), din("ffn2_w_down", [DEPTH, DFF, D])]
    abin_d = din("ab_w_in", [N_AB, D, 5120])
    about_d = din("ab_w_out", [N_AB, D, D])
    poolw_d = din("pool_w", [N_AB, 4, 128, 128])
    cvin_d = din("conv_w_in", [max(N_C, 1), D, 3 * D])
    cvout_d = din("conv_w_out", [max(N_C, 1), D, D])
    NV = 3 * DEPTH + N_AB + 3 * N_C
    vecs_d = din("vecs", [NV, D])
    gfin_d = din("gfin", [1, D])
    ident_d = din("ident", [128, 128])
    rope_d = din("rope", [3, NT, 128, 64])
    consts_d = din("consts", [128, 4096])

    y_d = dout("y", [T, D])
    kvrow_d = [dout("kvrow%d" % g, [N_AB, min(WINS[g], NP), 2, 8, 64]) for g in range(3)]
    kvs_d = [dout("kvs%d" % g, [N_AB, 128, 2, 8, 64]) for g in range(3)]
    poolp_d = dout("poolp", [N_AB, 15, 512])
    pools_d = dout("pools", [N_AB, 16, 15, 512])
    convp_d = dout("convp", [max(N_C, 1), 2, D])
    convs_d = dout("convs", [max(N_C, 1), 16, 2, D])
    out_keys = []

    x_sb = sb("x_sb", [128, NT, D], F32)
    hT = sb("hT", [128, 8, T], BF16)
    NSLOT = 2
    wring = sb("wring", [128, NSLOT, 6144], BF16)
    big = sb("big", [128, 8704], F32)
    ident_bf = sb("ident_bf", [128, 128], BF16)
    ident_f = sb("ident_f", [128, 128], F32)
    vT = sb("vT", [128, 8, NV], F32)
    vrows = big[0:32, 0:D]
    ss = sb("ss", [128, NT], F32)
    sd = sb("sd", [128, NT], F32)
    rstd = sb("rstd", [128, NT], F32)
    eps_t = sb("eps_t", [128, 1], F32)
    xn = sb("xn", [128, 2, D], BF16)
    junk = xn[:, 0, :]
    gfin = big[:, 0:D]
    ystage = big[:, D:3 * D].rearrange("p (b n) -> p b n", b=2)
    scr = sb("scr", [128, 6144], F32)
    den_acc = sb("den_acc", [64, T], F32)
    CA = 2848
    ca = sb("ca", [128, CA], BF16)
    cb = sb("cb", [128, 640], F32)
    ps = nc.alloc_psum_tensor("ps", [128, 8, 512], F32)

    def psb(bank):
        return ps[:, bank, :].bitcast(BF16)

    chunks = []

    def slotv(s, a, b):
        return wring[:, s, a:b]

    def kview(ap2d):
        return ap2d.rearrange("(k p) n -> p k n", p=128)

    def add_ffn_chunks(l, f):
        for c in range(11):
            chunks.append([
                (lambda s: slotv(s, 0, 2048).rearrange("p (k n) -> p k n", k=8), kview(wgu_d[f][l])[:, :, c * 256:(c + 1) * 256]),
                (lambda s: slotv(s, 2048, 4096).rearrange("p (k n) -> p k n", k=8), kview(wgu_d[f][l])[:, :, DFF + c * 256:DFF + (c + 1) * 256]),
                (lambda s: slotv(s, 4096, 6144).rearrange("p (k n) -> p k n", k=2), kview(wdn_d[f][l][c * 256:(c + 1) * 256, :])),
            ])

    def k8(a, b, n):
        return lambda s: slotv(s, a, b).rearrange("p (k n) -> p k n", k=8)

    def add_pool_chunk(j):
        chunks.append([
            (k8(0, 4096, 512), kview(abin_d[j])[:, :, 4608:5120]),
            (lambda s: slotv(s, 4096, 4608).rearrange("p (g d) -> p g d", g=4), poolw_d[j].rearrange("g c d -> c g d")),
        ])

    def add_out_chunks(wd):
        for hf in range(2):
            chunks.append([(k8(0, 4096, 512), kview(wd)[:, :, hf * 512:(hf + 1) * 512])])

    def add_ab_chunks(j):
        add_pool_chunk(j)
        for g in (2, 1, 0):
            for hh in range(2):
                chunks.append([(k8(part * 2048, (part + 1) * 2048, 256),
                                kview(abin_d[j])[:, :, g * 1536 + part * 512 + hh * 256:g * 1536 + part * 512 + hh * 256 + 256]) for part in range(3)])
        add_pool_chunk(j)
        add_out_chunks(about_d[j])

    def add_conv_chunks(j):
        for rep in range(2):
            for sc in range(4):
                chunks.append([(k8(part * 2048, (part + 1) * 2048, 256),
                                kview(cvin_d[j])[:, :, part * 1024 + sc * 256:part * 1024 + sc * 256 + 256]) for part in range(3)])
        add_out_chunks(cvout_d[j])

    for l in range(DEPTH):
        add_ffn_chunks(l, 0)
        if ENABLE_MIX:
            if l % 2 == 0 and ENABLE_AB:
                add_ab_chunks(l // 2)
            if l % 2 == 1 and ENABLE_CONV:
                add_conv_chunks(l // 2)
        add_ffn_chunks(l, 1)

    wstate = {"next_load": 0, "next_use": 0}

    def issue_loads(upto):
        while wstate["next_load"] <= min(upto, len(chunks) - 1):
            i = wstate["next_load"]
            s = i % NSLOT
            for dst_fn, src in chunks[i]:
                S.dma("pool", dst_fn(s), src, writes=[("w", s)])
            wstate["next_load"] += 1

    def next_chunk():
        i = wstate["next_use"]
        wstate["next_use"] += 1
        assert i < wstate["next_load"], "chunk not loaded"
        return i % NSLOT

    def chunk_done():
        issue_loads(wstate["next_load"])

    S.op("dve", lambda e: e.memset(eps_t[:], EPS), writes=["eps"])
    for t in range(NT):
        S.dma("sp", x_sb[:, t, :], x_d[t * 128:(t + 1) * 128, :], writes=[("x", t)])
    S.dma("sp", ident_f[:], ident_d, writes=["ident_f"])
    S.dma("pool", ident_bf[:], ident_d, writes=["ident_bf"])
    S.dma("sp", vrows[0:NV, :], vecs_d, writes=["vrows"])
    S.dma("pool", ca[:], consts_d[:, 0:CA], writes=["ca"])
    S.dma("sp", cb[:], consts_d[:, 3072:3072 + 640], writes=["cb"])
    issue_loads(NSLOT - 1)
    def f_vt(e):
        for k in range(8):
            ins = e.transpose(out=ps[:, 0, k * NV:(k + 1) * NV], in_=vrows[0:NV, k * 128:(k + 1) * 128], identity=ident_f[0:NV, 0:NV])
        return ins
    S.op("pe", f_vt, reads=["vrows", "ident_f"], writes=[("ps", 0)])
    S.op("act", lambda e: e.copy(out=vT[:], in_=ps[:, 0, 0:8 * NV].rearrange("p (k v) -> p k v", k=8)), reads=[("ps", 0)], writes=["vT"])

    ALL_HT = [("hT", t) for t in range(NT)]
    nstate = {"i": 0}

    def norm_phase(vidx):
        for t in range(NT):
            S.op("act", lambda e, t=t: e.activation(out=junk, in_=x_sb[:, t, :], func=AF.Square, accum_out=ss[:, t:t + 1]),
                 reads=[("x", t)], writes=[("ss", t)])
        S.op("act", lambda e: e.activation(out=sd[:], in_=ss[:], func=AF.Sqrt, scale=1.0 / D, bias=eps_t[:]),
             reads=[("ss", t) for t in range(NT)] + ["eps"], writes=["sd"])
        S.op("dve", lambda e: e.reciprocal(out=rstd[:], in_=sd[:]), reads=["sd"], writes=["rstd"])
        for t in range(NT):
            i = nstate["i"]
            nstate["i"] += 1
            b = i % 2
            S.op("dve", lambda e, t=t, b=b: e.tensor_scalar(out=xn[:, b, :], in0=x_sb[:, t, :], scalar1=rstd[:, t:t + 1], scalar2=None, op0=ALU.mult),
                 reads=[("x", t), "rstd"], writes=[("xn", b)])

            def f_tr(e, b=b):
                for k in range(8):
                    ins = e.transpose(out=psb(b)[:, k * 128:(k + 1) * 128], in_=xn[:, b, k * 128:(k + 1) * 128], identity=ident_bf[:])
                return ins
            S.op("pe", f_tr, reads=[("xn", b), "ident_bf"], writes=[("ps", b)])
            g3 = vT[:, :, vidx:vidx + 1].broadcast_to([128, 8, 128])
            S.op("dve", lambda e, t=t, b=b, g3=g3: e.tensor_tensor(out=hT[:, :, t * 128:(t + 1) * 128], in0=psb(b).rearrange("p (k n) -> p k n", k=8), in1=g3, op=ALU.mult),
                 reads=[("ps", b), "vT"], writes=[("hT", t)])

    sg = big[:, 0:2048].rearrange("p (b h n) -> p b h n", b=2, h=2)
    hid = big[:, 2048:3072].bitcast(BF16).rearrange("p (b h n) -> p b h n", b=2, h=2)
    fstate = {"set": 0}

    def ffn(l, f):
        norm_phase(f * 2 * DEPTH + l if f == 0 else 2 * DEPTH + l)
        S.barrier()
        seq = [(c, tg) for c in range(11) for tg in range(5)]
        slots = {}

        def gu(i):
            c, tg = seq[i]
            if c not in slots:
                slots[c] = next_chunk()
            s = slots[c]
            t0, tn = TG[tg]
            b = i % 2
            hkeys = [("hT", t) for t in range(t0 // 128, (t0 + tn) // 128)]
            for bank, off in ((0, 0), (1, 128), (2, 2048), (3, 2048 + 128)):
                def f_mm(e, bank=bank, off=off, s=s, t0=t0, tn=tn):
                    for k in range(8):
                        base = (off // 2048) * 2048 + k * 256 + (off % 2048)
                        ins = e.matmul(ps[:, bank, 0:tn], lhsT=wring[:, s, base:base + 128], rhs=hT[:, k, t0:t0 + tn], start=(k == 0), stop=(k == 7))
                    return ins
                S.op("pe", f_mm, reads=[("w", s)] + hkeys, writes=[("ps", bank)])
            for h in range(2):
                S.op("act", lambda e, h=h, b=b, tn=tn: e.activation(out=sg[:, b, h, 0:tn], in_=ps[:, h, 0:tn], func=AF.Silu),
                     reads=[("ps", h)], writes=[("sg", b, h)])
                S.op("dve", lambda e, h=h, b=b, tn=tn: e.tensor_tensor(out=hid[:, b, h, 0:tn], in0=sg[:, b, h, 0:tn], in1=ps[:, 2 + h, 0:tn], op=ALU.mult),
                     reads=[("sg", b, h), ("ps", 2 + h)], writes=[("hid", b, h)])

        def down(i):
            c, tg = seq[i]
            s = slots[c]
            t0, tn = TG[tg]
            b = i % 2
            for tt in range(tn // 128):
                t = t0 // 128 + tt
                st = fstate["set"]
                fstate["set"] ^= 1
                b0 = 4 + 2 * st

                def f_dn(e, tt=tt, b0=b0, s=s, b=b):
                    for ncol in range(2):
                        for h in range(2):
                            ins = e.matmul(ps[:, b0 + ncol, :], lhsT=hid[:, b, h, tt * 128:(tt + 1) * 128],
                                           rhs=wring[:, s, 4096 + h * 1024 + ncol * 512:4096 + h * 1024 + (ncol + 1) * 512],
                                           start=(h == 0), stop=(h == 1))
                    return ins
                S.op("pe", f_dn, reads=[("w", s), ("hid", b, 0), ("hid", b, 1)], writes=[("ps", b0), ("ps", b0 + 1)])
                S.op("dve", lambda e, t=t, b0=b0: e.scalar_tensor_tensor(out=x_sb[:, t, :], in0=ps[:, b0:b0 + 2, :].rearrange("p a n -> p (a n)"), scalar=0.5, in1=x_sb[:, t, :], op0=ALU.mult, op1=ALU.add),
                     reads=[("ps", b0), ("ps", b0 + 1), ("x", t)], writes=[("x", t)])

        gu(0)
        for i in range(1, len(seq)):
            gu(i)
            down(i - 1)
            if seq[i - 1][1] == 4:
                chunk_done()
        down(len(seq) - 1)
        chunk_done()

    class Scr:
        off = 0

        def reset(self):
            self.off = 0

        def f32(self, n):
            a = scr[:, self.off:self.off + n]
            self.off += n
            assert self.off <= 6144, self.off
            return a

        def bf16(self, n):
            m = (n + 1) // 2
            a = scr[:, self.off:self.off + m].bitcast(BF16)
            self.off += m
            assert self.off <= 6144, self.off
            return a
    SC = Scr()
    ENGS4 = ("pe", "act", "dve", "pool")
    BARK = [("bar", e) for e in ENGS4]
    scr_keys = []

    def phase_barrier():
        if not _os.environ.get("K_PB_NOKEYS"):
            for e in ("pe", "act", "dve"):
                waits = S._deps(e, scr_keys, scr_keys)
                S.prog[e].append((waits, None, None))
        if not _os.environ.get("K_PB_NOBAR"):
            S.barrier()
        del scr_keys[:]
        SC.reset()

    def skey(k):
        scr_keys.append(k)
        return k

    halo_on = cb[:, 64:65]
    Ef = cb[0:64, 128:640].rearrange("p (c m) -> p c m", c=4)
    mrow_n = ca[:, 0:512]
    mrow_0 = ca[:, 512:1024]
    smask = [ca[:, 1024 + g * 256:1024 + (g + 1) * 256] for g in range(3)]
    cmask = [ca[:, 1792:1824], ca[:, 1824:1952], ca[:, 1952:2208]]
    oh = ca[:, 2208:2720].rearrange("p (h m) -> p h m", h=8)
    zlhs = ca[:, 2720:2848]
    CW0 = 3 * DEPTH + N_AB

    def out_proj(lhsT_all, tagk):
        cnt = 0
        for hf in range(2):
            s = next_chunk()
            for t in range(NT):
                bank = 6 + (cnt % 2)
                cnt += 1

                def f_mm(e, t=t, s=s, bank=bank):
                    for k in range(8):
                        ins = e.matmul(ps[:, bank, :], lhsT=lhsT_all[:, k, t * 128:(t + 1) * 128], rhs=wring[:, s, k * 512:(k + 1) * 512], start=(k == 0), stop=(k == 7))
                    return ins
                S.op("pe", f_mm, reads=[("w", s), (tagk, t)], writes=[("ps", bank)])
                S.op("dve", lambda e, t=t, hf=hf, bank=bank: e.tensor_tensor(out=x_sb[:, t, hf * 512:(hf + 1) * 512], in0=ps[:, bank, :], in1=x_sb[:, t, hf * 512:(hf + 1) * 512], op=ALU.add),
                     reads=[("ps", bank), ("x", t)], writes=[("x", t)])
            chunk_done()

    cx_in = [dint("cx_in%d" % j, [2, D], F32) for j in range(N_C)]
    cx_out = [dint("cx_out%d" % j, [4, D], F32) for j in range(N_C)]

    def conv_mixer(l):
        j = l // 2
        norm_phase(DEPTH + l)
        phase_barrier()
        mT = big[:, :].bitcast(BF16).rearrange("p (k n) -> p k n", k=8)
        urows = SC.f32(2048).rearrange("p (a n) -> p a n", a=2)
        rowbuf = SC.f32(1024)
        t1 = rowbuf.rearrange("p (a n) -> p a n", a=2)
        RBK = ["rowbuf", ("c_t1", 0), ("c_t1", 1)]
        Ub = SC.f32(2 * 514).rearrange("p (a n) -> p a n", a=2)
        yb = SC.f32(512)
        tails = SC.f32(16).rearrange("p (f n) -> p f n", f=8)
        halo = SC.f32(16).rearrange("p (f n) -> p f n", f=8)
        sprev = SC.f32(256).rearrange("p (f b n) -> p f b n", f=8, b=16)
        Us = SC.f32(2 * 160).rearrange("p (a b n) -> p a b n", a=2, b=16)
        ys = SC.f32(128).rearrange("p (b n) -> p b n", b=16)
        wi = CW0 + 3 * j
        for sc in range(4):
            s = next_chunk()
            for ti, t in enumerate((15, 16)):
                for pi, part in enumerate((1, 2)):
                    bank = ti * 2 + pi

                    def f_mm(e, t=t, s=s, bank=bank, part=part):
                        for k in range(8):
                            ins = e.matmul(ps[:, bank, 0:256], lhsT=hT[:, k, t * 128:(t + 1) * 128], rhs=wring[:, s, part * 2048 + k * 256:part * 2048 + (k + 1) * 256], start=(k == 0), stop=(k == 7))
                        return ins
                    S.op("pe", f_mm, reads=[("w", s), ("hT", t)], writes=[("ps", bank)])
                S.op("act", lambda e, ti=ti: e.copy(out=t1[:, ti, 0:256], in_=ps[:, ti * 2, 0:256]), reads=[("ps", ti * 2)], writes=[("c_t1", ti)])
                S.op("dve", lambda e, ti=ti, sc=sc: e.tensor_tensor(out=urows[:, ti, sc * 256:(sc + 1) * 256], in0=t1[:, ti, 0:256], in1=ps[:, ti * 2 + 1, 0:256], op=ALU.mult),
                     reads=[("c_t1", ti), ("ps", ti * 2 + 1)], writes=[skey(("urows", ti, sc))])
            chunk_done()
        ur0 = [("urows", 0, sc) for sc in range(4)]
        ur1 = [("urows", 1, sc) for sc in range(4)]
        S.dma("sp", convp_d[j], urows[126:128, 0, :], reads=ur0, writes=[("convp", j)])
        S.dma("sp", convs_d[j, :, 0, :], urows[6:128:8, 1, :], reads=ur1, writes=[("convs0", j)])
        S.dma("sp", convs_d[j, :, 1, :], urows[7:128:8, 1, :], reads=ur1, writes=[("convs1", j)])
        out_keys.extend([("convp", j), ("convs0", j), ("convs1", j)])
        S.dma("sp", cx_in[j], urows[126:128, 0, :], reads=ur0, writes=[("cx_in", j)])
        S.collective(cx_in[j], cx_out[j], groups, reads=[("cx_in", j)], writes=[("cx_out", j)])
        S.dma("sp", rowbuf[0:2, :], cx_out[j][0:2, :], reads=[("cx_out", j)], writes=[skey("rowbuf")] + RBK[1:])

        def f_tr2(e):
            for f in range(8):
                ins = e.transpose(out=ps[:, 4, f * 2:(f + 1) * 2], in_=rowbuf[0:2, f * 128:(f + 1) * 128], identity=ident_f[0:2, 0:2])
            return ins
        S.op("pe", f_tr2, reads=["rowbuf", "ident_f"], writes=[("ps", 4)])
        S.op("dve", lambda e: e.tensor_scalar(out=halo[:, :, :], in0=ps[:, 4, 0:16].rearrange("p (f n) -> p f n", f=8), scalar1=halo_on, scalar2=None, op0=ALU.mult),
             reads=[("ps", 4), "cb"], writes=["c_halo"])
        S.dma("sp", rowbuf[0:32, :], sconv_d[j].rearrange("b n d -> (b n) d"), reads=[("ps", 4)], writes=[skey("rowbuf")] + RBK[1:])

        def f_tr32(e):
            for f in range(8):
                ins = e.transpose(out=ps[:, 5, f * 32:(f + 1) * 32], in_=rowbuf[0:32, f * 128:(f + 1) * 128], identity=ident_f[0:32, 0:32])
            return ins
        S.op("pe", f_tr32, reads=["rowbuf", "ident_f"], writes=[("ps", 5)])
        S.op("act", lambda e: e.copy(out=sprev[:, :, :, :], in_=ps[:, 5, 0:256].rearrange("p (f b n) -> p f b n", f=8, b=16)), reads=[("ps", 5)], writes=["c_sprev"])
        ui = 0
        for sc in range(4):
            s = next_chunk()
            for tg in range(5):
                t0, tn = TG[tg]
                hkeys = [("hT", t) for t in range(t0 // 128, (t0 + tn) // 128)]
                for fh in range(2):
                    for part in range(3):
                        bank = fh * 3 + part

                        def f_mm(e, s=s, bank=bank, part=part, fh=fh, t0=t0, tn=tn):
                            for k in range(8):
                                base = part * 2048 + k * 256 + fh * 128
                                ins = e.matmul(ps[:, bank, 0:tn], lhsT=wring[:, s, base:base + 128], rhs=hT[:, k, t0:t0 + tn], start=(k == 0), stop=(k == 7))
                            return ins
                        S.op("pe", f_mm, reads=[("w", s)] + hkeys, writes=[("ps", bank)])
                for fh in range(2):
                    f = sc * 2 + fh
                    bgb, bgc, bv = fh * 3, fh * 3 + 1, fh * 3 + 2
                    w0, w1, w2 = (vT[:, f, wi + tt:wi + tt + 1] for tt in range(3))
                    ub = ui % 2
                    ui += 1
                    S.op("act", lambda e, ub=ub, bgc=bgc, tn=tn: e.copy(out=t1[:, ub, 0:tn], in_=ps[:, bgc, 0:tn]), reads=[("ps", bgc)], writes=[("c_t1", ub), "rowbuf"])
                    mkeys = [("mT", t) for t in range(t0 // 128, (t0 + tn) // 128)]
                    if tg < 4:
                        S.op("dve", lambda e, ub=ub, bv=bv: e.tensor_tensor(out=Ub[:, ub, 2:514], in0=t1[:, ub, 0:512], in1=ps[:, bv, :], op=ALU.mult),
                             reads=[("c_t1", ub), ("ps", bv)], writes=[("c_U", ub)])
                        src = halo[:, f, :] if tg == 0 else tails[:, f, :]
                        S.op("pool", lambda e, ub=ub, src=src: e.tensor_copy(out=Ub[:, ub, 0:2], in_=src), reads=["c_halo", ("c_tail", f)], writes=[("c_Uh", ub)])
                        S.op("pool", lambda e, ub=ub, f=f: e.tensor_copy(out=tails[:, f, :], in_=Ub[:, ub, 512:514]), reads=[("c_U", ub), ("c_Uh", ub)], writes=[("c_tail", f)])
                        S.op("dve", lambda e, ub=ub, w0=w0: e.tensor_scalar(out=yb[:, :], in0=Ub[:, ub, 0:512], scalar1=w0, scalar2=None, op0=ALU.mult),
                             reads=[("c_U", ub), ("c_Uh", ub), "vT"], writes=["c_y"])
                        S.op("dve", lambda e, ub=ub, w1=w1: e.scalar_tensor_tensor(out=yb[:, :], in0=Ub[:, ub, 1:513], scalar=w1, in1=yb[:, :], op0=ALU.mult, op1=ALU.add),
                             reads=["c_y"], writes=["c_y"])
                        S.op("dve", lambda e, ub=ub, w2=w2: e.scalar_tensor_tensor(out=yb[:, :], in0=Ub[:, ub, 2:514], scalar=w2, in1=yb[:, :], op0=ALU.mult, op1=ALU.add),
                             reads=["c_y"], writes=["c_y"])
                        S.op("dve", lambda e, f=f, t0=t0, bgb=bgb: e.tensor_tensor(out=mT[:, f, t0:t0 + 512], in0=yb[:, :], in1=ps[:, bgb, :], op=ALU.mult),
                             reads=["c_y", ("ps", bgb)], writes=mkeys)
                    else:
                        v3 = lambda ap: ap.rearrange("p (b n) -> p b n", b=16)
                        S.op("dve", lambda e, ub=ub, bv=bv: e.tensor_tensor(out=Us[:, ub, :, 2:10], in0=v3(t1[:, ub, 0:128]), in1=v3(ps[:, bv, 0:128]), op=ALU.mult),
                             reads=[("c_t1", ub), ("ps", bv)], writes=[("c_Us", ub)])
                        S.op("pool", lambda e, ub=ub, f=f: e.tensor_copy(out=Us[:, ub, :, 0:2], in_=sprev[:, f, :, :]), reads=["c_sprev"], writes=[("c_Ush", ub)])
                        S.op("dve", lambda e, ub=ub, w0=w0: e.tensor_scalar(out=ys[:, :, :], in0=Us[:, ub, :, 0:8], scalar1=w0, scalar2=None, op0=ALU.mult),
                             reads=[("c_Us", ub), ("c_Ush", ub), "vT"], writes=["c_ys"])
                        S.op("dve", lambda e, ub=ub, w1=w1: e.scalar_tensor_tensor(out=ys[:, :, :], in0=Us[:, ub, :, 1:9], scalar=w1, in1=ys[:, :, :], op0=ALU.mult, op1=ALU.add),
                             reads=["c_ys"], writes=["c_ys"])
                        S.op("dve", lambda e, ub=ub, w2=w2: e.scalar_tensor_tensor(out=ys[:, :, :], in0=Us[:, ub, :, 2:10], scalar=w2, in1=ys[:, :, :], op0=ALU.mult, op1=ALU.add),
                             reads=["c_ys"], writes=["c_ys"])
                        S.op("dve", lambda e, f=f, bgb=bgb: e.tensor_tensor(out=v3(mT[:, f, 2048:2176]), in0=ys[:, :, :], in1=v3(ps[:, bgb, 0:128]), op=ALU.mult),
                             reads=["c_ys", ("ps", bgb)], writes=mkeys)
            chunk_done()
        out_proj(mT, "mT")

    NHT = (1, 4, 16)
    recs = [dint("recs%d" % j, [6 * NT * 128, 768], BF16) for j in range(N_AB)]
    ex_in = [[[dint("ex_in%d_%d_%d" % (j, g, hh), [NHT[g] * 128, 256], F32) for hh in range(2)] for g in range(3)] for j in range(N_AB)]
    ex_out = [[[dint("ex_out%d_%d_%d" % (j, g, hh), [2 * NHT[g] * 128, 256], F32) for hh in range(2)] for g in range(3)] for j in range(N_AB)]
    pt_in = [dint("pt_in%d" % j, [128, 60], F32) for j in range(N_AB)]
    pt_out = [dint("pt_out%d" % j, [256, 60], F32) for j in range(N_AB)]

    def tile_geom(g, jt):
        d = DILS[g]
        cls = 16 // d
        r, n = jt // cls, jt % cls
        start = r + d * 128 * n
        return d, r, n, start

    def ab_mixer(l):
        j = l // 2
        norm_phase(DEPTH + l)
        phase_barrier()
        O_acc = big[:, :].rearrange("p (c n) -> p c n", c=4)
        S.op("pool", lambda e: e.memset(big[:, :], 0.0), writes=["Oacc"])
        S.op("pool", lambda e: e.memset(den_acc[:, :], 0.0), writes=["den"])
        ptail = SC.f32(64).rearrange("p (g n) -> p g n", g=4)
        phalo = cb[:, 66:126].rearrange("p (g n) -> p g n", g=4)
        s = next_chunk()

        def f_pt(e, s=s):
            for gi in range(4):
                for k in range(8):
                    ins = e.matmul(ps[:, 2, gi * 16:(gi + 1) * 16], lhsT=wring[:, s, k * 512 + gi * 128:k * 512 + (gi + 1) * 128], rhs=hT[:, k, NP - 16:NP], start=(k == 0), stop=(k == 7))
            return ins
        S.op("pe", f_pt, reads=[("w", s), ("hT", 15)], writes=[("ps", 2)])
        chunk_done()
        S.op("act", lambda e: e.copy(out=ptail[:, :, :], in_=ps[:, 2, 0:64].rearrange("p (g n) -> p g n", g=4)), reads=[("ps", 2)], writes=[skey("ptail")])
        S.dma("sp", pt_in[j].rearrange("p (g n) -> p g n", g=4), ptail[:, :, 1:16], reads=["ptail"], writes=[("pt_in", j)])
        S.collective(pt_in[j], pt_out[j], groups, reads=[("pt_in", j)], writes=[("pt_out", j)])
        S.dma("sp", phalo[:, :, :], pt_out[j][0:128, :].rearrange("p (g n) -> p g n", g=4), reads=[("pt_out", j)], writes=[skey("phalo_raw")])
        S.op("dve", lambda e: e.tensor_scalar(out=phalo[:, :, :], in0=phalo[:, :, :], scalar1=halo_on, scalar2=None, op0=ALU.mult), reads=["phalo_raw", "cb"], writes=["phalo"])

        AB_STOP = int(_os.environ.get("K_AB_STOP", "9"))
        used = [1]

        def bail():
            for _ in range(10 - used[0]):
                next_chunk()
                chunk_done()
        if AB_STOP <= 1:
            return bail()
        ropeG = SC.f32(NT * 64).rearrange("p (t n) -> p t n", t=NT)
        tmp = SC.f32(1024).rearrange("p (a n) -> p a n", a=4)
        rot = SC.f32(1024).rearrange("p (a n) -> p a n", a=2)
        rot_bf2 = SC.bf16(1024).rearrange("p (a n) -> p a n", a=2)
        vf = SC.f32(512).rearrange("p (a n) -> p a n", a=2)
        rec = SC.bf16(2 * 768).rearrange("p (a n) -> p a n", a=2)
        ai = 0
        pA = 0
        pend = [None]
        for g in (2, 1, 0):
            d = DILS[g]
            rows_g = min(WINS[g], NP)
            S.dma("sp", ropeG[:, :, :], rope_d[g].rearrange("t p n -> p t n"), writes=[skey("ropeG")])
            for hh in range(2):
                pid = g * 2 + hh
                s = next_chunk()
                for jt in range(NT):
                    if jt < 16:
                        d_, r, n, start = tile_geom(g, jt)
                        tok = lambda k, start=start, d=d: hT[:, k, start:start + d * 127 + 1:d]
                        hk = ALL_HT[:16]
                    else:
                        tok = lambda k: hT[:, k, NP:T]
                        hk = [("hT", 16)]
                    b = ai % 2
                    ai += 1
                    bA = 4 + 2 * (pA % 2)
                    pA += 1

                    def f_mm(e, s=s, bA=bA, tok=tok):
                        for part in range(3):
                            for k in range(8):
                                o = ps[:, bA + part // 2, (part % 2) * 256:(part % 2) * 256 + 256]
                                ins = e.matmul(o, lhsT=tok(k), rhs=wring[:, s, part * 2048 + k * 256:part * 2048 + (k + 1) * 256], start=(k == 0), stop=(k == 7))
                        return ins
                    S.op("pe", f_mm, reads=[("w", s)] + hk, writes=[("ps", bA), ("ps", bA + 1)])
                    qk4 = ps[:, bA, :].rearrange("p (a h f) -> p a h f", a=8, h=2)
                    cosb = ropeG[:, jt, 0:32].unsqueeze(1).broadcast_to([128, 8, 32])
                    sinb = ropeG[:, jt, 32:64].unsqueeze(1).broadcast_to([128, 8, 32])
                    t4 = lambda i: tmp[:, i, :].rearrange("p (a f) -> p a f", a=8)
                    for i, (hsel, tb) in enumerate(((0, cosb), (1, sinb), (1, cosb), (0, sinb))):
                        S.op("dve", lambda e, i=i, hsel=hsel, tb=tb, qk4=qk4: e.tensor_tensor(out=t4(i), in0=qk4[:, :, hsel, :], in1=tb, op=ALU.mult),
                             reads=[("ps", bA), "ropeG"], writes=[("a_tmp", i)])
                    rot4 = rot[:, b, :].rearrange("p (a h f) -> p a h f", a=8, h=2)
                    S.op("dve", lambda e, rot4=rot4: e.tensor_tensor(out=rot4[:, :, 0, :], in0=t4(0), in1=t4(1), op=ALU.subtract),
                         reads=[("a_tmp", 0), ("a_tmp", 1)], writes=[skey(("a_rot0", b))])
                    S.op("dve", lambda e, rot4=rot4: e.tensor_tensor(out=rot4[:, :, 1, :], in0=t4(2), in1=t4(3), op=ALU.add),
                         reads=[("a_tmp", 2), ("a_tmp", 3)], writes=[skey(("a_rot1", b))])
                    S.op("act", lambda e, b=b: e.copy(out=rot_bf2[:, b, :], in_=rot[:, b, :]), reads=[("a_rot0", b), ("a_rot1", b)], writes=[("a_rotbf", b)])
                    S.op("act", lambda e, b=b, bA=bA: e.copy(out=vf[:, b, :], in_=ps[:, bA + 1, 0:256]), reads=[("ps", bA + 1)], writes=[skey(("a_vf", b))])
                    S.op("act", lambda e, b=b: e.copy(out=rec[:, b, 512:768], in_=vf[:, b, :]), reads=[("a_vf", b)], writes=[skey(("a_recv", b))])
                    kk = rot[:, b, 256:512].rearrange("p (h f) -> p h f", h=4)
                    vv = vf[:, b, :].rearrange("p (h f) -> p h f", h=4)
                    if jt == 16:
                        okk, okv = ("kvsK", j, g, hh), ("kvsV", j, g, hh)
                        S.dma("sp", kvs_d[g][j, :, 0, hh * 4:hh * 4 + 4, :], kk, reads=[("a_rot0", b), ("a_rot1", b)], writes=[okk])
                        S.dma("sp", kvs_d[g][j, :, 1, hh * 4:hh * 4 + 4, :], vv, reads=[("a_vf", b)], writes=[okv])
                        out_keys.extend([okk, okv])
                    elif start + d * 127 >= NP - rows_g and n == 16 // d - 1:
                        r0 = start - (NP - rows_g)
                        okk, okv = ("kvrK", j, g, hh, jt), ("kvrV", j, g, hh, jt)
                        S.dma("sp", kvrow_d[g][j, r0:r0 + d * 127 + 1:d, 0, hh * 4:hh * 4 + 4, :], kk, reads=[("a_rot0", b), ("a_rot1", b)], writes=[okk])
                        S.dma("sp", kvrow_d[g][j, r0:r0 + d * 127 + 1:d, 1, hh * 4:hh * 4 + 4, :], vv, reads=[("a_vf", b)], writes=[okv])
                        out_keys.extend([okk, okv])
                    bT = pA % 2
                    is_halo = jt < 16 and n == 16 // d - 1
                    hi = r if jt < 16 else 0

                    def back(b=b, bT=bT, jt=jt, pid=pid, g=g, hh=hh, is_halo=is_halo, hi=hi):
                        def f_tr(e):
                            for c4 in range(4):
                                ins = e.transpose(out=psb(bT)[:, c4 * 128:(c4 + 1) * 128], in_=rot_bf2[:, b, c4 * 128:(c4 + 1) * 128], identity=ident_bf[:])
                            return ins
                        S.op("pe", f_tr, reads=[("a_rotbf", b), "ident_bf"], writes=[("ps", bT)])
                        S.op("act", lambda e: e.copy(out=rec[:, b, 0:512], in_=psb(bT)[:, 0:512]), reads=[("ps", bT)], writes=[skey(("a_recqk", b))])
                        rrow = (pid * NT + jt) * 128
                        S.dma("sp", recs[j][rrow:rrow + 128, :], rec[:, b, :], reads=[("a_recqk", b), ("a_recv", b)], writes=[("rec", j, pid, jt)])
                        if is_halo:
                            erow = hi * 128
                            S.dma("sp", ex_in[j][g][hh][erow:erow + 128, :], rec[:, b, 256:768].bitcast(F32), reads=[("a_recqk", b), ("a_recv", b)], writes=[("ex_in", j, g, hh, hi)])
                    if pend[0] is not None:
                        pend[0]()
                    pend[0] = back
                pend[0]()
                pend[0] = None
                chunk_done()
                S.collective(ex_in[j][g][hh], ex_out[j][g][hh], groups, reads=[("ex_in", j, g, hh, hi_) for hi_ in range(NHT[g])], writes=[("ex_out", j, g, hh)])

        used[0] = 7
        if AB_STOP <= 2:
            return bail()
        phase_barrier()
        qk_c = SC.bf16(3 * 512).rearrange("p (a n) -> p a n", a=3)
        v_c = SC.bf16(3 * 256).rearrange("p (a n) -> p a n", a=3)
        k_p = SC.bf16(3 * 256).rearrange("p (a n) -> p a n", a=3)
        v_p = SC.bf16(3 * 256).rearrange("p (a n) -> p a n", a=3)
        pT = SC.bf16(2 * 1024).rearrange("p (a n) -> p a n", a=2)
        kc = SC.bf16(2 * 1024).rearrange("p (a b n) -> p a b n", a=2, b=4)
        vc = SC.bf16(2 * 1024).rearrange("p (a b n) -> p a b n", a=2, b=4)
        kTs = SC.bf16(2048).rearrange("p (a b c n) -> p a b c n", a=2, b=4, c=2)
        pTc = SC.bf16(2 * 128).rearrange("p (a n) -> p a n", a=2)
        bi = 0
        ci = 0
        pendB = [None]
        for g in (0, 1, 2):
            d = DILS[g]
            nblk = (1, 4, 8)[g]
            for hh in range(2):
                pid = g * 2 + hh
                order = [jt for jt in range(16) if jt % (16 // d) != 0] + [jt for jt in range(16) if jt % (16 // d) == 0] + [16]
                if _os.environ.get("K_PB_TILES") is not None:
                    order = [int(v) for v in _os.environ["K_PB_TILES"].split(",") if v != ""]
                for jt in order:
                    b = bi % 2
                    b3 = bi % 3
                    bi += 1
                    rrow = (pid * NT + jt) * 128
                    S.dma("sp", qk_c[:, b3, :], recs[j][rrow:rrow + 128, 0:512], reads=[("rec", j, pid, jt)], writes=[skey(("b_qk", b3))])
                    S.dma("sp", v_c[:, b3, :], recs[j][rrow:rrow + 128, 512:768], reads=[("rec", j, pid, jt)], writes=[skey(("b_v", b3))])
                    blocks = [(qk_c[:, b3, 256:512], v_c[:, b3, :], [("b_qk", b3), ("b_v", b3)])]
                    if jt < 16:
                        d_, r, n, start = tile_geom(g, jt)
                        if n > 0:
                            prow = (pid * NT + jt - 1) * 128
                            S.dma("sp", k_p[:, b3, :], recs[j][prow:prow + 128, 256:512], reads=[("rec", j, pid, jt - 1)], writes=[skey(("b_kp", b3))])
                            S.dma("sp", v_p[:, b3, :], recs[j][prow:prow + 128, 512:768], reads=[("rec", j, pid, jt - 1)], writes=[skey(("b_vp", b3))])
                        else:
                            erow = r * 128
                            S.dma("sp", k_p[:, b3, :].bitcast(F32), ex_out[j][g][hh][erow:erow + 128, 0:128], reads=[("ex_out", j, g, hh)], writes=[skey(("b_kp", b3))])
                            S.dma("sp", v_p[:, b3, :].bitcast(F32), ex_out[j][g][hh][erow:erow + 128, 128:256], reads=[("ex_out", j, g, hh)], writes=[skey(("b_vp", b3))])
                        blocks.append((k_p[:, b3, :], v_p[:, b3, :], [("b_kp", b3), ("b_vp", b3)]))
                        mrow = mrow_0 if n == 0 else mrow_n
                    nb = len(blocks)
                    bkeys = [k for blk in blocks for k in blk[2]]

                    samp = (jt == 16)
                    if samp:
                        pidx = lambda h4, cp: (h4 % 2) * 512 + (h4 // 2) * 128
                    else:
                        pidx = lambda h4, cp: (h4 % 2) * 512 + (h4 // 2) * 256 + cp * 128

                    sbk = 0 if b == 0 else 6

                    def f_sc(e, b3=b3, blocks=blocks, nb=nb, g=g, samp=samp, pidx=pidx, sbk=sbk, mrow=(mrow if jt < 16 else None)):
                        for bank in range(2):
                            if samp:
                                e.matmul(ps[:, sbk + bank, 0:256], lhsT=ident_bf[:], rhs=smask[g], start=True, stop=False)
                            else:
                                e.matmul(ps[:, sbk + bank, :], lhsT=ident_bf[:], rhs=mrow, start=True, stop=False)
                        for h4 in (0, 2, 1, 3):
                            pb = (h4 % 2) * 64
                            for cp in range(nb):
                                c0 = pidx(h4, cp) % 512
                                kT = blocks[cp][0]
                                ins = e.matmul(ps[:, sbk + h4 % 2, c0:c0 + 128], lhsT=kT[pb:pb + 64, (h4 // 2) * 128:(h4 // 2) * 128 + 128], rhs=qk_c[pb:pb + 64, b3, (h4 // 2) * 128:(h4 // 2) * 128 + 128], start=False, stop=True)
                        return ins
                    S.op("pe", f_sc, reads=bkeys + ["ca", "ident_bf"], writes=[("ps", sbk), ("ps", sbk + 1)])
                    if not samp:
                        S.op("act", lambda e, b=b, sbk=sbk: e.activation(out=pT[:, b, :], in_=ps[:, sbk:sbk + 2, :].rearrange("p a n -> p (a n)"), func=AF.Exp, scale=0.125),
                             reads=[("ps", sbk), ("ps", sbk + 1)], writes=[("b_pT", b)])
                    else:
                        S.op("act", lambda e, b=b, sbk=sbk: e.activation(out=pT[:, b, :].rearrange("p (a n) -> p a n", a=2)[:, :, 0:256], in_=ps[:, sbk:sbk + 2, 0:256], func=AF.Exp, scale=0.125),
                             reads=[("ps", sbk), ("ps", sbk + 1)], writes=[("b_pT", b)])

                    ob = 2 if (samp or b == 0) else 5

                    def f_pv(e, b=b, blocks=blocks, nb=nb, pidx=pidx, ob=ob):
                        for hs in range(2):
                            for cc in range(2):
                                h4 = 2 * cc + hs
                                o = ps[:, ob, (hs * 2 + cc) * 128:(hs * 2 + cc + 1) * 128]
                                for cp in range(nb):
                                    V = blocks[cp][1]
                                    ins = e.matmul(o, lhsT=V[:, cc * 128:(cc + 1) * 128], rhs=pT[:, b, pidx(h4, cp):pidx(h4, cp) + 128], start=(cp == 0), stop=(cp == nb - 1))
                        return ins

                    def f_dn(e, b=b, nb=nb, hh=hh, pidx=pidx):
                        first = True
                        for h4 in range(4):
                            for cp in range(nb):
                                ins = e.matmul(ps[0:64, 3, 0:128], lhsT=oh[:, hh * 4 + h4, :], rhs=pT[:, b, pidx(h4, cp):pidx(h4, cp) + 128], start=first, stop=(h4 == 3 and cp == nb - 1))
                                first = False
                        return ins
                    pv_reads = [("b_pT", b)] + bkeys + ["ca"]
                    if jt < 16:
                        cols = slice(start, start + d * 127 + 1, d)
                    else:
                        cols = slice(NP, T)
                    oa0 = O_acc[0:64, hh * 2:hh * 2 + 2, cols]
                    oa1 = O_acc[64:128, hh * 2:hh * 2 + 2, cols]

                    def accum(oa0=oa0, oa1=oa1, cols=cols, ob=ob):
                        S.op("dve", lambda e, cols=cols: e.tensor_tensor(out=den_acc[:, cols], in0=ps[0:64, 3, 0:128], in1=den_acc[:, cols], op=ALU.add),
                             reads=[("ps", 3), "den"], writes=["den"])
                        S.op("dve", lambda e, oa0=oa0, ob=ob: e.tensor_tensor(out=oa0, in0=ps[0:64, ob, 0:256].rearrange("p (c n) -> p c n", c=2), in1=oa0, op=ALU.add),
                             reads=[("ps", ob), "Oacc"], writes=["Oacc"])
                        S.op("dve", lambda e, oa1=oa1, ob=ob: e.tensor_tensor(out=oa1, in0=ps[64:128, ob, 256:512].rearrange("p (c n) -> p c n", c=2), in1=oa1, op=ALU.add),
                             reads=[("ps", ob), "Oacc"], writes=["Oacc"])

                    def fin(f_pv=f_pv, f_dn=f_dn, pv_reads=pv_reads, accum=accum, ob=ob):
                        S.op("pe", f_pv, reads=pv_reads, writes=[("ps", ob)])
                        S.op("pe", f_dn, reads=pv_reads, writes=[("ps", 3)])
                        accum()
                    if pendB[0] is not None:
                        pendB[0]()
                        pendB[0] = None
                    if samp:
                        fin()
                    else:
                        pendB[0] = fin
                    if samp and not _os.environ.get("K_NOCACHE"):
                        def f_z(e):
                            e.matmul(ps[:, 2, :], lhsT=zlhs, rhs=mrow_n, start=True, stop=False)
                            return e.matmul(ps[0:64, 3, 0:128], lhsT=zlhs[:, 0:64], rhs=ident_bf[:], start=True, stop=False)
                        S.op("pe", f_z, reads=["ca", "ident_bf"], writes=[("ps", 2), ("ps", 3)])
                    if jt == 16 and not _os.environ.get("K_NOCACHE"):
                        stp = [(0, nblk)] if nblk <= 4 else [(0, 4), (4, 8)]
                        steps = [(sb_, b0, b1) for sb_ in range(16) for (b0, b1) in stp]
                        NS = len(steps)

                        def cw_view(i):
                            sb_, b0, b1 = steps[i]
                            return cw_d[g][j, sb_].rearrange("(p r) kv h f -> p r kv h f", r=d), b0, b1

                        def loadK(i):
                            if i >= NS:
                                return
                            cwv, b0, b1 = cw_view(i)
                            S.dma("pool", kc[:, i % 2, 0:b1 - b0, :].rearrange("p r (h f) -> p r h f", h=4), cwv[:, b0:b1, 0, hh * 4:hh * 4 + 4, :], writes=[skey(("c_kc", i % 2))])

                        def loadV(i):
                            if i >= NS:
                                return
                            cwv, b0, b1 = cw_view(i)
                            S.dma("pool", vc[:, i % 2, 0:b1 - b0, :].rearrange("p r (h f) -> p r h f", h=4), cwv[:, b0:b1, 1, hh * 4:hh * 4 + 4, :], writes=[skey(("c_vc", i % 2))])

                        def stA(i):
                            if i >= NS:
                                return
                            sb_, b0, b1 = steps[i]
                            nbk = b1 - b0
                            c2 = i % 2

                            def f_ct(e):
                                for bl in range(nbk):
                                    for cc in range(2):
                                        ins = e.transpose(out=psb(4)[:, (bl * 2 + cc) * 128:(bl * 2 + cc + 1) * 128], in_=kc[:, c2, bl, cc * 128:(cc + 1) * 128], identity=ident_bf[:])
                                return ins
                            S.op("pe", f_ct, reads=[("c_kc", c2), "ident_bf"], writes=[("ps", 4)])
                            S.op("act", lambda e: e.copy(out=kTs[:, c2, 0:nbk, :, :], in_=psb(4)[:, 0:nbk * 256].rearrange("p (b c n) -> p b c n", b=nbk, c=2)), reads=[("ps", 4)], writes=[("c_kT", c2)])

                        def stB(i):
                            sb_, b0, b1 = steps[i]
                            nbk = b1 - b0
                            c2 = i % 2

                            def f_cs(e, g=g, b3=b3):
                                cm = cmask[g][:, b0 * 32:(b0 + nbk) * 32].rearrange("p (r h t) -> p r h t", r=nbk, h=4)[:, :, 0:2, :]
                                for par in range(2):
                                    e.matmul(ps[:, par, 0:nbk * 16].rearrange("p (r h t) -> p r h t", r=nbk, h=2), lhsT=ident_bf[:], rhs=cm, start=True, stop=False)
                                for par in range(2):
                                    pb = par * 64
                                    for bl in range(nbk):
                                        for hp in range(2):
                                            ins = e.matmul(ps[:, par, (bl * 2 + hp) * 8:(bl * 2 + hp + 1) * 8], lhsT=kTs[pb:pb + 64, c2, bl, hp, :],
                                                           rhs=qk_c[pb:pb + 64, b3, hp * 128 + sb_ * 8:hp * 128 + sb_ * 8 + 8], start=False, stop=True)
                                return ins
                            S.op("pe", f_cs, reads=[("c_kT", c2), ("b_qk", b3), "ca", "ident_bf"], writes=[("ps", 0), ("ps", 1)])
                            for par in range(2):
                                S.op("act", lambda e, par=par: e.activation(out=pTc[:, c2, par * 64:par * 64 + nbk * 16], in_=ps[:, par, 0:nbk * 16], func=AF.Exp, scale=0.125),
                                     reads=[("ps", par)], writes=[("c_pT", c2, par)])

                        def stC(i):
                            if i < 0:
                                return
                            sb_, b0, b1 = steps[i]
                            nbk = b1 - b0
                            c2 = i % 2

                            def f_cpv(e, hh=hh):
                                for hs in range(2):
                                    for cc in range(2):
                                        for bl in range(nbk):
                                            ins = e.matmul(ps[:, 2, (hs * 2 + cc) * 128 + sb_ * 8:(hs * 2 + cc) * 128 + sb_ * 8 + 8], lhsT=vc[:, c2, bl, cc * 128:(cc + 1) * 128],
                                                           rhs=pTc[:, c2, hs * 64 + (bl * 2 + cc) * 8:hs * 64 + (bl * 2 + cc + 1) * 8], start=False, stop=True)
                                for bl in range(nbk):
                                    for h4 in range(4):
                                        ins = e.matmul(ps[0:64, 3, sb_ * 8:sb_ * 8 + 8], lhsT=oh[:, hh * 4 + h4, :], rhs=pTc[:, c2, (h4 % 2) * 64 + (bl * 2 + h4 // 2) * 8:(h4 % 2) * 64 + (bl * 2 + h4 // 2 + 1) * 8], start=False, stop=True)
                                return ins
                            S.op("pe", f_cpv, reads=[("c_pT", c2, 0), ("c_pT", c2, 1), ("c_vc", c2), "ca"], writes=[("ps", 2), ("ps", 3)])

                        loadK(0)
                        loadK(1)
                        loadV(0)
                        stA(0)
                        for i in range(NS):
                            loadK(i + 2)
                            stA(i + 1)
                            stB(i)
                            stC(i - 1)
                            loadV(i + 1)
                        stC(NS - 1)
                        accum()

        if AB_STOP <= 3:
            return bail()
        phase_barrier()
        apT = hT
        Ab = SC.f32(2 * 527).rearrange("p (a n) -> p a n", a=2)
        ptl = SC.f32(60).rearrange("p (g n) -> p g n", g=4)
        dTb = SC.bf16(2 * 512).rearrange("p (a n) -> p a n", a=2)
        uoff = SC.off
        As = SC.f32(2 * 368).rearrange("p (a b n) -> p a b n", a=2, b=16)
        SCR_S0 = SC.f32(368).rearrange("p (b n) -> p b n", b=16)
        SCR_S1 = SC.f32(368).rearrange("p (b n) -> p b n", b=16)
        utok = scr[:, uoff:uoff + 1024].rearrange("p (a n) -> p a n", a=2)
        UTK = [("utok", 0), ("utok", 1)]
        sprevT = SC.f32(4 * 240).rearrange("p (g b n) -> p g b n", g=4, b=16)
        rowb = SC.f32(512)
        SCR_T0 = SC.f32(527)
        SCR_T1 = SC.f32(527)
        rden = SCR_T0[:, 0:512]
        rc_tmp = SC.f32(16)
        PS0 = 3 * DEPTH + j
        s = next_chunk()
        wpool = lambda k, a, b_: wring[:, s, k * 512 + a:k * 512 + b_]
        pw = wring[:, s, 4096:4608].rearrange("p (g d) -> p g d", g=4)
        for ti, t in enumerate((15, 16)):
            def f_mm(e, t=t, ti=ti):
                for k in range(8):
                    ins = e.matmul(ps[:, 6 + ti, :], lhsT=hT[:, k, t * 128:(t + 1) * 128], rhs=wpool(k, 0, 512), start=(k == 0), stop=(k == 7))
                return ins
            S.op("pe", f_mm, reads=[("w", s), ("hT", t)], writes=[("ps", 6 + ti)])
            S.op("act", lambda e, ti=ti: e.copy(out=utok[:, ti, :], in_=ps[:, 6 + ti, :]), reads=[("ps", 6 + ti)], writes=[skey(("utok", ti))])
        S.dma("sp", poolp_d[j], utok[113:128, 0, :], reads=[("utok", 0)], writes=[("poolp", j)])
        S.dma("sp", pools_d[j, :, 7:15, :], utok[:, 1, :], reads=[("utok", 1)], writes=[("pools_n", j)])
        S.dma("sp", pools_d[j, :, 0:7, :], spool_d[j, :, 8:15, :], writes=[("pools_o", j)])
        out_keys.extend([("poolp", j), ("pools_n", j), ("pools_o", j)])
        for half in range(2):
            S.dma("sp", rowb[0:120, :], spool_d[j, half * 8:(half + 1) * 8].rearrange("b n c -> (b n) c"), writes=[skey("rowb")])

            def f_tr(e):
                for gi in range(4):
                    ins = e.transpose(out=ps[:, 4, gi * 120:(gi + 1) * 120], in_=rowb[0:120, gi * 128:(gi + 1) * 128], identity=ident_f[0:120, 0:120])
                return ins
            S.op("pe", f_tr, reads=["rowb", "ident_f"], writes=[("ps", 4)])
            S.op("act", lambda e, half=half: e.copy(out=sprevT[:, :, half * 8:(half + 1) * 8, :], in_=ps[:, 4, 0:480].rearrange("p (g b n) -> p g b n", g=4, b=8)), reads=[("ps", 4)], writes=[("p_sprev", half)])
        ai = 0
        for tg in range(5):
            t0, tn = TG[tg]
            hkeys = [("hT", t) for t in range(t0 // 128, (t0 + tn) // 128)]
            for gi in range(4):
                def f_mm(e, gi=gi, t0=t0, tn=tn):
                    for k in range(8):
                        ins = e.matmul(ps[:, gi, 0:tn], lhsT=wpool(k, gi * 128, (gi + 1) * 128), rhs=hT[:, k, t0:t0 + tn], start=(k == 0), stop=(k == 7))
                    return ins
                S.op("pe", f_mm, reads=[("w", s)] + hkeys, writes=[("ps", gi)])
            for gi in range(4):
                w = 2 << gi
                a = ai % 2
                ai += 1
                if tg < 4:
                    A0, A1 = Ab[:, a, :], Ab[:, 1 - a, :]
                    S.op("act", lambda e, A0=A0, gi=gi: e.copy(out=A0[:, 15:527], in_=ps[:, gi, :]), reads=[("ps", gi)], writes=[("p_A", a)])
                    src = phalo[:, gi, :] if tg == 0 else ptl[:, gi, :]
                    S.op("pool", lambda e, A0=A0, src=src: e.tensor_copy(out=A0[:, 0:15], in_=src), reads=["phalo", ("p_tl", gi)], writes=[("p_Ah", a)])
                    S.op("pool", lambda e, A0=A0, gi=gi: e.tensor_copy(out=ptl[:, gi, :], in_=A0[:, 512:527]), reads=[("p_A", a), ("p_Ah", a)], writes=[("p_tl", gi)])
                    cur = A0
                    srcs = [("p_A", a), ("p_Ah", a)]
                    stepk = 1
                    tbuf = [SCR_T0, SCR_T1]
                    ti_ = 0
                    while stepk < w:
                        nxt = tbuf[ti_ % 2]
                        ti_ += 1
                        S.op("dve", lambda e, cur=cur, nxt=nxt, stepk=stepk: e.tensor_tensor(out=nxt[:, stepk:527], in0=cur[:, stepk:527], in1=cur[:, 0:527 - stepk], op=ALU.add),
                             reads=srcs, writes=[("p_T", ti_ % 2)])
                        srcs = [("p_T", ti_ % 2)]
                        cur = nxt
                        stepk *= 2
                    db = ai % 2
                    S.op("dve", lambda e, cur=cur, A0=A0, db=db, w=w: e.scalar_tensor_tensor(out=dTb[:, db, :], in0=cur[:, 15:527], scalar=1.0 / w, in1=A0[:, 15:527], op0=ALU.mult, op1=ALU.subtract),
                         reads=srcs + [("p_A", a)], writes=[("p_d", db)])
                    if tg == 0:
                        S.op("dve", lambda e, cur=cur, gi=gi: e.tensor_tensor(out=rc_tmp[:, :], in0=cur[:, 15:31], in1=cb[:, gi * 16:(gi + 1) * 16], op=ALU.mult), reads=srcs + ["cb"], writes=["p_rc"])
                        S.op("dve", lambda e, A0=A0, db=db: e.tensor_tensor(out=dTb[:, db, 0:16], in0=rc_tmp[:, :], in1=A0[:, 15:31], op=ALU.subtract), reads=["p_rc", ("p_A", a), ("p_d", db)], writes=[("p_d", db)])
                    rhs_d = dTb[:, db, :]
                else:
                    A0 = As[:, a, :, :]
                    S.op("act", lambda e, A0=A0, gi=gi: e.copy(out=A0[:, :, 15:23], in_=ps[:, gi, 0:128].rearrange("p (b n) -> p b n", b=16)), reads=[("ps", gi)], writes=[("p_As", a)] + UTK)
                    S.op("pool", lambda e, A0=A0, gi=gi: e.tensor_copy(out=A0[:, :, 0:15], in_=sprevT[:, gi, :, :]), reads=[("p_sprev", 0), ("p_sprev", 1)], writes=[("p_Ash", a)])
                    cur = A0
                    srcs = [("p_As", a), ("p_Ash", a)]
                    stepk = 1
                    tbuf = [SCR_S0, SCR_S1]
                    ti_ = 0
                    while stepk < w:
                        nxt = tbuf[ti_ % 2]
                        ti_ += 1
                        S.op("dve", lambda e, cur=cur, nxt=nxt, stepk=stepk: e.tensor_tensor(out=nxt[:, :, stepk:23], in0=cur[:, :, stepk:23], in1=cur[:, :, 0:23 - stepk], op=ALU.add),
                             reads=srcs, writes=[("p_TS", ti_ % 2)] + UTK)
                        srcs = [("p_TS", ti_ % 2)]
                        cur = nxt
                        stepk *= 2
                    db = ai % 2
                    S.op("dve", lambda e, cur=cur, A0=A0, db=db, w=w: e.scalar_tensor_tensor(out=dTb[:, db, 0:128].rearrange("p (b n) -> p b n", b=16), in0=cur[:, :, 15:23], scalar=1.0 / w, in1=A0[:, :, 15:23], op0=ALU.mult, op1=ALU.subtract),
                         reads=srcs + [("p_As", a)], writes=[("p_d", db)])
                    rhs_d = dTb[:, db, 0:128]
                ob = 6 + (ai % 2)
                S.op("pe", lambda e, gi=gi, rhs_d=rhs_d, ob=ob, tn=tn: e.matmul(ps[:, ob, 0:tn], lhsT=pw[:, gi, :], rhs=rhs_d, start=True, stop=True), reads=[("w", s), ("p_d", db)], writes=[("ps", ob)])
                S.op("act", lambda e, gi=gi, ob=ob, t0=t0, tn=tn: e.activation(out=apT[:, 4 + gi, t0:t0 + tn], in_=ps[:, ob, 0:tn], func=AF.Copy, scale=vT[:, gi, PS0:PS0 + 1]),
                     reads=[("ps", ob), "vT"] + [("ps", g_) for g_ in range(4)], writes=[("hT", t) for t in range(t0 // 128, (t0 + tn) // 128)])
        chunk_done()
        used[0] = 8
        if AB_STOP <= 4:
            return bail()
        ni = 0
        for tg in range(5):
            t0, tn = TG[tg]
            akeys = [("hT", t) for t in range(t0 // 128, (t0 + tn) // 128)]
            for c in range(4):
                bk = 4 + (ni % 2)
                ni += 1
                S.op("pe", lambda e, c=c, bk=bk, t0=t0, tn=tn: e.matmul(ps[:, bk, 0:tn], lhsT=Ef[:, c, :], rhs=den_acc[:, t0:t0 + tn], start=True, stop=True), reads=["den", "cb"], writes=[("ps", bk)])
                S.op("dve", lambda e, bk=bk, tn=tn: e.reciprocal(out=rden[:, 0:tn], in_=ps[:, bk, 0:tn]), reads=[("ps", bk)], writes=["p_rden", ("p_T", 0), ("p_T", 1)])
                S.op("dve", lambda e, c=c, t0=t0, tn=tn: e.tensor_tensor(out=apT[:, c, t0:t0 + tn], in0=O_acc[:, c, t0:t0 + tn], in1=rden[:, 0:tn], op=ALU.mult), reads=["p_rden", "Oacc"] + akeys, writes=akeys)
        out_proj(apT, "hT")
    for l in range(DEPTH):
        ffn(l, 0)
        if ENABLE_MIX:
            if l % 2 == 0 and ENABLE_AB:
                ab_mixer(l)
            if l % 2 == 1 and ENABLE_CONV:
                conv_mixer(l)
        ffn(l, 1)

    S.barrier()
    S.dma("sp", gfin, bass.AP(gfin_d.tensor, 0, [[0, 128], [1, D]]), writes=["gfin"])
    for t in range(NT):
        S.op("act", lambda e, t=t: e.activation(out=junk, in_=x_sb[:, t, :], func=AF.Square, accum_out=ss[:, t:t + 1]),
             reads=[("x", t)], writes=[("ss", t)])
    S.op("act", lambda e: e.activation(out=sd[:], in_=ss[:], func=AF.Sqrt, scale=1.0 / D, bias=eps_t[:]),
         reads=[("ss", t) for t in range(NT)] + ["eps"], writes=["sd"])
    S.op("dve", lambda e: e.reciprocal(out=rstd[:], in_=sd[:]), reads=["sd"], writes=["rstd"])
    for t in range(NT):
        b = t % 2
        S.op("dve", lambda e, t=t, b=b: e.scalar_tensor_tensor(out=ystage[:, b, :], in0=x_sb[:, t, :], scalar=rstd[:, t:t + 1], in1=gfin, op0=ALU.mult, op1=ALU.mult),
             reads=[("x", t), "rstd", "gfin"], writes=[("ys", b)])
        S.dma("sp", y_d[t * 128:(t + 1) * 128, :], ystage[:, b, :], reads=[("ys", b)], writes=[("y", t)])
        out_keys.append(("y", t))

    S.wait_keys("sp", out_keys)
    S.replay()
    return nc


_PROG_CACHE = {}


def _host_consts(half):
    c = np.zeros((128, 4096), np.float32)
    kk = np.arange(128)[:, None]
    qq = np.arange(128)[None, :]
    maskC = np.where(kk <= qq, 0.0, NEG).astype(np.float32)
    maskP = np.where(kk >= qq, 0.0, NEG).astype(np.float32)
    mp0 = maskP if half == 1 else np.full((128, 128), NEG, np.float32)
    c[:, 0:512] = np.concatenate([maskC, maskP, maskC, maskP], 1)
    c[:, 512:1024] = np.concatenate([maskC, mp0, maskC, mp0], 1)
    bk, tk = kk // 8, kk % 8
    bq, tq = qq // 8, qq % 8
    same = bk == bq
    conds = [tk <= tq, (tk <= tq) & ((tq - tk) % 4 == 0), tk == tq]
    for g in range(3):
        sm = np.where(same & conds[g], 0.0, NEG)
        c[:, 1024 + g * 256:1024 + (g + 1) * 256] = np.concatenate([sm, sm], 1)
    p = np.arange(128)[:, None, None, None]
    t = np.arange(8)[None, None, None, :]
    ones4 = np.ones((1, 1, 4, 1), bool)
    c[:, 1792:1824] = np.where((p >= t) & ones4, 0.0, NEG).reshape(128, 32)
    r4 = np.arange(4)[None, :, None, None]
    c[:, 1824:1952] = np.where(((t % 4) == r4) & (r4 + 4 * p >= t) & ones4, 0.0, NEG).reshape(128, 128)
    r8 = np.arange(8)[None, :, None, None]
    c[:, 1952:2208] = np.where((t == r8) & ones4 & (p >= 0), 0.0, NEG).reshape(128, 256)
    oh = np.zeros((128, 8, 64), np.float32)
    for hh in range(2):
        for h4 in range(4):
            oh[:, hh * 4 + h4, hh * 32 + h4] = 1.0
    c[:, 2208:2720] = oh.reshape(128, 512)
    o = 3072
    for gi in range(4):
        w = 2 << gi
        i = np.arange(16)
        c[:, o + gi * 16:o + (gi + 1) * 16] = (1.0 / np.minimum(w, i + 1)) if half == 0 else (1.0 / w)
    c[:, o + 64] = float(half)
    E = np.zeros((64, 4, 128), np.float32)
    for hh in range(2):
        for h4 in range(4):
            h = hh * 4 + h4
            E[hh * 32 + h4, h // 2, (h % 2) * 64:(h % 2) * 64 + 64] = 1.0
    c[0:64, o + 128:o + 640] = E.reshape(64, 512)
    return c


def _host_rope(half):
    inv = np.power(np.float32(10000.0), -np.arange(32, dtype=np.float32) / np.float32(32)).astype(np.float32)
    tab = np.zeros((3, NT, 128, 64), np.float32)
    p = np.arange(128)
    for g in range(3):
        d = DILS[g]
        cls = 16 // d
        for jt in range(NT):
            if jt < 16:
                r, n = jt // cls, jt % cls
                pos = half * NP + r + d * (128 * n + p)
            else:
                pos = 2048 + (p % 8)
            ang = pos.astype(np.float32)[:, None] * inv[None, :]
            tab[g, jt, :, 0:32] = np.cos(ang)
            tab[g, jt, :, 32:64] = np.sin(ang)
    return tab


def kernel(**inputs):
    x_prompt = np.asarray(inputs["x_prompt"], np.float32)
    x_sample = np.asarray(inputs["x_sample"], np.float32)
    B = x_prompt.shape[0]
    DEPTH = inputs["ffn1_norm"].shape[0]
    N_AB = (DEPTH + 1) // 2
    N_C = DEPTH // 2
    n_cores = 2 * B
    DB = x_sample.shape[0]
    assert DB == 16 * n_cores and x_prompt.shape[1] == 2 * NP
    key = (DEPTH, n_cores)
    if key not in _PROG_CACHE:
        _PROG_CACHE[key] = build_program(DEPTH, n_cores)
    nc = _PROG_CACHE[key]

    f32 = lambda a: np.ascontiguousarray(np.asarray(a, np.float32))
    vec_rows = [inputs["ffn1_norm"][l] for l in range(DEPTH)] + [inputs["mix_norm"][l] for l in range(DEPTH)] + \
               [inputs["ffn2_norm"][l] for l in range(DEPTH)]
    for j in range(N_AB):
        vec_rows.append(np.concatenate([np.asarray(inputs["pool_scale"][j]), np.asarray(inputs["pool_scale"][j])]))
    for j in range(N_C):
        for t in range(3):
            vec_rows.append(inputs["conv_w"][j][t])
    vecs = f32(np.stack([np.asarray(v, np.float32) for v in vec_rows]))
    shared = {
        "ffn1_w_gu": f32(inputs["ffn1_w_gu"]), "ffn2_w_gu": f32(inputs["ffn2_w_gu"]),
        "ffn1_w_down": f32(inputs["ffn1_w_down"]), "ffn2_w_down": f32(inputs["ffn2_w_down"]),
        "ab_w_in": f32(inputs["ab_w_in"]), "ab_w_out": f32(inputs["ab_w_out"]), "pool_w": f32(inputs["pool_w"]),
        "conv_w_in": f32(inputs["conv_w_in"]) if N_C else np.zeros((1, D, 3 * D), np.float32),
        "conv_w_out": f32(inputs["conv_w_out"]) if N_C else np.zeros((1, D, D), np.float32),
        "vecs": vecs, "gfin": f32(np.asarray(inputs["final_norm"]).reshape(1, D)),
        "ident": np.eye(128, dtype=np.float32),
    }
    in_maps = []
    for c in range(n_cores):
        b, half = c // 2, c % 2
        m = dict(shared)
        m["x"] = f32(np.concatenate([x_prompt[b, half * NP:(half + 1) * NP], x_sample[c * 16:(c + 1) * 16].reshape(128, D)], 0))
        m["cw0"] = f32(inputs["cache_win0"][:, c * 16:(c + 1) * 16])
        m["cw1"] = f32(inputs["cache_win1"][:, c * 16:(c + 1) * 16])
        m["cw2"] = f32(inputs["cache_win2"][:, c * 16:(c + 1) * 16])
        m["spool"] = f32(inputs["state_pool"][:, c * 16:(c + 1) * 16])
        m["sconv"] = f32(inputs["state_conv"][:, c * 16:(c + 1) * 16]) if N_C else np.zeros((1, 16, 2, D), np.float32)
        m["rope"] = _host_rope(half)
        m["consts"] = _host_consts(half)
        in_maps.append(m)
    res = run_bass_kernel_spmd(nc, in_maps, core_ids=list(range(n_cores)))
    R = res.results
    S_ = x_prompt.shape[1]
    y_prompt = np.zeros((B, S_, D), np.float32)
    y_sample = np.zeros((DB, 8, D), np.float32)
    for c in range(n_cores):
        b, half = c // 2, c % 2
        y_prompt[b, half * NP:(half + 1) * NP] = R[c]["y"][:NP]
        y_sample[c * 16:(c + 1) * 16] = R[c]["y"][NP:].reshape(16, 8, D)
    outs = [y_prompt, y_sample]
    for g in range(3):
        outs.append(np.stack([R[2 * b + 1]["kvrow%d" % g] for b in range(B)], 1))
    outs.append(np.stack([R[2 * b + 1]["poolp"] for b in range(B)], 1))
    outs.append(np.stack([R[2 * b + 1]["convp"][:N_C] for b in range(B)], 1))
    for g in range(3):
        outs.append(np.concatenate([R[c]["kvs%d" % g].reshape(N_AB, 16, 8, 2, 8, 64) for c in range(n_cores)], 1))
    outs.append(np.concatenate([R[c]["pools"] for c in range(n_cores)], 1))
    outs.append(np.concatenate([R[c]["convs"][:N_C] for c in range(n_cores)], 1))
    return tuple(outs)
```
